# Optimizing a Trainium2 kernel written in Bass

```python
import jax, jax.numpy as jnp
from jax import lax
import numpy as np

D_MODEL = 2048
BATCH = 2
SEQ = 4096
DEPTH = 2

GRID_W = 64
CTX_LEN = 256
FOURIER_GROUPS = 4
FOURIER_GROUP_W = 128
FOURIER_W = FOURIER_GROUPS * FOURIER_GROUP_W
MLA_HEADS = 8
Q_RANK = 512
KV_RANK = 256
QK_NOPE = 128
QK_ROPE = 64
QK_HEAD = QK_NOPE + QK_ROPE
V_HEAD = 128
ROPE_THETA = 10000.0
AXIS_PAIRS = QK_ROPE // 4
Q_BLOCK = 128
CONV_W = 512
CONV_K = 3
N_BRANCH = 3
D_FF = ((8 * D_MODEL // 3 + 255) // 256) * 256
N_MOD = 6
RMS_EPS = 1e-6

OFF_F = 0
OFF_CQ = OFF_F + FOURIER_W
OFF_CKV = OFF_CQ + Q_RANK
OFF_KR = OFF_CKV + KV_RANK
OFF_CX = OFF_KR + QK_ROPE
OFF_CB = OFF_CX + CONV_W
OFF_CC = OFF_CB + CONV_W
OFF_G = OFF_CC + CONV_W
N_IN = OFF_G + N_BRANCH * D_MODEL

kernel_name = "hybrid_fourier_mla_shortconv_dit"


def rms_norm(x, gain):
    xf = x.astype(jnp.float32)
    y = xf * lax.rsqrt(jnp.mean(jnp.square(xf), axis=-1, keepdims=True) + RMS_EPS)
    return (y * gain.astype(jnp.float32)).astype(x.dtype)


def modulate(h, shift, scale):
    return h * (1.0 + scale) + shift


def axial_rope_tables(rows, dtype):
    row = jnp.repeat(jnp.arange(rows), GRID_W)
    col = jnp.tile(jnp.arange(GRID_W), rows)
    inv_freq = ROPE_THETA ** (-jnp.arange(AXIS_PAIRS, dtype=jnp.float32) / AXIS_PAIRS)
    ang = jnp.concatenate([row[:, None] * inv_freq, col[:, None] * inv_freq], axis=-1)
    return jnp.cos(ang)[:, None, :].astype(dtype), jnp.sin(ang)[:, None, :].astype(dtype)


def apply_rope(x, cos, sin):
    x1, x2 = x[..., : QK_ROPE // 2], x[..., QK_ROPE // 2:]
    return jnp.concatenate([x1 * cos - x2 * sin, x1 * sin + x2 * cos], axis=-1)


def rope_tail(x, rope):
    if rope is None:
        return x
    cos, sin = rope
    return jnp.concatenate([x[..., :QK_NOPE], apply_rope(x[..., QK_NOPE:], cos, sin)], axis=-1)


def mla_queries(p_cq, lw, rope):
    B, T, _ = p_cq.shape
    cq = rms_norm(p_cq, lw["q_a_norm"])
    q = (cq @ lw["w_uq"]).reshape(B, T, MLA_HEADS, QK_HEAD)
    q = rms_norm(q, lw["q_norm"])
    return rope_tail(q, rope)


def mla_keys_values(p_ckv, p_kr, lw, rope):
    B, T, _ = p_ckv.shape
    ckv = rms_norm(p_ckv, lw["kv_a_norm"])
    kv = (ckv @ lw["w_ukv"]).reshape(B, T, MLA_HEADS, QK_NOPE + V_HEAD)
    k_nope, v = kv[..., :QK_NOPE], kv[..., QK_NOPE:]
    k_rope = jnp.broadcast_to(p_kr[:, :, None, :], (B, T, MLA_HEADS, QK_ROPE))
    k = rms_norm(jnp.concatenate([k_nope, k_rope], axis=-1), lw["k_norm"])
    return rope_tail(k, rope), v


def attention(q, k, v):
    B, T, H, Dh = q.shape
    nb = T // Q_BLOCK
    scale = QK_HEAD ** -0.5
    qb = q.reshape(B, nb, Q_BLOCK, H, Dh).transpose(1, 0, 2, 3, 4)

    def one_block(q_blk):
        s = jnp.einsum("bqhd,bkhd->bhqk", q_blk, k).astype(jnp.float32) * scale
        pr = jax.nn.softmax(s, axis=-1).astype(v.dtype)
        return jnp.einsum("bhqk,bkhd->bqhd", pr, v)

    out = lax.map(one_block, qb)
    return out.transpose(1, 0, 2, 3, 4).reshape(B, T, H * V_HEAD)


def fourier_mix(pf):
    B, T, _ = pf.shape
    f = pf.astype(jnp.float32).reshape(B, T, FOURIER_GROUPS, FOURIER_GROUP_W)
    f = jnp.fft.fft2(f, axes=(1, 3), norm="ortho").real
    return f.reshape(B, T, FOURIER_W).astype(pf.dtype)


def short_conv_mix(px, pb, pc, conv_w):
    u = pc * px
    up = jnp.pad(u, ((0, 0), (1, 1), (0, 0)))
    y = up[:, :-2] * conv_w[0] + up[:, 1:-1] * conv_w[1] + up[:, 2:] * conv_w[2]
    return pb * y


def merge_branches(p, attn, lw):
    B, T, _ = p.shape
    y_f = fourier_mix(p[..., OFF_F:OFF_CQ]) @ lw["w_f_out"]
    y_m = attn @ lw["w_mla_out"]
    y_c = short_conv_mix(p[..., OFF_CX:OFF_CB], p[..., OFF_CB:OFF_CC], p[..., OFF_CC:OFF_G],
                         lw["conv_w"]) @ lw["w_conv_out"]
    g = jax.nn.sigmoid(p[..., OFF_G:] + lw["b_gate"]).reshape(B, T, N_BRANCH, D_MODEL)
    merged = g[..., 0, :] * y_f + g[..., 1, :] * y_m + g[..., 2, :] * y_c
    return merged @ lw["w_out"]


def latent_mixer(p, rope, k_ctx, v_ctx, lw):
    q = mla_queries(p[..., OFF_CQ:OFF_CKV], lw, rope)
    k, v = mla_keys_values(p[..., OFF_CKV:OFF_KR], p[..., OFF_KR:OFF_CX], lw, rope)
    attn = attention(q, jnp.concatenate([k_ctx, k], axis=1), jnp.concatenate([v_ctx, v], axis=1))
    return merge_branches(p, attn, lw)


def context_mixer(pc, k_ctx, v_ctx, lw):
    q = mla_queries(pc[..., OFF_CQ:OFF_CKV], lw, None)
    return merge_branches(pc, attention(q, k_ctx, v_ctx), lw)


def swiglu(h, lw):
    return (jax.nn.silu(h @ lw["w_ffn_gate"]) * (h @ lw["w_ffn_up"])) @ lw["w_ffn_down"]


def setup_inputs(seed: int = 0) -> dict:
    key = jax.random.key(seed)
    ks = jax.random.split(key, 24)
    L, D = DEPTH, D_MODEL

    def nrm(k, shape):
        return jax.random.normal(k, shape, jnp.float32)

    def w(k, shape, fan_in, gain=1.0):
        return (gain * fan_in ** -0.5) * nrm(k, shape)

    def g(k, shape):
        return 1.0 + 0.01 * nrm(k, shape)

    def b(k, shape):
        return 0.02 * nrm(k, shape)

    return {
        "x": nrm(ks[0], (BATCH, SEQ, D)),
        "c": nrm(ks[1], (BATCH, D)),
        "ctx": nrm(ks[2], (BATCH, CTX_LEN, D)),
        "c_ctx": nrm(ks[3], (D,)),
        "w_ada": w(ks[4], (L, D, N_MOD * D), D, 0.5),
        "b_ada": b(ks[5], (L, N_MOD * D)),
        "norm_mix": g(ks[6], (L, D)),
        "norm_ffn": g(ks[7], (L, D)),
        "w_in": w(ks[8], (L, D, N_IN), D),
        "b_gate": b(ks[9], (L, N_BRANCH * D)),
        "q_a_norm": g(ks[10], (L, Q_RANK)),
        "kv_a_norm": g(ks[11], (L, KV_RANK)),
        "w_uq": w(ks[12], (L, Q_RANK, MLA_HEADS * QK_HEAD), Q_RANK),
        "w_ukv": w(ks[13], (L, KV_RANK, MLA_HEADS * (QK_NOPE + V_HEAD)), KV_RANK),
        "q_norm": g(ks[14], (L, QK_HEAD)),
        "k_norm": g(ks[15], (L, QK_HEAD)),
        "w_f_out": w(ks[16], (L, FOURIER_W, D), FOURIER_W),
        "w_mla_out": w(ks[17], (L, MLA_HEADS * V_HEAD, D), MLA_HEADS * V_HEAD),
        "conv_w": w(ks[18], (L, CONV_K, CONV_W), CONV_K),
        "w_conv_out": w(ks[19], (L, CONV_W, D), CONV_W),
        "w_out": w(ks[20], (L, D, D), D),
        "w_ffn_gate": w(ks[21], (L, D, D_FF), D),
        "w_ffn_up": w(ks[22], (L, D, D_FF), D),
        "w_ffn_down": w(ks[23], (L, D_FF, D), D_FF),
    }


def reference(x, c, ctx, c_ctx, w_ada, b_ada, norm_mix, norm_ffn, w_in, b_gate,
              q_a_norm, kv_a_norm, w_uq, w_ukv, q_norm, k_norm,
              w_f_out, w_mla_out, conv_w, w_conv_out, w_out,
              w_ffn_gate, w_ffn_up, w_ffn_down):
    B, S, D = x.shape
    rows = S // GRID_W
    rope = axial_rope_tables(rows, x.dtype)
    ada_lat = jax.nn.silu(c)
    ada_ctx = jax.nn.silu(c_ctx)
    xc = ctx
    for l in range(DEPTH):
        last = l == DEPTH - 1
        lw = {
            "w_in": w_in[l], "b_gate": b_gate[l],
            "q_a_norm": q_a_norm[l], "kv_a_norm": kv_a_norm[l],
            "w_uq": w_uq[l], "w_ukv": w_ukv[l], "q_norm": q_norm[l], "k_norm": k_norm[l],
            "w_f_out": w_f_out[l], "w_mla_out": w_mla_out[l],
            "conv_w": conv_w[l], "w_conv_out": w_conv_out[l], "w_out": w_out[l],
            "w_ffn_gate": w_ffn_gate[l], "w_ffn_up": w_ffn_up[l], "w_ffn_down": w_ffn_down[l],
        }
        mod = (ada_lat @ w_ada[l] + b_ada[l]).reshape(B, N_MOD, 1, D)
        modc = (ada_ctx @ w_ada[l] + b_ada[l]).reshape(N_MOD, D)

        h = modulate(rms_norm(x, norm_mix[l]), mod[:, 0], mod[:, 1])
        hc = modulate(rms_norm(xc, norm_mix[l]), modc[0], modc[1])
        p = h @ lw["w_in"]
        if last:
            pc = hc @ lw["w_in"][:, OFF_CKV:OFF_CX]
            pc_ckv, pc_kr = pc[..., :KV_RANK], pc[..., KV_RANK:]
        else:
            pc = hc @ lw["w_in"]
            pc_ckv, pc_kr = pc[..., OFF_CKV:OFF_KR], pc[..., OFF_KR:OFF_CX]
        k_ctx, v_ctx = mla_keys_values(pc_ckv, pc_kr, lw, None)
        x = x + mod[:, 2] * latent_mixer(p, rope, k_ctx, v_ctx, lw)
        if not last:
            xc = xc + modc[2] * context_mixer(pc, k_ctx, v_ctx, lw)

        h2 = modulate(rms_norm(x, norm_ffn[l]), mod[:, 3], mod[:, 4])
        x = x + mod[:, 5] * swiglu(h2, lw)
        if not last:
            hc2 = modulate(rms_norm(xc, norm_ffn[l]), modc[3], modc[4])
            xc = xc + modc[5] * swiglu(hc2, lw)
    return x
```

```python
import math
import numpy as np
import ml_dtypes
import concourse.bass as bass
import concourse.mybir as mybir
from concourse.bass_utils import run_bass_kernel_spmd

F32 = mybir.dt.float32
BF16 = mybir.dt.bfloat16
AF = mybir.ActivationFunctionType
ALU = mybir.AluOpType

FULL_CFG = dict(D=2048, B=2, SEQ=4096, DEPTH=2, GRID_W=64, CTX=256, FG=4, H=8, QR=512, KVR=256,
                CW=512, DFF=5632, NOPE=128, ROPE=64, VH=128, EPS=1e-6, THETA=10000.0)
R = 4


class Cfg:
    def __init__(self, d):
        self.__dict__.update(d)
        c = self
        c.KD = c.D // 128
        c.FWID = c.FG * 128
        c.QC = c.QR // 128
        c.KVC = c.KVR // 128
        c.CC = c.CW // 128
        c.QKH = c.NOPE + c.ROPE
        c.OFF_F = 0
        c.OFF_CQ = c.OFF_F + c.FWID
        c.OFF_CKV = c.OFF_CQ + c.QR
        c.OFF_KR = c.OFF_CKV + c.KVR
        c.OFF_CX = c.OFF_KR + c.ROPE
        c.OFF_CB = c.OFF_CX + c.CW
        c.OFF_CC = c.OFF_CB + c.CW
        c.OFF_G = c.OFF_CC + c.CW
        c.N_IN = c.OFF_G + 3 * c.D
        c.LT = c.SEQ // R
        c.CT = c.CTX // R
        c.NT = c.LT + c.CT
        c.TS = d.get('TS', min(512, c.LT))
        c.NLT = c.LT // c.TS
        c.LCH = c.LT // 128
        c.NTC = c.LCH + 1
        c.NK = c.SEQ + c.CTX
        c.CKS = min(128, c.CTX)
        c.NCK = c.CTX // c.CKS
        c.NKC = c.SEQ // 128 + c.NCK
        c.FC = c.DFF // 128
        c.FH = c.FC // 2
        off = {}
        n = 0

        def add(name, w):
            nonlocal n
            off[name] = n
            n += w
        add("cT", c.KD * 2)
        add("sel", 8)
        for l in range(c.DEPTH):
            add(("nmix", l), c.KD)
            add(("nffn", l), c.KD)
            add(("bada", l), 6 * c.KD * 2)
            add(("bgate", l), 3 * c.KD)
            add(("qa", l), c.QC)
            add(("kva", l), c.KVC)
            for nm in ("gq_n", "gq_r", "gq_s", "gk_n", "gk_r", "gk_s"):
                add((nm, l), 1)
            add(("convw", l), 3 * c.CC)
        c.voff = off
        c.NV = n


class Slot:
    def __init__(self, sched):
        self.sched = sched
        self.sems = {}
        self.counts = {}

    def toks(self):
        return [(self.sems[k], self.counts[k]) for k in self.sems if self.counts[k]]

    def sem(self, kind):
        if kind not in self.sems:
            self.sems[kind] = self.sched.newsem("sl%d" % self.sched.nsem)
            self.counts[kind] = 0
        return self.sems[kind]


class Sched:
    def __init__(self, nc):
        self.nc = nc
        self.eng = {"pe": nc.tensor, "act": nc.scalar, "dve": nc.vector, "pool": nc.gpsimd, "sp": nc.sync}
        self.esem = {}
        self.ecnt = {}
        self.nsem = 0
        for e in ("pe", "act", "dve"):
            self.esem[e] = self.newsem("e_" + e)
            self.ecnt[e] = 0
        self.seen = {e: {} for e in self.eng}
        self.res = {}
        self.slots = []
        self.free_slots = []
        self.nwait = 0

    def newsem(self, name):
        self.nsem += 1
        return self.nc.semaphore(name).__enter__()

    def slot(self, name=None):
        if self.free_slots:
            return self.free_slots.pop()
        s = Slot(self)
        self.slots.append(s)
        return s

    def put_slot(self, sl):
        self.free_slots.append(sl)

    def _deps(self, e, reads, writes):
        toks = []
        for r in reads:
            st = self.res.get(r)
            if st and st[0]:
                toks.append(st[0])
        for w in writes:
            st = self.res.get(w)
            if st:
                if st[0]:
                    toks.append(st[0])
                toks.extend(st[1])
        return toks

    def _wait(self, e, toks):
        need = {}
        for (sem, v) in toks:
            if e == "pe" and sem is self.esem["pe"]:
                continue
            if self.seen[e].get(sem, 0) >= v:
                continue
            if need.get(sem, 0) < v:
                need[sem] = v
        for sem, v in need.items():
            self.eng[e].wait_ge(sem, v)
            self.seen[e][sem] = v
            self.nwait += 1

    def _record(self, reads, writes, tok):
        for r in reads:
            st = self.res.get(r)
            if st is None:
                st = self.res[r] = [None, []]
            st[1].append(tok)
        for w in writes:
            self.res[w] = [tok, []]

    def op(self, e, reads, writes, fn):
        self._wait(e, self._deps(e, reads, writes))
        ins = fn(self.eng[e])
        self.ecnt[e] += 1
        ins.then_inc(self.esem[e], 1)
        tok = (self.esem[e], self.ecnt[e])
        self.seen[e][self.esem[e]] = max(self.seen[e].get(self.esem[e], 0), 0)
        self._record(reads, writes, tok)
        return tok

    def dma(self, q, slot, reads, writes, fns):
        toks = self._deps(q, reads, writes)
        toks.extend(slot.toks())
        self._wait(q, toks)
        sem = slot.sem(q)
        for fn in fns:
            ins = fn(self.eng[q])
            ins.then_inc(sem, 16)
            slot.counts[q] += 16
        tok = (sem, slot.counts[q])
        self._record(reads, writes, tok)
        return tok

    def collective(self, reads, writes, fn):
        sem = self.newsem("cc%d" % self.nsem)
        self._wait("pool", self._deps("pool", reads, writes))
        fn(self.eng["pool"]).then_inc(sem)
        tok = (sem, 1)
        self._record(reads, writes, tok)
        return tok

    def barrier(self):
        toks = [(self.esem[e], self.ecnt[e]) for e in self.esem if self.ecnt[e]]
        for sl in self.slots:
            toks += sl.toks()
        for st in self.res.values():
            if st is None:
                continue
            if st[0]:
                toks.append(st[0])
            toks.extend(st[1])
        for e in self.eng:
            self._wait(e, toks)
        self.res = {}


class Ring:
    def __init__(self, s, nc, name, n, shape, dtype):
        self.bufs = [nc.sbuf_tensor("%s%d" % (name, i), shape, dtype) for i in range(n)]
        self.t = [b.__enter__() for b in self.bufs]
        self.slots = [None] * n
        self.s = s
        self.name = name
        self.n = n
        self.i = 0

    def next(self):
        i = self.i % self.n
        self.i += 1
        if self.slots[i] is None:
            self.slots[i] = self.s.slot()
        return self.t[i], self.slots[i], (self.name, i)

    def close(self):
        for sl in self.slots:
            if sl is not None:
                self.s.put_slot(sl)
        for b in reversed(self.bufs):
            b.__exit__(None, None, None)


def build_program(cfg, upto=None):
    c = cfg
    nc = bass.Bass("TRN2", target_bir_lowering=False)
    s = Sched(nc)
    L, D, KD, NT, LT, CT, TS = c.DEPTH, c.D, c.KD, c.NT, c.LT, c.CT, c.TS
    H, QC, KVC, CC, NK, NKC = c.H, c.QC, c.KVC, c.CC, c.NK, c.NKC

    def din(name, shape, dt=F32):
        return nc.dram_tensor(name, list(shape), dt, kind="ExternalInput").ap()

    def dscr(name, shape, dt):
        return nc.dram_tensor(name, list(shape), dt).ap()

    xT_in = din("xT", [KD, 128, NT])
    vecs_in = din("vecs", [128, c.NV])
    rope_in = din("rope", [64, 2, NT])
    dft_c = din("dft_c", [c.SEQ, LT], BF16)
    dft_s = din("dft_s", [c.SEQ, LT], BF16)
    dftx_c = din("dftx_c", [c.CTX, CT], BF16)
    dftx_s = din("dftx_s", [c.CTX, CT], BF16)
    dft_ch = din("dft_ch", [128, 2, 128], BF16)
    w_ada = din("w_ada", [L, D, 6 * D])
    w_in = din("w_in", [L, D, c.N_IN])
    w_uq = din("w_uq", [L, c.QR, H * c.QKH])
    w_ukv = din("w_ukv", [L, c.KVR, H * (c.NOPE + c.VH)])
    w_f_out = din("w_f_out", [L, c.FWID, D])
    w_mla_out = din("w_mla_out", [L, H * c.VH, D])
    w_conv_out = din("w_conv_out", [L, c.CW, D])
    w_out = din("w_out", [L, D, D])
    w_ffn_gate = din("w_ffn_gate", [L, D, c.DFF])
    w_ffn_up = din("w_ffn_up", [L, D, c.DFF])
    w_ffn_down = din("w_ffn_down", [L, c.DFF, D])
    out_T = nc.dram_tensor("outT", [KD, 128, LT], F32, kind="ExternalOutput").ap()

    xs = dscr("xs", [KD, 128, NT], F32)
    h_s = dscr("h_s", [KD, 128, NT], BF16)
    q_s = dscr("q_s", [H, 192, NT], BF16)
    m_s = dscr("m_s", [KD, 128, NT], BF16)
    exa_t = [dscr("exa%d" % tc, [128 if tc < c.LCH else CT, c.FWID], BF16) for tc in range(c.NTC)]
    ga_t = [dscr("ga%d" % tc, [R * (128 if tc < c.LCH else CT), c.FWID], BF16) for tc in range(c.NTC)]
    NXB = 2 * KVC + 2
    exb_t = [dscr("exb%d" % i, [64, NT], BF16) for i in range(NXB)]
    gb_t = [dscr("gb%d" % i, [R * 64, NT], BF16) for i in range(NXB)]
    exc = dscr("exc", [128, 4 * CC], F32)
    gc = dscr("gc", [R * 128, 4 * CC], F32)
    groups = [[0, 1, 2, 3], [4, 5, 6, 7]]

    tiles = [(i * TS, TS, False) for i in range(c.NLT)] + [(LT, CT, True)]
    kchunks = [(i * 128, 128) for i in range(c.SEQ // 128)] + [(c.SEQ + i * c.CKS, c.CKS) for i in range(c.NCK)]
    ctx_kchunks = list(range(c.SEQ // 128, NKC))

    uid = [0]

    def T(name, shape, dt):
        uid[0] += 1
        cm = nc.sbuf_tensor("%s_u%d" % (name, uid[0]), list(shape), dt)
        return cm, cm.__enter__()

    class Phase:
        def __init__(self):
            self.cms = []
            self.rings = []

        def t(self, name, shape, dt):
            cm, t = T(name, shape, dt)
            self.cms.append(cm)
            return t

        def ring(self, name, n, shape, dt):
            uid[0] += 1
            r = Ring(s, nc, "%s_u%d_" % (name, uid[0]), n, shape, dt)
            self.cms.append(r)
            return r

        def close(self):
            s.barrier()
            for cm in reversed(self.cms):
                if isinstance(cm, Ring):
                    cm.close()
                else:
                    cm.__exit__(None, None, None)

    G = Phase()
    vecs = G.t("vecs", [128, c.NV], F32)
    ones = G.t("ones", [128, 128], BF16)
    rope = G.t("ropet", [64, 2, NT], F32)
    dch = G.t("dch", [128, 2, 128], BF16)
    modt = G.t("modt", [128, 6, KD, 2], F32)
    A1 = G.t("A1", [128, KD, 2], F32)
    A2 = G.t("A2", [128, KD, 2], F32)
    gsc = G.t("gsc", [128, 8], F32)
    s2 = G.t("s2", [128, KD, 2], BF16)
    epsb = G.t("epsb", [128, 1], F32)
    wring = G.ring("wr", 3, [128, KD, 512], BF16)
    ps_cm = [nc.psum_tensor("ps%d" % i, [128, 512], F32) for i in range(8)]
    ps = [p.__enter__() for p in ps_cm]
    psi = [0]

    reserved = set()

    def nb():
        while True:
            b = psi[0] % 8
            psi[0] += 1
            if b not in reserved:
                return b

    def reserve():
        b = nb()
        reserved.add(b)
        return b

    def release(*bs):
        for b in bs:
            reserved.discard(b)

    ld0 = s.slot("ld0")
    st_slots = [s.slot("st%d" % i) for i in range(4)]
    sti = [0]

    def stslot():
        sl = st_slots[sti[0] % 4]
        sti[0] += 1
        return sl

    ld_slots = [s.slot("ldx%d" % i) for i in range(4)]
    ldi = [0]

    def ldslot():
        sl = ld_slots[ldi[0] % 4]
        ldi[0] += 1
        return sl

    V = c.voff

    def vcol(name, i=0, n=1, rows=128):
        o = V[name] + i
        return vecs[0:rows, o:o + n]

    s.dma("sp", ld0, [], ["vecs", "rope", "dch"], [
        lambda q: q.dma_start(out=vecs[:], in_=vecs_in),
        lambda q: q.dma_start(out=rope[:], in_=rope_in),
        lambda q: q.dma_start(out=dch[:], in_=dft_ch),
    ])
    s.op("dve", [], ["ones"], lambda e: e.memset(ones[:], 1.0))
    s.op("dve", [], ["epsb"], lambda e: e.memset(epsb[:], c.EPS))
    s.dma("sp", stslot(), [], [("xs", j, ti) for j in range(KD) for ti in range(len(tiles))],
          [lambda q: q.dma_start(out=xs, in_=xT_in)])
    s.op("act", ["vecs"], ["s2"], lambda e: e.activation(
        out=s2[:].rearrange("p k t -> p (k t)"), in_=vcol("cT", 0, KD * 2), func=AF.Silu))

    def wload(wap, l, k0, nk, c0, ncol, dst, slot, key, extra_reads=()):
        fns = []
        step = 8
        for a in range(0, nk, step):
            b = min(nk, a + step)
            src = wap[l, (k0 + a) * 128:(k0 + b) * 128, c0:c0 + ncol].rearrange("(k p) c -> p k c", p=128)
            fns.append(lambda q, src=src, a=a, b=b: q.dma_start(out=dst[:, a:b, 0:ncol], in_=src))
        return s.dma("pool", slot, list(extra_reads), [key], fns)

    def mm_group(bank, reads, pairs, outap=None, writes=None):
        o = outap if outap is not None else ps[bank][:]
        n = len(pairs)

        def fn(e):
            ins = None
            for i, (a, b) in enumerate(pairs):
                ins = e.matmul(o, a, b, start=(i == 0), stop=(i == n - 1))
            return ins
        return s.op("pe", reads, writes if writes is not None else [("ps", bank)], fn)

    def rms_rstd(bank, sz, dim, dst, dst_key, rt, rt_key, rows=128):
        s.op("act", [("ps", bank), "epsb"], [rt_key], lambda e: e.activation(
            out=rt[0:rows, 0:sz], in_=ps[bank][0:rows, 0:sz], func=AF.Sqrt, scale=1.0 / dim, bias=epsb[0:rows, 0:1]))
        s.op("dve", [rt_key], [dst_key], lambda e: e.reciprocal(out=dst, in_=rt[0:rows, 0:sz]))

    for l in range(L):
        last = (l == L - 1)
        act_tiles = [t for t in tiles if not (last and t[2])]

        nblk = (6 * D) // 512
        mb = nb()
        for blk in range(nblk):
            wt, wslot, wkey = wring.next()
            wload(w_ada, l, 0, KD, blk * 512, 512, wt, wslot, wkey)
            for jj in range(4):
                j = blk * 4 + jj
                mm_group(mb, [wkey, "s2"], [(wt[:, k, jj * 128:(jj + 1) * 128], s2[:, k, :]) for k in range(KD)],
                         outap=ps[mb][:, j * 2:(j + 1) * 2])
        s.op("dve", [("ps", mb), "vecs"], ["modt"], lambda e: e.tensor_tensor(
            out=modt[:].rearrange("p m k t -> p (m k t)"), in0=ps[mb][:, 0:6 * KD * 2],
            in1=vcol(("bada", l), 0, 6 * KD * 2), op=ALU.add))
        for (Ax, nm, mi) in ((A1, "nmix", 1), (A2, "nffn", 4)):
            for t2 in range(2):
                s.op("dve", ["modt", "vecs"], [("A", nm, t2)], lambda e, Ax=Ax, nm=nm, mi=mi, t2=t2: e.scalar_tensor_tensor(
                    out=Ax[:, :, t2], in0=modt[:, mi, :, t2], scalar=1.0, in1=vcol((nm, l), 0, KD),
                    op0=ALU.add, op1=ALU.mult))
        sc = float(c.QKH) ** -0.5
        for i, nm in enumerate(("gq_n", "gq_r", "gq_s")):
            s.op("dve", ["vecs"], [("gsc", i)], lambda e, i=i, nm=nm: e.tensor_scalar(
                out=gsc[:, i:i + 1], in0=vcol((nm, l)), scalar1=sc, scalar2=None, op0=ALU.mult))
        if upto == "mod" and l == 0:
            break

        def norm_phase(P, Ax, mi_shift, hT, use_tiles, store_h):
            KH = (KD + 1) // 2
            xr = P.ring("xr", 2, [128, KH, TS], F32)
            sqr = P.ring("sqr", 1, [128, KD, TS], BF16)
            rt = P.ring("rt", 2, [128, TS], F32)
            rr = P.ring("rr", 2, [128, TS], F32)
            tmp = P.ring("ntmp", 3, [128, TS], F32)
            for ti, (t0, sz, isc) in enumerate(tiles):
                if (t0, sz, isc) not in use_tiles:
                    continue
                m = 1 if isc else 0
                xch = {}
                for hf in range(2):
                    k0, k1 = hf * KH, min(KD, (hf + 1) * KH)
                    if k0 >= k1:
                        continue
                    xt, xslot, xkey = xr.next()
                    s.dma("sp", xslot, [("xs", j, ti) for j in range(k0, k1)], [xkey], [
                        lambda q, xt=xt, t0=t0, sz=sz, k0=k0, k1=k1: q.dma_start(
                            out=xt[:, 0:k1 - k0, 0:sz], in_=xs[k0:k1, :, t0:t0 + sz].rearrange("k p t -> p k t"))])
                    for k in range(k0, k1):
                        xch[k] = (xt, k - k0, xkey)
                sq, _, sqkey = sqr.next()
                for k in range(KD):
                    xt, kk, xkey = xch[k]
                    s.op("act", [xkey], [(sqkey, k)], lambda e, k=k, kk=kk, sq=sq, xt=xt, sz=sz: e.activation(
                        out=sq[:, k, 0:sz], in_=xt[:, kk, 0:sz], func=AF.Square))
                b = nb()
                mm_group(b, [(sqkey, k) for k in range(KD)] + ["ones"],
                         [(ones[:], sq[:, k, 0:sz]) for k in range(KD)], outap=ps[b][:, 0:sz])
                rtt, _, rtkey = rt.next()
                rrt, _, rrkey = rr.next()
                rms_rstd(b, sz, D, rrt[:, 0:sz], rrkey, rtt, rtkey)
                for k in range(KD):
                    xt, kk, xkey = xch[k]
                    tt, _, tkey = tmp.next()
                    s.op("dve", [xkey, rrkey, ("A", "x", m)], [tkey], lambda e, k=k, kk=kk, tt=tt, xt=xt, rrt=rrt, sz=sz, m=m: e.scalar_tensor_tensor(
                        out=tt[:, 0:sz], in0=xt[:, kk, 0:sz], scalar=Ax[:, k, m:m + 1], in1=rrt[:, 0:sz],
                        op0=ALU.mult, op1=ALU.mult))
                    s.op("act", [tkey, "modt"], [("hT", ti, k)], lambda e, k=k, tt=tt, sz=sz, t0=t0, m=m: e.activation(
                        out=hT[:, k, t0:t0 + sz], in_=tt[:, 0:sz], func=AF.Identity,
                        bias=modt[:, mi_shift, k, m:m + 1], scale=1.0))
                if store_h:
                    s.dma("sp", stslot(), [("hT", ti, k) for k in range(KD)], [("h_s", ti)], [
                        lambda q, t0=t0, sz=sz: q.dma_start(
                            out=h_s[:, :, t0:t0 + sz].rearrange("k p t -> p k t"), in_=hT[:, :, t0:t0 + sz])])

        for m in range(2):
            s.res[("A", "x", m)] = s.res.get(("A", "nmix", m))

        PXc = Phase()
        convT = PXc.t("convT", [128, CC, NT], BF16)
        PA = Phase()
        hT = PA.t("hT", [128, KD, NT], BF16)
        stg = PA.ring("stg", 2, [128, 512], BF16)
        PN = Phase()
        norm_phase(PN, A1, 0, hT, tiles, True)
        PN.close()
        if upto == "norm" and l == 0:
            PA.close()
            PXc.close()
            break

        def proj_block(col0, ncol, handlers, use_tiles):
            wt, wslot, wkey = wring.next()
            wload(w_in, l, 0, KD, col0, ncol, wt, wslot, wkey)
            for idx, (co, M, pb, evac) in enumerate(handlers):
                for ti, (t0, sz, isc) in enumerate(tiles):
                    if (t0, sz, isc) not in use_tiles:
                        continue
                    b = nb()
                    mm_group(b, [wkey] + [("hT", ti, k) for k in range(KD)],
                             [(wt[:, k, co:co + M], hT[:, k, t0:t0 + sz]) for k in range(KD)],
                             outap=ps[b][pb:pb + M, 0:sz])
                    evac(b, ti, t0, sz, M, idx, pb)
            return wkey

        def ev_copy(dst_fn, key_fn):
            def evac(b, ti, t0, sz, M, idx, pb):
                dst = dst_fn(idx, t0, sz, M)
                if (idx + ti) % 2:
                    s.op("act", [("ps", b)], [key_fn(idx, ti)], lambda e: e.activation(out=dst, in_=ps[b][pb:pb + M, 0:sz], func=AF.Copy))
                else:
                    s.op("dve", [("ps", b)], [key_fn(idx, ti)], lambda e: e.tensor_copy(out=dst, in_=ps[b][pb:pb + M, 0:sz]))
            return evac

        def feat_norm(P, src, nch, dim, gname, dst, ti, t0, sz, srckey, dstkey):
            sq, _, sqkey = P.sqb.next()
            for i in range(nch):
                s.op("act", [(srckey, i, ti)], [(sqkey, i)], lambda e, i=i, sq=sq: e.activation(
                    out=sq[:, i, 0:sz], in_=src[:, i, t0:t0 + sz], func=AF.Square))
            b = nb()
            mm_group(b, [(sqkey, i) for i in range(nch)] + ["ones"],
                     [(ones[:], sq[:, i, 0:sz]) for i in range(nch)], outap=ps[b][:, 0:sz])
            rtt, _, rtkey = P.rt.next()
            rrt, _, rrkey = P.rr.next()
            rms_rstd(b, sz, dim, rrt[:, 0:sz], rrkey, rtt, rtkey)
            for i in range(nch):
                s.op("dve", [(srckey, i, ti), rrkey, "vecs"], [(dstkey, i, ti)], lambda e, i=i, rrt=rrt: e.scalar_tensor_tensor(
                    out=dst[:, i, t0:t0 + sz], in0=src[:, i, t0:t0 + sz], scalar=vcol((gname, l), i), in1=rrt[:, 0:sz],
                    op0=ALU.mult, op1=ALU.mult))

        nti = len(tiles)
        P1 = Phase()
        ckvf = P1.t("ckvf", [128, KVC, NT], F32)
        krf = P1.t("krf", [64, 2, NT], F32)
        ckvn = P1.t("ckvn", [128, KVC, NT], BF16)
        krb = P1.t("krb", [64, 2, NT], BF16)
        P1.sqb = P1.ring("sqb", 2, [128, max(QC, KVC), TS], BF16)
        P1.rt = P1.ring("rt", 2, [128, TS], F32)
        P1.rr = P1.ring("rr", 2, [128, TS], F32)
        tmp64 = P1.ring("t64", 4, [64, TS], F32)
        hs = [(i * 128, 128, 0, ev_copy(lambda i_, t0, sz, M: ckvf[:, i_, t0:t0 + sz], lambda i_, ti: ("ckvf", i_, ti)))
              for i in range(KVC)]
        hs.append((c.KVR, 64, 0, ev_copy(lambda i_, t0, sz, M: krf[0:64, 0, t0:t0 + sz], lambda i_, ti: ("krf", 0, ti))))
        hs.append((c.KVR + 32, 32, 0, ev_copy(lambda i_, t0, sz, M: krf[0:32, 1, t0:t0 + sz], lambda i_, ti: ("krf", 1, ti))))
        hs.append((c.KVR, 32, 32, ev_copy(lambda i_, t0, sz, M: krf[32:64, 1, t0:t0 + sz], lambda i_, ti: ("krf", 2, ti))))
        proj_block(c.OFF_CKV, c.KVR + 64, hs, tiles)
        for ti, (t0, sz, isc) in enumerate(tiles):
            feat_norm(P1, ckvf, KVC, c.KVR, "kva", ckvn, ti, t0, sz, "ckvf", "ckvn")
        for ti, (t0, sz, isc) in enumerate(tiles):
            t1, _, k1 = tmp64.next()
            t2, _, k2 = tmp64.next()
            s.op("dve", [("krf", 0, ti), "vecs", "rope"], [k1], lambda e, t1=t1, t0=t0, sz=sz: e.scalar_tensor_tensor(
                out=t1[:, 0:sz], in0=krf[0:64, 0, t0:t0 + sz], scalar=vcol(("gk_r", l), rows=64), in1=rope[:, 0, t0:t0 + sz],
                op0=ALU.mult, op1=ALU.mult))
            s.op("dve", [("krf", 1, ti), ("krf", 2, ti), "vecs", "rope"], [k2], lambda e, t2=t2, t0=t0, sz=sz: e.scalar_tensor_tensor(
                out=t2[:, 0:sz], in0=krf[0:64, 1, t0:t0 + sz], scalar=vcol(("gk_s", l), rows=64), in1=rope[:, 1, t0:t0 + sz],
                op0=ALU.mult, op1=ALU.mult))
            s.op("dve", [k1, k2], [("krb", 0, ti)], lambda e, t1=t1, t2=t2, t0=t0, sz=sz: e.tensor_tensor(
                out=krb[:, 0, t0:t0 + sz], in0=t1[:, 0:sz], in1=t2[:, 0:sz], op=ALU.add))
            s.op("act", [("krf", 0, ti)], [("krb", 1, ti)], lambda e, t0=t0, sz=sz: e.activation(
                out=krb[:, 1, t0:t0 + sz], in_=krf[0:64, 0, t0:t0 + sz], func=AF.Square))
        for xi in range(NXB):
            if xi < 2 * KVC:
                i, hf = xi // 2, xi % 2
                rd = [("ckvn", i, ti) for ti in range(nti)]
                src = ckvn[hf * 64:(hf + 1) * 64, i, :]
            else:
                j = xi - 2 * KVC
                rd = [("krb", j, ti) for ti in range(nti)]
                src = krb[:, j, :]
            s.dma("sp", stslot(), rd, [("exb", xi)], [lambda q, xi=xi, src=src: q.dma_start(out=exb_t[xi], in_=src)])
            s.collective([("exb", xi)], [("gb", xi)], lambda g, xi=xi: g.collective_compute(
                "AllGather", ALU.bypass, replica_groups=groups, ins=[exb_t[xi].opt()], outs=[gb_t[xi].opt()]))
        P1.close()

        wt, wslot, wkey = wring.next()
        wload(w_in, l, 0, KD, c.OFF_F, c.FWID, wt, wslot, wkey)
        for tc in range(c.NTC):
            isc = tc == c.LCH
            if isc and last:
                continue
            t0 = tc * 128
            rows = CT if isc else 128
            ti = len(tiles) - 1 if isc else t0 // TS
            b = nb()
            mm_group(b, [wkey] + [("hT", ti, k) for k in range(KD)],
                     [(hT[:, k, t0:t0 + rows], wt[:, k, 0:c.FWID]) for k in range(KD)],
                     outap=ps[b][0:rows, 0:c.FWID])
            st, _, skey = stg.next()
            if tc % 2:
                s.op("act", [("ps", b)], [skey], lambda e, st=st, b=b, rows=rows: e.activation(
                    out=st[0:rows, 0:c.FWID], in_=ps[b][0:rows, 0:c.FWID], func=AF.Copy))
            else:
                s.op("dve", [("ps", b)], [skey], lambda e, st=st, b=b, rows=rows: e.tensor_copy(
                    out=st[0:rows, 0:c.FWID], in_=ps[b][0:rows, 0:c.FWID]))
            s.dma("sp", stslot(), [skey], [("exa", tc)], [
                lambda q, st=st, tc=tc, rows=rows: q.dma_start(out=exa_t[tc], in_=st[0:rows, 0:c.FWID])])
            s.collective([("exa", tc)], [("ga", tc)], lambda g, tc=tc: g.collective_compute(
                "AllGather", ALU.bypass, replica_groups=groups, ins=[exa_t[tc].opt()], outs=[ga_t[tc].opt()]))

        PQ = Phase()
        cqf = PQ.t("cqf", [128, QC, NT], F32)
        cqn = PQ.t("cqn", [128, QC, NT], BF16)
        PQ.sqb = PQ.ring("sqb", 2, [128, max(QC, KVC), TS], BF16)
        PQ.rt = PQ.ring("rt", 2, [128, TS], F32)
        PQ.rr = PQ.ring("rr", 2, [128, TS], F32)
        tmp64 = PQ.ring("t64", 4, [64, TS], F32)
        wuq = PQ.t("wuq", [128, QC, H * c.QKH], BF16)
        s.dma("pool", s.slot("wuq%d" % l), [], ["wuq"], [
            lambda q: q.dma_start(out=wuq[:], in_=w_uq[l].rearrange("(k p) c -> p k c", p=128))])
        qnf = PQ.ring("qnf", 2, [128, TS], F32)
        sqA = PQ.ring("sqA", 2, [128, TS], BF16)
        sqB = PQ.ring("sqB", 2, [64, TS], BF16)
        rqr = PQ.ring("rq", 2, [128, TS], F32)
        qnb = PQ.ring("qnb", 2, [128, TS], BF16)
        qrb = PQ.ring("qrb", 2, [64, TS], BF16)
        rtq = PQ.ring("rtq", 2, [128, TS], F32)
        proj_block(c.OFF_CQ, c.QR, [(i * 128, 128, 0, ev_copy(lambda i_, t0, sz, M: cqf[:, i_, t0:t0 + sz], lambda i_, ti: ("cqf", i_, ti)))
                                    for i in range(QC)], act_tiles)
        for ti, (t0, sz, isc) in enumerate(tiles):
            if (t0, sz, isc) not in act_tiles:
                continue
            feat_norm(PQ, cqf, QC, c.QR, "qa", cqn, ti, t0, sz, "cqf", "cqn")
            cq_reads = [("cqn", i, ti) for i in range(QC)]
            for h in range(H):
                hb = h * c.QKH
                bA = nb()
                mm_group(bA, ["wuq"] + cq_reads,
                         [(wuq[:, i, hb:hb + 128], cqn[:, i, t0:t0 + sz]) for i in range(QC)], outap=ps[bA][:, 0:sz])
                bB = nb()
                mm_group(bB, ["wuq"] + cq_reads,
                         [(wuq[:, i, hb + 128:hb + 192], cqn[:, i, t0:t0 + sz]) for i in range(QC)], outap=ps[bB][0:64, 0:sz])
                bC = nb()

                def fnsw(e, hb=hb, bC=bC, t0=t0, sz=sz):
                    ins = None
                    for (pb, co) in ((0, hb + 160), (32, hb + 128)):
                        for i in range(QC):
                            ins = e.matmul(ps[bC][pb:pb + 32, 0:sz], wuq[:, i, co:co + 32], cqn[:, i, t0:t0 + sz],
                                           start=(i == 0), stop=(i == QC - 1))
                    return ins
                s.op("pe", ["wuq"] + cq_reads, [("ps", bC)], fnsw)
                sa, _, sak = sqA.next()
                sb_, _, sbk = sqB.next()
                s.op("act", [("ps", bA)], [sak], lambda e, sa=sa, bA=bA, sz=sz: e.activation(out=sa[:, 0:sz], in_=ps[bA][:, 0:sz], func=AF.Square))
                s.op("act", [("ps", bB)], [sbk], lambda e, sb_=sb_, bB=bB, sz=sz: e.activation(out=sb_[:, 0:sz], in_=ps[bB][0:64, 0:sz], func=AF.Square))
                bS = nb()
                mm_group(bS, [sak, sbk, "ones"], [(ones[:], sa[:, 0:sz]), (ones[0:64, :], sb_[:, 0:sz])], outap=ps[bS][:, 0:sz])
                rtt, _, rtk = rtq.next()
                rq, _, rqk = rqr.next()
                rms_rstd(bS, sz, c.QKH, rq[:, 0:sz], rqk, rtt, rtk)
                qn, _, qnk = qnf.next()
                s.op("dve", [("ps", bA), ("gsc", 0)], [qnk], lambda e, qn=qn, bA=bA, sz=sz: e.tensor_scalar(
                    out=qn[:, 0:sz], in0=ps[bA][:, 0:sz], scalar1=gsc[:, 0:1], scalar2=None, op0=ALU.mult))
                qb, _, qbk = qnb.next()
                s.op("dve", [qnk, rqk], [qbk], lambda e, qb=qb, qn=qn, rq=rq, sz=sz: e.tensor_tensor(
                    out=qb[:, 0:sz], in0=qn[:, 0:sz], in1=rq[:, 0:sz], op=ALU.mult))
                s.dma("sp", stslot(), [qbk], [("q_s", h, ti, 0)], [
                    lambda q, qb=qb, h=h, t0=t0, sz=sz: q.dma_start(out=q_s[h, 0:128, t0:t0 + sz], in_=qb[:, 0:sz])])
                t1, _, k1 = tmp64.next()
                t2, _, k2 = tmp64.next()
                s.op("dve", [("ps", bB), ("gsc", 1), "rope"], [k1], lambda e, t1=t1, bB=bB, t0=t0, sz=sz: e.scalar_tensor_tensor(
                    out=t1[:, 0:sz], in0=ps[bB][0:64, 0:sz], scalar=gsc[0:64, 1:2], in1=rope[:, 0, t0:t0 + sz],
                    op0=ALU.mult, op1=ALU.mult))
                s.op("dve", [("ps", bC), ("gsc", 2), "rope"], [k2], lambda e, t2=t2, bC=bC, t0=t0, sz=sz: e.scalar_tensor_tensor(
                    out=t2[:, 0:sz], in0=ps[bC][0:64, 0:sz], scalar=gsc[0:64, 2:3], in1=rope[:, 1, t0:t0 + sz],
                    op0=ALU.mult, op1=ALU.mult))
                s.op("dve", [k1, k2], [k1], lambda e, t1=t1, t2=t2, sz=sz: e.tensor_tensor(
                    out=t1[:, 0:sz], in0=t1[:, 0:sz], in1=t2[:, 0:sz], op=ALU.add))
                qb2, _, qbk2 = qrb.next()
                s.op("dve", [k1, rqk], [qbk2], lambda e, t1=t1, qb2=qb2, rq=rq, sz=sz: e.tensor_tensor(
                    out=qb2[:, 0:sz], in0=t1[:, 0:sz], in1=rq[0:64, 0:sz], op=ALU.mult))
                s.dma("sp", stslot(), [qbk2], [("q_s", h, ti, 1)], [
                    lambda q, qb2=qb2, h=h, t0=t0, sz=sz: q.dma_start(out=q_s[h, 128:192, t0:t0 + sz], in_=qb2[:, 0:sz])])
        PQ.close()

        PC = Phase()
        uT = PC.t("uT", [128, CC, NT + 4], BF16)
        pbT = PC.t("pbT", [128, CC, NT], BF16)
        cxT = PC.t("cxT", [128, CC, NT], BF16)
        exct = PC.t("exct", [128, CC, 4], F32)
        gct = PC.t("gct", [128, R, CC, 4], F32)
        hal = PC.t("hal", [128, CC, 4], F32)
        ctmp = PC.ring("ctmp", 2, [128, LT], F32)

        def ucol(t0):
            return t0 + 1 if t0 < LT else t0 + 3

        proj_block(c.OFF_CX, c.CW, [(i * 128, 128, 0, ev_copy(lambda i_, t0, sz, M: cxT[:, i_, t0:t0 + sz], lambda i_, ti: ("cxT", i_, ti)))
                                    for i in range(CC)], act_tiles)

        def ev_u(b, ti, t0, sz, M, idx, pb):
            uc = ucol(t0)
            s.op("dve", [("ps", b), ("cxT", idx, ti)], [("uT", idx, ti)], lambda e: e.tensor_tensor(
                out=uT[:, idx, uc:uc + sz], in0=ps[b][:, 0:sz], in1=cxT[:, idx, t0:t0 + sz], op=ALU.mult))
        proj_block(c.OFF_CC, c.CW, [(i * 128, 128, 0, ev_u) for i in range(CC)], act_tiles)
        proj_block(c.OFF_CB, c.CW, [(i * 128, 128, 0, ev_copy(lambda i_, t0, sz, M: pbT[:, i_, t0:t0 + sz], lambda i_, ti: ("pbT", i_, ti)))
                                    for i in range(CC)], act_tiles)
        ti_last_lat = c.NLT - 1
        ti_ctx = len(tiles) - 1
        srcs = [(1, 0), (LT, ti_last_lat)]
        if not last:
            srcs += [(LT + 3, ti_ctx), (LT + 3 + CT - 1, ti_ctx)]
        else:
            s.op("dve", [], [("exct", 2), ("exct", 3)], lambda e: e.memset(exct[:, :, 2:4], 0.0))
        for kind, (col, ti) in enumerate(srcs):
            s.op("dve", [("uT", i, ti) for i in range(CC)], [("exct", kind)], lambda e, kind=kind, col=col: e.tensor_copy(
                out=exct[:, :, kind], in_=uT[:, :, col]))
        s.dma("sp", stslot(), [("exct", k) for k in range(4)], ["exc"], [
            lambda q: q.dma_start(out=exc, in_=exct[:].rearrange("p c k -> p (c k)"))])
        s.collective(["exc"], ["gc"], lambda g: g.collective_compute(
            "AllGather", ALU.bypass, replica_groups=groups, ins=[exc.opt()], outs=[gc.opt()]))
        s.dma("sp", ldslot(), ["gc"], ["gct"], [
            lambda q: q.dma_start(out=gct[:].rearrange("p r c k -> p r (c k)"), in_=gc.rearrange("(r p) x -> p r x", p=128))])
        streams = [(0, LT, 1, 1, 0)] if last else [(0, LT, 1, 1, 0), (LT, CT, 3, 3, 2)]
        for (t0, n, uo, kl, kr_) in streams:
            for side, kind, selo in ((0, kl, 0), (1, kr_, 4)):
                hk = ("hal", t0, side)
                dst = hal[:, :, (0 if t0 == 0 else 2) + side]
                s.op("dve", ["gct", "vecs"], [hk], lambda e, dst=dst, kind=kind, selo=selo: e.tensor_scalar(
                    out=dst, in0=gct[:, 0, :, kind], scalar1=vcol("sel", selo), scalar2=None, op0=ALU.mult))
                for r in range(1, R):
                    s.op("dve", ["gct", "vecs", hk], [hk], lambda e, dst=dst, kind=kind, selo=selo, r=r: e.scalar_tensor_tensor(
                        out=dst, in0=gct[:, r, :, kind], scalar=vcol("sel", selo + r), in1=dst, op0=ALU.mult, op1=ALU.add))
                ucolumn = (t0 + uo - 1) if side == 0 else (t0 + uo + n)
                s.op("dve", [hk], [("uTh", t0, side)], lambda e, dst=dst, ucolumn=ucolumn: e.tensor_copy(
                    out=uT[:, :, ucolumn], in_=dst))
            tis = [ti for ti, tl in enumerate(tiles) if tl[0] >= t0 and tl[0] < t0 + n]
            for i in range(CC):
                ureads = [("uT", i, ti) for ti in tis] + [("uTh", t0, 0), ("uTh", t0, 1), "vecs"]
                ct, _, ck = ctmp.next()
                base = t0 + uo
                s.op("dve", ureads, [ck], lambda e, ct=ct, i=i, base=base, n=n: e.tensor_scalar(
                    out=ct[:, 0:n], in0=uT[:, i, base - 1:base - 1 + n], scalar1=vcol(("convw", l), 0 * CC + i), scalar2=None, op0=ALU.mult))
                s.op("dve", ureads + [ck], [ck], lambda e, ct=ct, i=i, base=base, n=n: e.scalar_tensor_tensor(
                    out=ct[:, 0:n], in0=uT[:, i, base:base + n], scalar=vcol(("convw", l), 1 * CC + i), in1=ct[:, 0:n], op0=ALU.mult, op1=ALU.add))
                s.op("dve", ureads + [ck], [ck], lambda e, ct=ct, i=i, base=base, n=n: e.scalar_tensor_tensor(
                    out=ct[:, 0:n], in0=uT[:, i, base + 1:base + 1 + n], scalar=vcol(("convw", l), 2 * CC + i), in1=ct[:, 0:n], op0=ALU.mult, op1=ALU.add))
                s.op("dve", [ck] + [("pbT", i, ti) for ti in tis], [("convT", i, t0)], lambda e, ct=ct, i=i, n=n, t0=t0: e.tensor_tensor(
                    out=convT[:, i, t0:t0 + n], in0=ct[:, 0:n], in1=pbT[:, i, t0:t0 + n], op=ALU.mult))
        PC.close()
        PA.close()
        if upto == "p1" and l == 0:
            PXc.close()
            break

        PA = Phase()
        fT = PA.t("fT", [128, c.FG, NT], BF16)
        PF_ = Phase()
        pf = PF_.t("pf", [128, NKC, c.FWID], BF16)
        fns = []
        for tc in range(c.LCH):
            fns.append(lambda q, tc=tc: q.dma_start(out=pf[:, tc * R:(tc + 1) * R, :], in_=ga_t[tc].rearrange("(r p) f -> p r f", p=128)))
        if not last:
            for r in range(R):
                kc = c.SEQ // 128 + (r * CT) // 128
                p0 = (r * CT) % 128
                fns.append(lambda q, r=r, kc=kc, p0=p0: q.dma_start(out=pf[p0:p0 + CT, kc, :], in_=ga_t[c.LCH][r * CT:(r + 1) * CT, :]))
        s.dma("sp", ldslot(), [("ga", tc) for tc in range(c.NTC) if not (last and tc == c.LCH)], ["pf"], fns)
        csr = PF_.ring("csr", 4, [128, 2, TS], BF16)
        zt = PF_.ring("zt", 2, [128, 4, TS], BF16)
        fstreams = [(tl, False) for tl in tiles if not tl[2]] + ([] if last else [(tiles[-1], True)])
        for gp in range(c.FG // 2):
            for ((t0, sz, isc), _) in fstreams:
                banks = [nb() for _ in range(4)]
                if not isc:
                    ksteps = []
                    for k in range(c.SEQ // 128):
                        p0 = (k % R) * LT + (k // R) * 128
                        ksteps.append((k, 128, dft_c[p0:p0 + 128, t0:t0 + sz], dft_s[p0:p0 + 128, t0:t0 + sz]))
                else:
                    ksteps = [(c.SEQ // 128 + k, c.CKS, dftx_c[k * c.CKS:(k + 1) * c.CKS, :], dftx_s[k * c.CKS:(k + 1) * c.CKS, :])
                              for k in range(c.NCK)]
                nks = len(ksteps)
                for si, (kc, ksz, srcc, srcs_) in enumerate(ksteps):
                    cs, cslot, ckey = csr.next()
                    s.dma("sp", cslot, [], [ckey], [
                        lambda q, cs=cs, srcc=srcc, ksz=ksz: q.dma_start(out=cs[0:ksz, 0, 0:sz], in_=srcc),
                        lambda q, cs=cs, srcs_=srcs_, ksz=ksz: q.dma_start(out=cs[0:ksz, 1, 0:sz], in_=srcs_)])

                    def fn(e, cs=cs, kc=kc, ksz=ksz, si=si):
                        ins = None
                        for gi in range(2):
                            g = gp * 2 + gi
                            for tr in range(2):
                                ins = e.matmul(ps[banks[gi * 2 + tr]][:, 0:sz], pf[0:ksz, kc, g * 128:(g + 1) * 128],
                                               cs[0:ksz, tr, 0:sz], start=(si == 0), stop=(si == nks - 1))
                        return ins
                    s.op("pe", ["pf", ckey], [("ps", b) for b in banks], fn)
                z, _, zkey = zt.next()
                for j4 in range(4):
                    if j4 % 2:
                        s.op("act", [("ps", banks[j4])], [(zkey, j4)], lambda e, z=z, j4=j4: e.activation(
                            out=z[:, j4, 0:sz], in_=ps[banks[j4]][:, 0:sz], func=AF.Copy))
                    else:
                        s.op("dve", [("ps", banks[j4])], [(zkey, j4)], lambda e, z=z, j4=j4: e.tensor_copy(
                            out=z[:, j4, 0:sz], in_=ps[banks[j4]][:, 0:sz]))
                for gi in range(2):
                    g = gp * 2 + gi
                    b = nb()
                    mm_group(b, ["dch", (zkey, gi * 2), (zkey, gi * 2 + 1)],
                             [(dch[:, 0, :], z[:, gi * 2, 0:sz]), (dch[:, 1, :], z[:, gi * 2 + 1, 0:sz])], outap=ps[b][:, 0:sz])
                    s.op("act" if gi else "dve", [("ps", b)], [("fT", g, t0)],
                         (lambda e, g=g, b=b: e.activation(out=fT[:, g, t0:t0 + sz], in_=ps[b][:, 0:sz], func=AF.Copy)) if gi else
                         (lambda e, g=g, b=b: e.tensor_copy(out=fT[:, g, t0:t0 + sz], in_=ps[b][:, 0:sz])))
        PF_.close()
        if upto == "fourier" and l == 0:
            PA.close()
            PXc.close()
            break

        attnT = PA.t("attnT", [128, H, NT], BF16)
        PT = Phase()
        ckv = PT.t("ckv", [128, KVC, NK], BF16)
        krk = PT.t("krk", [128, NK], BF16)
        wukv = PT.t("wukv", [128, KVC, H * (c.NOPE + c.VH)], BF16)
        s.dma("pool", s.slot("wukv%d" % l), [], ["wukv"], [
            lambda q: q.dma_start(out=wukv[:], in_=w_ukv[l].rearrange("(k p) c -> p k c", p=128))])
        fns = []
        for xi in range(NXB):
            if xi < 2 * KVC:
                i, hf = xi // 2, xi % 2
                dl = ckv[hf * 64:(hf + 1) * 64, i, 0:c.SEQ]
                dc = ckv[hf * 64:(hf + 1) * 64, i, c.SEQ:NK]
            else:
                j = xi - 2 * KVC
                dl = krk[j * 64:(j + 1) * 64, 0:c.SEQ]
                dc = krk[j * 64:(j + 1) * 64, c.SEQ:NK]
            fns.append(lambda q, xi=xi, dl=dl: q.dma_start(out=dl.rearrange("p (r t) -> p r t", r=R),
                                                         in_=gb_t[xi][:, 0:LT].rearrange("(r p) t -> p r t", p=64)))
            fns.append(lambda q, xi=xi, dc=dc: q.dma_start(out=dc.rearrange("p (r t) -> p r t", r=R),
                                                         in_=gb_t[xi][:, LT:NT].rearrange("(r p) t -> p r t", p=64)))
        s.dma("sp", ldslot(), [("gb", xi) for xi in range(NXB)], ["ckv", "krk"], fns)
        knr = PT.ring("kn", 2, [128, NK], BF16)
        vr = PT.ring("vv", 2, [128, NKC, c.VH], BF16)
        rkr = PT.ring("rk", 2, [128, NKC], F32)
        rkt = PT.ring("rkt", 2, [128, NKC], F32)
        sqk = PT.ring("sqk", 2, [128, 512], BF16)
        qnr = PT.ring("qn", 2, [128, NT], BF16)
        qrr = PT.ring("qr", 2, [64, NT], BF16)
        ptr = PT.ring("pt", 3, [128, TS], BF16)
        rsr = PT.ring("rs", 2, [128, TS], F32)
        KT = min(512, c.SEQ)
        ktiles = [(i * KT, KT) for i in range(c.SEQ // KT)] + [(c.SEQ, c.CTX)]
        for h in range(H):
            kn, _, knk = knr.next()
            vv, _, vk = vr.next()
            rk, _, rkk = rkr.next()
            c0k = h * (c.NOPE + c.VH)
            bss = reserve()
            for (k0, kn_sz) in ktiles:
                b = nb()
                mm_group(b, ["wukv", "ckv"], [(wukv[:, i, c0k:c0k + 128], ckv[:, i, k0:k0 + kn_sz]) for i in range(KVC)],
                         outap=ps[b][:, 0:kn_sz])
                s.op("act", [("ps", b), "vecs"], [(knk, k0)], lambda e, kn=kn, b=b, k0=k0, kn_sz=kn_sz: e.activation(
                    out=kn[:, k0:k0 + kn_sz], in_=ps[b][:, 0:kn_sz], func=AF.Identity, scale=vcol(("gk_n", l)), bias=0.0))
                sq, _, sqkey = sqk.next()
                s.op("act", [("ps", b)], [sqkey], lambda e, sq=sq, b=b, kn_sz=kn_sz: e.activation(
                    out=sq[:, 0:kn_sz], in_=ps[b][:, 0:kn_sz], func=AF.Square))
                sub = [(ci, cs_, csz) for ci, (cs_, csz) in enumerate(kchunks) if cs_ >= k0 and cs_ < k0 + kn_sz]

                def fn(e, sq=sq, sub=sub, k0=k0):
                    ins = None
                    for (ci, cs_, csz) in sub:
                        e.matmul(ps[bss][0:csz, ci:ci + 1], sq[:, cs_ - k0:cs_ - k0 + csz], ones[:, 0:1], start=True, stop=False)
                        ins = e.matmul(ps[bss][0:csz, ci:ci + 1], krk[64:128, cs_:cs_ + csz], ones[64:128, 0:1], start=False, stop=True)
                    return ins
                s.op("pe", [sqkey, "krk", "ones"], [("ps", bss)], fn)
            if c.CKS < 128:
                pass
            rt_, _, rtk_ = rkt.next()
            for (ci, (cs_, csz)) in enumerate(kchunks):
                pass
            full = [ci for ci, (cs_, csz) in enumerate(kchunks) if csz == 128]
            part = [ci for ci, (cs_, csz) in enumerate(kchunks) if csz < 128]
            nf = len(full)
            s.op("act", [("ps", bss), "epsb"], [(rtk_, 0)], lambda e, rt_=rt_: e.activation(
                out=rt_[:, 0:nf], in_=ps[bss][:, 0:nf], func=AF.Sqrt, scale=1.0 / c.QKH, bias=epsb[:, 0:1]))
            s.op("dve", [(rtk_, 0)], [(rkk, 0)], lambda e, rt_=rt_, rk=rk: e.reciprocal(out=rk[:, 0:nf], in_=rt_[:, 0:nf]))
            for ci in part:
                csz = kchunks[ci][1]
                s.op("act", [("ps", bss), "epsb"], [(rtk_, 1)], lambda e, rt_=rt_, ci=ci, csz=csz: e.activation(
                    out=rt_[0:csz, ci:ci + 1], in_=ps[bss][0:csz, ci:ci + 1], func=AF.Sqrt, scale=1.0 / c.QKH, bias=epsb[0:csz, 0:1]))
                s.op("dve", [(rtk_, 1)], [(rkk, 1)], lambda e, rt_=rt_, rk=rk, ci=ci, csz=csz: e.reciprocal(
                    out=rk[0:csz, ci:ci + 1], in_=rt_[0:csz, ci:ci + 1]))
            rk_reads = [(rkk, 0)] + ([(rkk, 1)] if part else [])
            release(bss)
            c0v = c0k + c.NOPE
            for g0 in range(0, NKC, 4):
                b = nb()
                grp = list(range(g0, min(NKC, g0 + 4)))

                def fn(e, grp=grp, b=b):
                    ins = None
                    for gi, ci in enumerate(grp):
                        cs_, csz = kchunks[ci]
                        for i in range(KVC):
                            ins = e.matmul(ps[b][0:csz, gi * 128:(gi + 1) * 128], ckv[:, i, cs_:cs_ + csz], wukv[:, i, c0v:c0v + c.VH],
                                           start=(i == 0), stop=(i == KVC - 1))
                    return ins
                s.op("pe", ["wukv", "ckv"], [("ps", b)], fn)
                allfull = all(kchunks[ci][1] == 128 for ci in grp)
                if allfull:
                    ng = len(grp)
                    s.op("dve" if (g0 // 4) % 2 else "act", [("ps", b)], [(vk, g0)],
                         (lambda e, vv=vv, b=b, g0=g0, ng=ng: e.tensor_copy(out=vv[:, g0:g0 + ng, :].rearrange("p a b -> p (a b)"), in_=ps[b][:, 0:ng * 128])) if (g0 // 4) % 2 else
                         (lambda e, vv=vv, b=b, g0=g0, ng=ng: e.activation(out=vv[:, g0:g0 + ng, :].rearrange("p a b -> p (a b)"), in_=ps[b][:, 0:ng * 128], func=AF.Copy)))
                else:
                    for gi, ci in enumerate(grp):
                        csz = kchunks[ci][1]
                        s.op("dve", [("ps", b)], [(vk, g0, gi)], lambda e, vv=vv, b=b, gi=gi, ci=ci, csz=csz: e.tensor_copy(
                            out=vv[0:csz, ci, :], in_=ps[b][0:csz, gi * 128:(gi + 1) * 128]))
            v_reads = []
            for g0 in range(0, NKC, 4):
                grp = list(range(g0, min(NKC, g0 + 4)))
                if all(kchunks[ci][1] == 128 for ci in grp):
                    v_reads.append((vk, g0))
                else:
                    v_reads += [(vk, g0, gi) for gi in range(len(grp))]
            k_reads = [(knk, k0) for (k0, _) in ktiles]
            qn, qslot, qnk = qnr.next()
            qr_, qslot2, qrk = qrr.next()
            qtis = [ti for ti, tl in enumerate(tiles) if tl in act_tiles]
            ncols = LT if last else NT
            s.dma("sp", qslot, [("q_s", h, ti, 0) for ti in qtis], [qnk], [
                lambda q, qn=qn, h=h: q.dma_start(out=qn[:, 0:ncols], in_=q_s[h, 0:128, 0:ncols])])
            s.dma("sp", qslot2, [("q_s", h, ti, 1) for ti in qtis], [qrk], [
                lambda q, qr_=qr_, h=h: q.dma_start(out=qr_[:, 0:ncols], in_=q_s[h, 128:192, 0:ncols])])
            for ti in qtis:
                t0, sz, isc = tiles[ti]
                kcs = ctx_kchunks if isc else list(range(NKC))
                bo = reserve()
                bsum = reserve()
                sbanks = {}
                pts = {}

                def emit_s(ci):
                    cs_, csz = kchunks[ci]
                    b = nb()
                    sbanks[ci] = b
                    mm_group(b, k_reads + ["krk", qnk, qrk],
                             [(kn[:, cs_:cs_ + csz], qn[:, t0:t0 + sz]), (krk[0:64, cs_:cs_ + csz], qr_[:, t0:t0 + sz])],
                             outap=ps[b][0:csz, 0:sz])

                def emit_exp(ci):
                    cs_, csz = kchunks[ci]
                    b = sbanks[ci]
                    pt, _, pk = ptr.next()
                    pts[ci] = (pt, pk)
                    s.op("act", [("ps", b)] + rk_reads, [pk], lambda e, pt=pt, b=b, ci=ci, csz=csz: e.activation(
                        out=pt[0:csz, 0:sz], in_=ps[b][0:csz, 0:sz], func=AF.Exp, scale=rk[0:csz, ci:ci + 1]))

                def emit_pv(idx, ci):
                    cs_, csz = kchunks[ci]
                    pt, pk = pts[ci]

                    def fn(e):
                        e.matmul(ps[bo][:, 0:sz], vv[0:csz, ci, :], pt[0:csz, 0:sz], start=(idx == 0), stop=(idx == len(kcs) - 1))
                        return e.matmul(ps[bsum][:, 0:sz], ones[0:csz, :], pt[0:csz, 0:sz], start=(idx == 0), stop=(idx == len(kcs) - 1))
                    s.op("pe", v_reads + [pk, "ones"], [("ps", bo), ("ps", bsum)], fn)

                emit_s(kcs[0])
                if len(kcs) > 1:
                    emit_s(kcs[1])
                for idx, ci in enumerate(kcs):
                    emit_exp(ci)
                    emit_pv(idx, ci)
                    if idx + 2 < len(kcs):
                        emit_s(kcs[idx + 2])
                rs, _, rsk = rsr.next()
                s.op("dve", [("ps", bsum)], [rsk], lambda e, rs=rs, bsum=bsum: e.reciprocal(out=rs[:, 0:sz], in_=ps[bsum][:, 0:sz]))
                s.op("dve", [("ps", bo), rsk], [("attnT", h, ti)], lambda e, rs=rs, bo=bo, h=h: e.tensor_tensor(
                    out=attnT[:, h, t0:t0 + sz], in0=ps[bo][:, 0:sz], in1=rs[:, 0:sz], op=ALU.mult))
                release(bo, bsum)
        PT.close()
        if upto == "attn" and l == 0:
            PA.close()
            PXc.close()
            break

        PM = Phase()
        hT = PM.t("hTm", [128, KD, NT], BF16)
        for ti, (t0, sz, isc) in enumerate(tiles):
            if (t0, sz, isc) not in act_tiles:
                continue
            s.dma("sp", ldslot(), [("h_s", ti)], [("hTm", ti)], [
                lambda q, t0=t0, sz=sz: q.dma_start(out=hT[:, :, t0:t0 + sz], in_=h_s[:, :, t0:t0 + sz].rearrange("k p t -> p k t"))])
        nbr = c.FG + H + CC
        wmr = PM.ring("wm", 2, [128, nbr, 128], BF16)
        gtr = PM.ring("gt", 3, [128, TS], F32)
        ttr = PM.ring("tt", 3, [128, TS], F32)
        mstg = PM.ring("mstg", 2, [128, TS], BF16)
        for j in range(KD):
            wg, wgslot, wgkey = wring.next()
            fns = []
            for br in range(3):
                for a in range(0, KD, 8):
                    bnd = min(KD, a + 8)
                    src = w_in[l, a * 128:bnd * 128, c.OFF_G + br * D + j * 128:c.OFF_G + br * D + (j + 1) * 128].rearrange("(k p) c -> p k c", p=128)
                    fns.append(lambda q, src=src, a=a, bnd=bnd, br=br, wg=wg: q.dma_start(out=wg[:, a:bnd, br * 128:(br + 1) * 128], in_=src))
            s.dma("pool", wgslot, [], [wgkey], fns)
            wm, wmslot, wmkey = wmr.next()
            s.dma("pool", wmslot, [], [wmkey], [
                lambda q, wm=wm, j=j: q.dma_start(out=wm[:, 0:c.FG, :], in_=w_f_out[l, :, j * 128:(j + 1) * 128].rearrange("(k p) c -> p k c", p=128)),
                lambda q, wm=wm, j=j: q.dma_start(out=wm[:, c.FG:c.FG + H, :], in_=w_mla_out[l, :, j * 128:(j + 1) * 128].rearrange("(k p) c -> p k c", p=128)),
                lambda q, wm=wm, j=j: q.dma_start(out=wm[:, c.FG + H:nbr, :], in_=w_conv_out[l, :, j * 128:(j + 1) * 128].rearrange("(k p) c -> p k c", p=128))])
            for ti, (t0, sz, isc) in enumerate(tiles):
                if (t0, sz, isc) not in act_tiles:
                    continue
                hreads = [("hTm", ti)]
                gts = []
                for br in range(3):
                    b = nb()
                    mm_group(b, [wgkey] + hreads, [(wg[:, k, br * 128:(br + 1) * 128], hT[:, k, t0:t0 + sz]) for k in range(KD)],
                             outap=ps[b][:, 0:sz])
                    gt, _, gk = gtr.next()
                    s.op("act", [("ps", b), "vecs"], [gk], lambda e, gt=gt, b=b, br=br, j=j: e.activation(
                        out=gt[:, 0:sz], in_=ps[b][:, 0:sz], func=AF.Sigmoid, bias=vcol(("bgate", l), br * KD + j), scale=1.0))
                    gts.append((gt, gk))
                srcs3 = [
                    ([(wm[:, g, :], fT[:, g, t0:t0 + sz]) for g in range(c.FG)], [("fT", g, t0) for g in range(c.FG)]),
                    ([(wm[:, c.FG + h, :], attnT[:, h, t0:t0 + sz]) for h in range(H)], [("attnT", h, ti) for h in range(H)]),
                    ([(wm[:, c.FG + H + i, :], convT[:, i, t0:t0 + sz]) for i in range(CC)],
                     [("convT", i, 0 if not isc else LT) for i in range(CC)]),
                ]
                tts = []
                for br in range(3):
                    b = nb()
                    mm_group(b, [wmkey] + srcs3[br][1], srcs3[br][0], outap=ps[b][:, 0:sz])
                    tt, _, tk = ttr.next()
                    gt, gk = gts[br]
                    s.op("dve", [("ps", b), gk], [tk], lambda e, tt=tt, b=b, gt=gt: e.tensor_tensor(
                        out=tt[:, 0:sz], in0=ps[b][:, 0:sz], in1=gt[:, 0:sz], op=ALU.mult))
                    tts.append((tt, tk))
                s.op("dve", [tts[0][1], tts[1][1]], [tts[0][1]], lambda e, a=tts[0][0], b_=tts[1][0]: e.tensor_tensor(
                    out=a[:, 0:sz], in0=a[:, 0:sz], in1=b_[:, 0:sz], op=ALU.add))
                ms, _, mk = mstg.next()
                s.op("dve", [tts[0][1], tts[2][1]], [mk], lambda e, ms=ms, a=tts[0][0], b_=tts[2][0]: e.tensor_tensor(
                    out=ms[:, 0:sz], in0=a[:, 0:sz], in1=b_[:, 0:sz], op=ALU.add))
                s.dma("sp", stslot(), [mk], [("m_s", j, ti)], [
                    lambda q, ms=ms, j=j, t0=t0, sz=sz: q.dma_start(out=m_s[j, :, t0:t0 + sz], in_=ms[:, 0:sz])])
        PM.close()
        PA.close()
        PXc.close()
        if upto == "merge" and l == 0:
            break

        def residual_update(P, xcr, b, j, ti, t0, sz, isc, gate_mi, final):
            m = 1 if isc else 0
            xc, xslot, xk = xcr.next()
            s.dma("sp", xslot, [("xs", j, ti)], [xk], [
                lambda q, xc=xc: q.dma_start(out=xc[:, 0:sz], in_=xs[j, :, t0:t0 + sz])])
            s.op("dve", [("ps", b), xk, "modt"], [xk], lambda e, xc=xc: e.scalar_tensor_tensor(
                out=xc[:, 0:sz], in0=ps[b][:, 0:sz], scalar=modt[:, gate_mi, j, m:m + 1], in1=xc[:, 0:sz],
                op0=ALU.mult, op1=ALU.add))
            if final:
                s.dma("sp", stslot(), [xk], [("out", j, ti)], [
                    lambda q, xc=xc: q.dma_start(out=out_T[j, :, t0:t0 + sz], in_=xc[:, 0:sz])])
            else:
                s.dma("sp", stslot(), [xk], [("xs", j, ti)], [
                    lambda q, xc=xc: q.dma_start(out=xs[j, :, t0:t0 + sz], in_=xc[:, 0:sz])])

        PO = Phase()
        wo = PO.t("wo", [128, KD, D], BF16)
        woslot = s.slot("wo%d" % l)
        fns = []
        for a in range(0, KD, 4):
            src = w_out[l, a * 128:(a + 4) * 128, :].rearrange("(k p) c -> p k c", p=128) if KD >= 4 else None
            if KD >= 4:
                fns.append(lambda q, src=src, a=a: q.dma_start(out=wo[:, a:a + 4, :], in_=src))
        if KD < 4:
            fns = [lambda q: q.dma_start(out=wo[:], in_=w_out[l].rearrange("(k p) c -> p k c", p=128))]
        s.dma("pool", woslot, [], ["wo"], fns)
        mtr = PO.ring("mt", 2, [128, KD, TS], BF16)
        xcr = PO.ring("xc", 3, [128, TS], F32)
        for ti, (t0, sz, isc) in enumerate(tiles):
            if (t0, sz, isc) not in act_tiles:
                continue
            mt, mslot, mtk = mtr.next()
            s.dma("sp", mslot, [("m_s", j, ti) for j in range(KD)], [mtk], [
                lambda q, mt=mt, t0=t0, sz=sz: q.dma_start(out=mt[:, :, 0:sz], in_=m_s[:, :, t0:t0 + sz].rearrange("k p t -> p k t"))])
            for j in range(KD):
                b = nb()
                mm_group(b, ["wo", mtk], [(wo[:, k, j * 128:(j + 1) * 128], mt[:, k, 0:sz]) for k in range(KD)], outap=ps[b][:, 0:sz])
                residual_update(PO, xcr, b, j, ti, t0, sz, isc, 2, False)
        PO.close()
        if upto == "mixer" and l == 0:
            break

        for m in range(2):
            s.res[("A", "x", m)] = s.res.get(("A", "nffn", m))
        PFN = Phase()
        h2 = PFN.t("h2", [128, KD, NT], BF16)
        PN2 = Phase()
        norm_phase(PN2, A2, 3, h2, act_tiles, False)
        PN2.close()
        aT = PFN.t("aT", [128, c.FH, NT], BF16)
        sgr = PFN.ring("sg", 3, [128, TS], F32)
        wdr = PFN.ring("wd", 2, [128, c.FH, 128], BF16)
        xcr = PFN.ring("xc", 3, [128, TS], F32)
        for half in range(2):
            for fl in range(c.FH):
                f = half * c.FH + fl
                wt, wslot, wkey = wring.next()
                fns = []
                for wi, wsrc in enumerate((w_ffn_gate, w_ffn_up)):
                    for a in range(0, KD, 8):
                        bnd = min(KD, a + 8)
                        src = wsrc[l, a * 128:bnd * 128, f * 128:(f + 1) * 128].rearrange("(k p) c -> p k c", p=128)
                        fns.append(lambda q, src=src, a=a, bnd=bnd, wi=wi, wt=wt: q.dma_start(out=wt[:, a:bnd, wi * 128:(wi + 1) * 128], in_=src))
                s.dma("pool", wslot, [], [wkey], fns)
                for ti, (t0, sz, isc) in enumerate(tiles):
                    if (t0, sz, isc) not in act_tiles:
                        continue
                    hreads = [("hT", ti, k) for k in range(KD)]
                    bg = nb()
                    mm_group(bg, [wkey] + hreads, [(wt[:, k, 0:128], h2[:, k, t0:t0 + sz]) for k in range(KD)], outap=ps[bg][:, 0:sz])
                    bu = nb()
                    mm_group(bu, [wkey] + hreads, [(wt[:, k, 128:256], h2[:, k, t0:t0 + sz]) for k in range(KD)], outap=ps[bu][:, 0:sz])
                    sg, _, sgk = sgr.next()
                    s.op("act", [("ps", bg)], [sgk], lambda e, sg=sg, bg=bg: e.activation(out=sg[:, 0:sz], in_=ps[bg][:, 0:sz], func=AF.Silu))
                    s.op("dve", [("ps", bu), sgk], [("aT", fl, ti)], lambda e, sg=sg, bu=bu, fl=fl: e.tensor_tensor(
                        out=aT[:, fl, t0:t0 + sz], in0=ps[bu][:, 0:sz], in1=sg[:, 0:sz], op=ALU.mult))
            for j in range(KD):
                wd, wdslot, wdk = wdr.next()
                fns = []
                for a in range(0, c.FH, 8):
                    bnd = min(c.FH, a + 8)
                    src = w_ffn_down[l, (half * c.FH + a) * 128:(half * c.FH + bnd) * 128, j * 128:(j + 1) * 128].rearrange("(k p) c -> p k c", p=128)
                    fns.append(lambda q, src=src, a=a, bnd=bnd, wd=wd: q.dma_start(out=wd[:, a:bnd, :], in_=src))
                s.dma("pool", wdslot, [], [wdk], fns)
                for ti, (t0, sz, isc) in enumerate(tiles):
                    if (t0, sz, isc) not in act_tiles:
                        continue
                    b = nb()
                    mm_group(b, [wdk] + [("aT", fl, ti) for fl in range(c.FH)],
                             [(wd[:, fl, :], aT[:, fl, t0:t0 + sz]) for fl in range(c.FH)], outap=ps[b][:, 0:sz])
                    residual_update(PFN, xcr, b, j, ti, t0, sz, isc, 5, last and half == 1)
        PFN.close()

    if upto is not None:
        s.barrier()
        s.dma("sp", stslot(), [], ["outdbg"], [lambda q: q.dma_start(out=out_T, in_=xs[:, :, 0:LT])])
    s.barrier()
    G.close()
    for p in reversed(ps_cm):
        p.__exit__(None, None, None)
    return nc, s


def host_inputs(cfg, inp):
    c = cfg
    f32 = np.float32
    KD = c.KD

    def fm(v, rows=128):
        v = np.asarray(v, f32)
        return np.ascontiguousarray(v.reshape(-1, rows).T)

    inv_freq = (c.THETA ** (-np.arange(16, dtype=f32) / f32(16))).astype(f32)
    cc_idx = np.arange(128)
    ang_c = 2.0 * np.pi * np.outer(cc_idx, cc_idx) / 128.0
    dft_ch = np.stack([np.cos(ang_c) / math.sqrt(128.0), -np.sin(ang_c) / math.sqrt(128.0)], axis=1).astype(ml_dtypes.bfloat16)
    shared = {}
    for k in ("w_ada", "w_in", "w_uq", "w_ukv", "w_f_out", "w_mla_out", "w_conv_out", "w_out",
              "w_ffn_gate", "w_ffn_up", "w_ffn_down"):
        shared[k] = np.ascontiguousarray(np.asarray(inp[k], f32))
    shared["dft_ch"] = dft_ch
    maps = []
    for core in range(8):
        b, r = core // R, core % R
        m = dict(shared)
        xl = np.asarray(inp["x"][b, r * c.LT:(r + 1) * c.LT, :], f32)
        xc = np.asarray(inp["ctx"][b, r * c.CT:(r + 1) * c.CT, :], f32)
        xt = np.concatenate([xl, xc], axis=0).T
        m["xT"] = np.ascontiguousarray(xt.reshape(KD, 128, c.NT))
        vecs = np.zeros((128, c.NV), f32)
        V = c.voff
        ct = np.stack([fm(inp["c"][b]), fm(inp["c_ctx"])], axis=2)
        vecs[:, V["cT"]:V["cT"] + KD * 2] = ct.reshape(128, KD * 2)
        if r > 0:
            vecs[:, V["sel"] + (r - 1)] = 1.0
        if r < R - 1:
            vecs[:, V["sel"] + 4 + (r + 1)] = 1.0
        for l in range(c.DEPTH):
            vecs[:, V[("nmix", l)]:V[("nmix", l)] + KD] = fm(inp["norm_mix"][l])
            vecs[:, V[("nffn", l)]:V[("nffn", l)] + KD] = fm(inp["norm_ffn"][l])
            ba = fm(inp["b_ada"][l])
            vecs[:, V[("bada", l)]:V[("bada", l)] + 12 * KD] = np.repeat(ba, 2, axis=1)
            vecs[:, V[("bgate", l)]:V[("bgate", l)] + 3 * KD] = fm(inp["b_gate"][l])
            vecs[:, V[("qa", l)]:V[("qa", l)] + c.QC] = fm(inp["q_a_norm"][l])
            vecs[:, V[("kva", l)]:V[("kva", l)] + c.KVC] = fm(inp["kv_a_norm"][l])
            for pre, nm in (("gq", "q_norm"), ("gk", "k_norm")):
                g = np.asarray(inp[nm][l], f32)
                vecs[:, V[(pre + "_n", l)]] = g[0:128]
                vecs[0:64, V[(pre + "_r", l)]] = g[128:192]
                vecs[0:64, V[(pre + "_s", l)]] = np.concatenate([g[160:192], g[128:160]])
            cw = np.asarray(inp["conv_w"][l], f32)
            for tap in range(3):
                vecs[:, V[("convw", l)] + tap * c.CC:V[("convw", l)] + (tap + 1) * c.CC] = fm(cw[tap])
        m["vecs"] = vecs
        t = (r * c.LT + np.arange(c.LT)).astype(np.int64)
        row = (t // c.GRID_W).astype(f32)
        col = (t % c.GRID_W).astype(f32)
        ang = np.concatenate([row[:, None] * inv_freq[None, :], col[:, None] * inv_freq[None, :]], axis=1).astype(f32)
        cs = np.cos(ang).astype(f32).T
        sn = np.sin(ang).astype(f32).T
        rope = np.zeros((64, 2, c.NT), f32)
        rope[:, 0, :] = 1.0
        rope[0:32, 0, :c.LT] = cs
        rope[32:64, 0, :c.LT] = cs
        rope[0:32, 1, :c.LT] = -sn
        rope[32:64, 1, :c.LT] = sn
        m["rope"] = rope
        tt = np.arange(c.SEQ, dtype=np.int64)[:, None]
        tp = (r * c.LT + np.arange(c.LT, dtype=np.int64))[None, :]
        a = 2.0 * np.pi * ((tt * tp) % c.SEQ).astype(np.float64) / c.SEQ
        m["dft_c"] = (np.cos(a) / math.sqrt(c.SEQ)).astype(ml_dtypes.bfloat16)
        m["dft_s"] = (np.sin(a) / math.sqrt(c.SEQ)).astype(ml_dtypes.bfloat16)
        tt = np.arange(c.CTX, dtype=np.int64)[:, None]
        tp = (r * c.CT + np.arange(c.CT, dtype=np.int64))[None, :]
        a = 2.0 * np.pi * ((tt * tp) % c.CTX).astype(np.float64) / c.CTX
        m["dftx_c"] = (np.cos(a) / math.sqrt(c.CTX)).astype(ml_dtypes.bfloat16)
        m["dftx_s"] = (np.sin(a) / math.sqrt(c.CTX)).astype(ml_dtypes.bfloat16)
        maps.append(m)
    return maps


def assemble(cfg, results):
    c = cfg
    out = np.zeros((c.B, c.SEQ, c.D), np.float32)
    for core in range(8):
        b, r = core // R, core % R
        o = np.asarray(results[core]["outT"], np.float32).reshape(c.D, c.LT)
        out[b, r * c.LT:(r + 1) * c.LT, :] = o.T
    return out


def kernel(**inputs):
    cfg = Cfg(FULL_CFG)
    nc, _ = build_program(cfg)
    maps = host_inputs(cfg, inputs)
    res = run_bass_kernel_spmd(nc, maps, core_ids=list(range(8)))
    return assemble(cfg, res.results)
```

```python
import math
import numpy as np
import ml_dtypes
import concourse.bass as bass
import concourse.mybir as mybir
from concourse.bass_utils import run_bass_kernel_spmd

F32 = mybir.dt.float32
BF16 = mybir.dt.bfloat16
AF = mybir.ActivationFunctionType
ALU = mybir.AluOpType

FULL_CFG = dict(D=2048, B=2, SEQ=4096, DEPTH=2, GRID_W=64, CTX=256, FG=4, H=8, QR=512, KVR=256,
                CW=512, DFF=5632, NOPE=128, ROPE=64, VH=128, EPS=1e-6, THETA=10000.0)
R = 4


class Cfg:
    def __init__(self, d):
        self.__dict__.update(d)
        c = self
        c.KD = c.D // 128
        c.FWID = c.FG * 128
        c.QC = c.QR // 128
        c.KVC = c.KVR // 128
        c.CC = c.CW // 128
        c.QKH = c.NOPE + c.ROPE
        c.OFF_F = 0
        c.OFF_CQ = c.OFF_F + c.FWID
        c.OFF_CKV = c.OFF_CQ + c.QR
        c.OFF_KR = c.OFF_CKV + c.KVR
        c.OFF_CX = c.OFF_KR + c.ROPE
        c.OFF_CB = c.OFF_CX + c.CW
        c.OFF_CC = c.OFF_CB + c.CW
        c.OFF_G = c.OFF_CC + c.CW
        c.N_IN = c.OFF_G + 3 * c.D
        c.LT = c.SEQ // R
        c.CT = c.CTX // R
        c.NT = c.LT + c.CT
        c.TS = d.get('TS', min(512, c.LT))
        c.NLT = c.LT // c.TS
        c.LCH = c.LT // 128
        c.NTC = c.LCH + 1
        c.NK = c.SEQ + c.CTX
        c.CKS = min(128, c.CTX)
        c.NCK = c.CTX // c.CKS
        c.NKC = c.SEQ // 128 + c.NCK
        c.FC = c.DFF // 128
        c.FH = c.FC // 2
        c.MW = 6 * c.D // R
        c.MJ = c.MW // 128
        assert c.MW % 128 == 0
        off = {}
        n = 0

        def add(name, w):
            nonlocal n
            off[name] = n
            n += w
        add("cT", c.KD * 2)
        add("sel", 8)
        for l in range(c.DEPTH):
            add(("nmix", l), c.KD)
            add(("nffn", l), c.KD)
            add(("bada", l), 6 * c.KD * 2)
            add(("bgate", l), 3 * c.KD)
            add(("qa", l), c.QC)
            add(("kva", l), c.KVC)
            for nm in ("gq_n", "gq_r", "gq_s", "gk_n", "gk_r", "gk_s"):
                add((nm, l), 1)
            add(("convw", l), 3 * c.CC)
        c.voff = off
        c.NV = n


class Slot:
    def __init__(self, sched):
        self.sched = sched
        self.sems = {}
        self.counts = {}

    def toks(self):
        return [(self.sems[k], self.counts[k]) for k in self.sems if self.counts[k]]

    def sem(self, kind):
        if kind not in self.sems:
            self.sems[kind] = self.sched.newsem("sl%d" % self.sched.nsem)
            self.counts[kind] = 0
        return self.sems[kind]


class Sched:
    def __init__(self, nc):
        self.nc = nc
        self.eng = {"pe": nc.tensor, "act": nc.scalar, "dve": nc.vector, "pool": nc.gpsimd, "sp": nc.sync}
        self.esem = {}
        self.ecnt = {}
        self.nsem = 0
        for e in ("pe", "act", "dve"):
            self.esem[e] = self.newsem("e_" + e)
            self.ecnt[e] = 0
        self.seen = {e: {} for e in self.eng}
        self.res = {}
        self.slots = []
        self.free_slots = []
        self.nwait = 0

    def newsem(self, name):
        self.nsem += 1
        return self.nc.semaphore(name).__enter__()

    def slot(self, name=None):
        if self.free_slots:
            return self.free_slots.pop()
        s = Slot(self)
        self.slots.append(s)
        return s

    def put_slot(self, sl):
        self.free_slots.append(sl)

    def _deps(self, e, reads, writes):
        toks = []
        for r in reads:
            st = self.res.get(r)
            if st and st[0]:
                toks.append(st[0])
        for w in writes:
            st = self.res.get(w)
            if st:
                if st[0]:
                    toks.append(st[0])
                toks.extend(st[1])
        return toks

    def _wait(self, e, toks):
        need = {}
        for (sem, v) in toks:
            if e == "pe" and sem is self.esem["pe"]:
                continue
            if self.seen[e].get(sem, 0) >= v:
                continue
            if need.get(sem, 0) < v:
                need[sem] = v
        for sem, v in need.items():
            self.eng[e].wait_ge(sem, v)
            self.seen[e][sem] = v
            self.nwait += 1

    def _record(self, reads, writes, tok):
        for r in reads:
            st = self.res.get(r)
            if st is None:
                st = self.res[r] = [None, []]
            st[1].append(tok)
        for w in writes:
            self.res[w] = [tok, []]

    def op(self, e, reads, writes, fn):
        self._wait(e, self._deps(e, reads, writes))
        ins = fn(self.eng[e])
        self.ecnt[e] += 1
        ins.then_inc(self.esem[e], 1)
        tok = (self.esem[e], self.ecnt[e])
        self.seen[e][self.esem[e]] = max(self.seen[e].get(self.esem[e], 0), 0)
        self._record(reads, writes, tok)
        return tok

    def dma(self, q, slot, reads, writes, fns):
        toks = self._deps(q, reads, writes)
        toks.extend(slot.toks())
        self._wait(q, toks)
        sem = slot.sem(q)
        for fn in fns:
            ins = fn(self.eng[q])
            ins.then_inc(sem, 16)
            slot.counts[q] += 16
        tok = (sem, slot.counts[q])
        self._record(reads, writes, tok)
        return tok

    def collective(self, reads, writes, fn):
        sem = self.newsem("cc%d" % self.nsem)
        self._wait("pool", self._deps("pool", reads, writes))
        fn(self.eng["pool"]).then_inc(sem)
        tok = (sem, 1)
        self._record(reads, writes, tok)
        return tok

    def barrier(self):
        toks = [(self.esem[e], self.ecnt[e]) for e in self.esem if self.ecnt[e]]
        for sl in self.slots:
            toks += sl.toks()
        for st in self.res.values():
            if st is None:
                continue
            if st[0]:
                toks.append(st[0])
            toks.extend(st[1])
        for e in self.eng:
            self._wait(e, toks)
        self.res = {}


class Ring:
    def __init__(self, s, nc, name, n, shape, dtype):
        self.bufs = [nc.sbuf_tensor("%s%d" % (name, i), shape, dtype) for i in range(n)]
        self.t = [b.__enter__() for b in self.bufs]
        self.slots = [None] * n
        self.s = s
        self.name = name
        self.n = n
        self.i = 0

    def next(self):
        i = self.i % self.n
        self.i += 1
        if self.slots[i] is None:
            self.slots[i] = self.s.slot()
        return self.t[i], self.slots[i], (self.name, i)

    def close(self):
        for sl in self.slots:
            if sl is not None:
                self.s.put_slot(sl)
        for b in reversed(self.bufs):
            b.__exit__(None, None, None)


def build_program(cfg, upto=None):
    c = cfg
    nc = bass.Bass("TRN2", target_bir_lowering=False)
    s = Sched(nc)
    L, D, KD, NT, LT, CT, TS = c.DEPTH, c.D, c.KD, c.NT, c.LT, c.CT, c.TS
    H, QC, KVC, CC, NK, NKC = c.H, c.QC, c.KVC, c.CC, c.NK, c.NKC

    def din(name, shape, dt=F32):
        return nc.dram_tensor(name, list(shape), dt, kind="ExternalInput").ap()

    def dscr(name, shape, dt):
        return nc.dram_tensor(name, list(shape), dt).ap()

    xT_in = din("xT", [KD, 128, NT])
    vecs_in = din("vecs", [128, c.NV])
    rope_in = din("rope", [64, 2, NT])
    dft_c = din("dft_c", [c.SEQ, LT], BF16)
    dft_s = din("dft_s", [c.SEQ, LT], BF16)
    dftx_c = din("dftx_c", [c.CTX, CT], BF16)
    dftx_s = din("dftx_s", [c.CTX, CT], BF16)
    dft_ch = din("dft_ch", [128, 2, 128], BF16)
    w_ada = din("w_ada", [L, D, c.MW])
    w_in = din("w_in", [L, D, c.N_IN])
    w_uq = din("w_uq", [L, c.QR, H * c.QKH])
    w_ukv = din("w_ukv", [L, c.KVR, H * (c.NOPE + c.VH)])
    w_f_out = din("w_f_out", [L, c.FWID, D])
    w_mla_out = din("w_mla_out", [L, H * c.VH, D])
    w_conv_out = din("w_conv_out", [L, c.CW, D])
    w_out = din("w_out", [L, D, D])
    w_ffn_gate = din("w_ffn_gate", [L, D, c.DFF])
    w_ffn_up = din("w_ffn_up", [L, D, c.DFF])
    w_ffn_down = din("w_ffn_down", [L, c.DFF, D])
    out_T = nc.dram_tensor("outT", [KD, 128, LT], F32, kind="ExternalOutput").ap()

    xs = dscr("xs", [KD, 128, NT], F32)
    h_s = dscr("h_s", [KD, 128, NT], BF16)
    q_s = dscr("q_s", [H, 192, NT], BF16)
    m_s = dscr("m_s", [KD, 128, NT], BF16)
    exa_t = [dscr("exa%d" % tc, [128 if tc < c.LCH else CT, c.FWID], BF16) for tc in range(c.NTC)]
    ga_t = [dscr("ga%d" % tc, [R * (128 if tc < c.LCH else CT), c.FWID], BF16) for tc in range(c.NTC)]
    NXB = 2 * KVC + 2
    exb_t = [dscr("exb%d" % i, [64, NT], BF16) for i in range(NXB)]
    gb_t = [dscr("gb%d" % i, [R * 64, NT], BF16) for i in range(NXB)]
    exc = dscr("exc", [128, 4 * CC], F32)
    exm = dscr("exm", [128, c.MJ * 2], F32)
    gm = dscr("gm", [R * 128, c.MJ * 2], F32)
    gc = dscr("gc", [R * 128, 4 * CC], F32)
    groups = [[0, 1, 2, 3], [4, 5, 6, 7]]

    tiles = [(i * TS, TS, False) for i in range(c.NLT)] + [(LT, CT, True)]
    kchunks = [(i * 128, 128) for i in range(c.SEQ // 128)] + [(c.SEQ + i * c.CKS, c.CKS) for i in range(c.NCK)]
    ctx_kchunks = list(range(c.SEQ // 128, NKC))

    uid = [0]

    def T(name, shape, dt):
        uid[0] += 1
        cm = nc.sbuf_tensor("%s_u%d" % (name, uid[0]), list(shape), dt)
        return cm, cm.__enter__()

    class Phase:
        def __init__(self):
            self.cms = []
            self.rings = []

        def t(self, name, shape, dt):
            cm, t = T(name, shape, dt)
            self.cms.append(cm)
            return t

        def ring(self, name, n, shape, dt):
            uid[0] += 1
            r = Ring(s, nc, "%s_u%d_" % (name, uid[0]), n, shape, dt)
            self.cms.append(r)
            return r

        def close(self):
            s.barrier()
            for cm in reversed(self.cms):
                if isinstance(cm, Ring):
                    cm.close()
                else:
                    cm.__exit__(None, None, None)

    G = Phase()
    vecs = G.t("vecs", [128, c.NV], F32)
    ones = G.t("ones", [128, 128], BF16)
    rope = G.t("ropet", [64, 2, NT], F32)
    dch = G.t("dch", [128, 2, 128], BF16)
    modt = G.t("modt", [128, 6, KD, 2], F32)
    A1 = G.t("A1", [128, KD, 2], F32)
    A2 = G.t("A2", [128, KD, 2], F32)
    gsc = G.t("gsc", [128, 8], F32)
    s2 = G.t("s2", [128, KD, 2], BF16)
    gmt = G.t("gmt", [128, R * c.MJ * 2], F32)
    mst = G.t("mst", [128, c.MJ * 2], F32)
    epsb = G.t("epsb", [128, 1], F32)
    wring = G.ring("wr", 3, [128, KD, 512], BF16)
    ps_cm = [nc.psum_tensor("ps%d" % i, [128, 512], F32) for i in range(8)]
    ps = [p.__enter__() for p in ps_cm]
    psi = [0]

    reserved = set()

    def nb():
        while True:
            b = psi[0] % 8
            psi[0] += 1
            if b not in reserved:
                return b

    def reserve():
        b = nb()
        reserved.add(b)
        return b

    def release(*bs):
        for b in bs:
            reserved.discard(b)

    ld0 = s.slot("ld0")
    st_slots = [s.slot("st%d" % i) for i in range(4)]
    sti = [0]

    def stslot():
        sl = st_slots[sti[0] % 4]
        sti[0] += 1
        return sl

    ld_slots = [s.slot("ldx%d" % i) for i in range(4)]
    ldi = [0]

    def ldslot():
        sl = ld_slots[ldi[0] % 4]
        ldi[0] += 1
        return sl

    V = c.voff

    def vcol(name, i=0, n=1, rows=128):
        o = V[name] + i
        return vecs[0:rows, o:o + n]

    s.dma("sp", ld0, [], ["vecs", "rope", "dch"], [
        lambda q: q.dma_start(out=vecs[:], in_=vecs_in),
        lambda q: q.dma_start(out=rope[:], in_=rope_in),
        lambda q: q.dma_start(out=dch[:], in_=dft_ch),
    ])
    s.op("dve", [], ["ones"], lambda e: e.memset(ones[:], 1.0))
    s.op("dve", [], ["epsb"], lambda e: e.memset(epsb[:], c.EPS))
    s.dma("sp", stslot(), [], [("xs", j, ti) for j in range(KD) for ti in range(len(tiles))],
          [lambda q: q.dma_start(out=xs, in_=xT_in)])
    s.op("act", ["vecs"], ["s2"], lambda e: e.activation(
        out=s2[:].rearrange("p k t -> p (k t)"), in_=vcol("cT", 0, KD * 2), func=AF.Silu))

    def wload(wap, l, k0, nk, c0, ncol, dst, slot, key, extra_reads=()):
        fns = []
        step = 8
        for a in range(0, nk, step):
            b = min(nk, a + step)
            src = wap[l, (k0 + a) * 128:(k0 + b) * 128, c0:c0 + ncol].rearrange("(k p) c -> p k c", p=128)
            fns.append(lambda q, src=src, a=a, b=b: q.dma_start(out=dst[:, a:b, 0:ncol], in_=src))
        return s.dma("pool", slot, list(extra_reads), [key], fns)

    def mm_group(bank, reads, pairs, outap=None, writes=None):
        o = outap if outap is not None else ps[bank][:]
        n = len(pairs)

        def fn(e):
            ins = None
            for i, (a, b) in enumerate(pairs):
                ins = e.matmul(o, a, b, start=(i == 0), stop=(i == n - 1))
            return ins
        return s.op("pe", reads, writes if writes is not None else [("ps", bank)], fn)

    def rms_rstd(bank, sz, dim, dst, dst_key, rt, rt_key, rows=128):
        s.op("act", [("ps", bank), "epsb"], [rt_key], lambda e: e.activation(
            out=rt[0:rows, 0:sz], in_=ps[bank][0:rows, 0:sz], func=AF.Sqrt, scale=1.0 / dim, bias=epsb[0:rows, 0:1]))
        s.op("dve", [rt_key], [dst_key], lambda e: e.reciprocal(out=dst, in_=rt[0:rows, 0:sz]))

    for l in range(L):
        last = (l == L - 1)
        act_tiles = [t for t in tiles if not (last and t[2])]

        mb = reserve()
        jj = 0
        for c0 in range(0, c.MW, 512):
            ncol = min(512, c.MW - c0)
            wt, wslot, wkey = wring.next()
            wload(w_ada, l, 0, KD, c0, ncol, wt, wslot, wkey)
            for j4 in range(ncol // 128):
                mm_group(mb, [wkey, "s2"], [(wt[:, k, j4 * 128:(j4 + 1) * 128], s2[:, k, :]) for k in range(KD)],
                         outap=ps[mb][:, jj * 2:(jj + 1) * 2])
                jj += 1
        s.op("dve", [("ps", mb)], ["mst"], lambda e: e.tensor_copy(out=mst[:], in_=ps[mb][:, 0:c.MJ * 2]))
        release(mb)
        s.dma("sp", stslot(), ["mst"], ["exm"], [lambda q: q.dma_start(out=exm, in_=mst[:])])
        s.collective(["exm"], ["gm"], lambda g: g.collective_compute(
            "AllGather", ALU.bypass, replica_groups=groups, ins=[exm.opt()], outs=[gm.opt()]))
        s.dma("sp", ldslot(), ["gm"], ["gmt"], [lambda q: q.dma_start(
            out=gmt[:].rearrange("p (r x) -> p r x", r=R), in_=gm.rearrange("(r p) x -> p r x", p=128))])
        s.op("dve", ["gmt", "vecs"], ["modt"], lambda e: e.tensor_tensor(
            out=modt[:].rearrange("p m k t -> p (m k t)"), in0=gmt[:], in1=vcol(("bada", l), 0, 6 * KD * 2), op=ALU.add))
        for (Ax, nm, mi) in ((A1, "nmix", 1), (A2, "nffn", 4)):
            for t2 in range(2):
                s.op("dve", ["modt", "vecs"], [("A", nm, t2)], lambda e, Ax=Ax, nm=nm, mi=mi, t2=t2: e.scalar_tensor_tensor(
                    out=Ax[:, :, t2], in0=modt[:, mi, :, t2], scalar=1.0, in1=vcol((nm, l), 0, KD),
                    op0=ALU.add, op1=ALU.mult))
        sc = float(c.QKH) ** -0.5
        for i, nm in enumerate(("gq_n", "gq_r", "gq_s")):
            s.op("dve", ["vecs"], [("gsc", i)], lambda e, i=i, nm=nm: e.tensor_scalar(
                out=gsc[:, i:i + 1], in0=vcol((nm, l)), scalar1=sc, scalar2=None, op0=ALU.mult))
        if upto == "mod" and l == 0:
            break

        def norm_phase(P, Ax, mi_shift, hT, use_tiles, store_h):
            KH = (KD + 1) // 2
            xr = P.ring("xr", 2, [128, KH, TS], F32)
            sqr = P.ring("sqr", 1, [128, KD, TS], BF16)
            rt = P.ring("rt", 2, [128, TS], F32)
            rr = P.ring("rr", 2, [128, TS], F32)
            tmp = P.ring("ntmp", 3, [128, TS], F32)
            for ti, (t0, sz, isc) in enumerate(tiles):
                if (t0, sz, isc) not in use_tiles:
                    continue
                m = 1 if isc else 0
                xch = {}
                for hf in range(2):
                    k0, k1 = hf * KH, min(KD, (hf + 1) * KH)
                    if k0 >= k1:
                        continue
                    xt, xslot, xkey = xr.next()
                    s.dma("sp", xslot, [("xs", j, ti) for j in range(k0, k1)], [xkey], [
                        lambda q, xt=xt, t0=t0, sz=sz, k0=k0, k1=k1: q.dma_start(
                            out=xt[:, 0:k1 - k0, 0:sz], in_=xs[k0:k1, :, t0:t0 + sz].rearrange("k p t -> p k t"))])
                    for k in range(k0, k1):
                        xch[k] = (xt, k - k0, xkey)
                sq, _, sqkey = sqr.next()
                for k in range(KD):
                    xt, kk, xkey = xch[k]
                    s.op("act", [xkey], [(sqkey, k)], lambda e, k=k, kk=kk, sq=sq, xt=xt, sz=sz: e.activation(
                        out=sq[:, k, 0:sz], in_=xt[:, kk, 0:sz], func=AF.Square))
                b = nb()
                mm_group(b, [(sqkey, k) for k in range(KD)] + ["ones"],
                         [(ones[:], sq[:, k, 0:sz]) for k in range(KD)], outap=ps[b][:, 0:sz])
                rtt, _, rtkey = rt.next()
                rrt, _, rrkey = rr.next()
                rms_rstd(b, sz, D, rrt[:, 0:sz], rrkey, rtt, rtkey)
                for k in range(KD):
                    xt, kk, xkey = xch[k]
                    tt, _, tkey = tmp.next()
                    s.op("dve", [xkey, rrkey, ("A", "x", m)], [tkey], lambda e, k=k, kk=kk, tt=tt, xt=xt, rrt=rrt, sz=sz, m=m: e.scalar_tensor_tensor(
                        out=tt[:, 0:sz], in0=xt[:, kk, 0:sz], scalar=Ax[:, k, m:m + 1], in1=rrt[:, 0:sz],
                        op0=ALU.mult, op1=ALU.mult))
                    s.op("act", [tkey, "modt"], [("hT", ti, k)], lambda e, k=k, tt=tt, sz=sz, t0=t0, m=m: e.activation(
                        out=hT[:, k, t0:t0 + sz], in_=tt[:, 0:sz], func=AF.Identity,
                        bias=modt[:, mi_shift, k, m:m + 1], scale=1.0))
                if store_h:
                    s.dma("sp", stslot(), [("hT", ti, k) for k in range(KD)], [("h_s", ti)], [
                        lambda q, t0=t0, sz=sz: q.dma_start(
                            out=h_s[:, :, t0:t0 + sz].rearrange("k p t -> p k t"), in_=hT[:, :, t0:t0 + sz])])

        for m in range(2):
            s.res[("A", "x", m)] = s.res.get(("A", "nmix", m))

        PXc = Phase()
        convT = PXc.t("convT", [128, CC, NT], BF16)
        PA = Phase()
        hT = PA.t("hT", [128, KD, NT], BF16)
        stg = PA.ring("stg", 2, [128, 512], BF16)
        PN = Phase()
        norm_phase(PN, A1, 0, hT, tiles, True)
        PN.close()
        if upto == "norm" and l == 0:
            PA.close()
            PXc.close()
            break

        def proj_block(col0, ncol, handlers, use_tiles):
            wt, wslot, wkey = wring.next()
            wload(w_in, l, 0, KD, col0, ncol, wt, wslot, wkey)
            for idx, (co, M, pb, evac) in enumerate(handlers):
                for ti, (t0, sz, isc) in enumerate(tiles):
                    if (t0, sz, isc) not in use_tiles:
                        continue
                    b = nb()
                    mm_group(b, [wkey] + [("hT", ti, k) for k in range(KD)],
                             [(wt[:, k, co:co + M], hT[:, k, t0:t0 + sz]) for k in range(KD)],
                             outap=ps[b][pb:pb + M, 0:sz])
                    evac(b, ti, t0, sz, M, idx, pb)
            return wkey

        def ev_copy(dst_fn, key_fn):
            def evac(b, ti, t0, sz, M, idx, pb):
                dst = dst_fn(idx, t0, sz, M)
                if (idx + ti) % 2:
                    s.op("act", [("ps", b)], [key_fn(idx, ti)], lambda e: e.activation(out=dst, in_=ps[b][pb:pb + M, 0:sz], func=AF.Copy))
                else:
                    s.op("dve", [("ps", b)], [key_fn(idx, ti)], lambda e: e.tensor_copy(out=dst, in_=ps[b][pb:pb + M, 0:sz]))
            return evac

        def feat_norm(P, src, nch, dim, gname, dst, ti, t0, sz, srckey, dstkey):
            sq, _, sqkey = P.sqb.next()
            for i in range(nch):
                s.op("act", [(srckey, i, ti)], [(sqkey, i)], lambda e, i=i, sq=sq: e.activation(
                    out=sq[:, i, 0:sz], in_=src[:, i, t0:t0 + sz], func=AF.Square))
            b = nb()
            mm_group(b, [(sqkey, i) for i in range(nch)] + ["ones"],
                     [(ones[:], sq[:, i, 0:sz]) for i in range(nch)], outap=ps[b][:, 0:sz])
            rtt, _, rtkey = P.rt.next()
            rrt, _, rrkey = P.rr.next()
            rms_rstd(b, sz, dim, rrt[:, 0:sz], rrkey, rtt, rtkey)
            for i in range(nch):
                s.op("dve", [(srckey, i, ti), rrkey, "vecs"], [(dstkey, i, ti)], lambda e, i=i, rrt=rrt: e.scalar_tensor_tensor(
                    out=dst[:, i, t0:t0 + sz], in0=src[:, i, t0:t0 + sz], scalar=vcol((gname, l), i), in1=rrt[:, 0:sz],
                    op0=ALU.mult, op1=ALU.mult))

        nti = len(tiles)
        P1 = Phase()
        ckvf = P1.t("ckvf", [128, KVC, NT], F32)
        krf = P1.t("krf", [64, 2, NT], F32)
        ckvn = P1.t("ckvn", [128, KVC, NT], BF16)
        krb = P1.t("krb", [64, 2, NT], BF16)
        P1.sqb = P1.ring("sqb", 2, [128, max(QC, KVC), TS], BF16)
        P1.rt = P1.ring("rt", 2, [128, TS], F32)
        P1.rr = P1.ring("rr", 2, [128, TS], F32)
        tmp64 = P1.ring("t64", 4, [64, TS], F32)
        hs = [(i * 128, 128, 0, ev_copy(lambda i_, t0, sz, M: ckvf[:, i_, t0:t0 + sz], lambda i_, ti: ("ckvf", i_, ti)))
              for i in range(KVC)]
        hs.append((c.KVR, 64, 0, ev_copy(lambda i_, t0, sz, M: krf[0:64, 0, t0:t0 + sz], lambda i_, ti: ("krf", 0, ti))))
        hs.append((c.KVR + 32, 32, 0, ev_copy(lambda i_, t0, sz, M: krf[0:32, 1, t0:t0 + sz], lambda i_, ti: ("krf", 1, ti))))
        hs.append((c.KVR, 32, 32, ev_copy(lambda i_, t0, sz, M: krf[32:64, 1, t0:t0 + sz], lambda i_, ti: ("krf", 2, ti))))
        proj_block(c.OFF_CKV, c.KVR + 64, hs, tiles)
        for ti, (t0, sz, isc) in enumerate(tiles):
            feat_norm(P1, ckvf, KVC, c.KVR, "kva", ckvn, ti, t0, sz, "ckvf", "ckvn")
        for ti, (t0, sz, isc) in enumerate(tiles):
            t1, _, k1 = tmp64.next()
            t2, _, k2 = tmp64.next()
            s.op("dve", [("krf", 0, ti), "vecs", "rope"], [k1], lambda e, t1=t1, t0=t0, sz=sz: e.scalar_tensor_tensor(
                out=t1[:, 0:sz], in0=krf[0:64, 0, t0:t0 + sz], scalar=vcol(("gk_r", l), rows=64), in1=rope[:, 0, t0:t0 + sz],
                op0=ALU.mult, op1=ALU.mult))
            s.op("dve", [("krf", 1, ti), ("krf", 2, ti), "vecs", "rope"], [k2], lambda e, t2=t2, t0=t0, sz=sz: e.scalar_tensor_tensor(
                out=t2[:, 0:sz], in0=krf[0:64, 1, t0:t0 + sz], scalar=vcol(("gk_s", l), rows=64), in1=rope[:, 1, t0:t0 + sz],
                op0=ALU.mult, op1=ALU.mult))
            s.op("dve", [k1, k2], [("krb", 0, ti)], lambda e, t1=t1, t2=t2, t0=t0, sz=sz: e.tensor_tensor(
                out=krb[:, 0, t0:t0 + sz], in0=t1[:, 0:sz], in1=t2[:, 0:sz], op=ALU.add))
            s.op("act", [("krf", 0, ti)], [("krb", 1, ti)], lambda e, t0=t0, sz=sz: e.activation(
                out=krb[:, 1, t0:t0 + sz], in_=krf[0:64, 0, t0:t0 + sz], func=AF.Square))
        for xi in range(NXB):
            if xi < 2 * KVC:
                i, hf = xi // 2, xi % 2
                rd = [("ckvn", i, ti) for ti in range(nti)]
                src = ckvn[hf * 64:(hf + 1) * 64, i, :]
            else:
                j = xi - 2 * KVC
                rd = [("krb", j, ti) for ti in range(nti)]
                src = krb[:, j, :]
            s.dma("sp", stslot(), rd, [("exb", xi)], [lambda q, xi=xi, src=src: q.dma_start(out=exb_t[xi], in_=src)])
            s.collective([("exb", xi)], [("gb", xi)], lambda g, xi=xi: g.collective_compute(
                "AllGather", ALU.bypass, replica_groups=groups, ins=[exb_t[xi].opt()], outs=[gb_t[xi].opt()]))
        P1.close()

        wt, wslot, wkey = wring.next()
        wload(w_in, l, 0, KD, c.OFF_F, c.FWID, wt, wslot, wkey)
        for tc in range(c.NTC):
            isc = tc == c.LCH
            if isc and last:
                continue
            t0 = tc * 128
            rows = CT if isc else 128
            ti = len(tiles) - 1 if isc else t0 // TS
            b = nb()
            mm_group(b, [wkey] + [("hT", ti, k) for k in range(KD)],
                     [(hT[:, k, t0:t0 + rows], wt[:, k, 0:c.FWID]) for k in range(KD)],
                     outap=ps[b][0:rows, 0:c.FWID])
            st, _, skey = stg.next()
            if tc % 2:
                s.op("act", [("ps", b)], [skey], lambda e, st=st, b=b, rows=rows: e.activation(
                    out=st[0:rows, 0:c.FWID], in_=ps[b][0:rows, 0:c.FWID], func=AF.Copy))
            else:
                s.op("dve", [("ps", b)], [skey], lambda e, st=st, b=b, rows=rows: e.tensor_copy(
                    out=st[0:rows, 0:c.FWID], in_=ps[b][0:rows, 0:c.FWID]))
            s.dma("sp", stslot(), [skey], [("exa", tc)], [
                lambda q, st=st, tc=tc, rows=rows: q.dma_start(out=exa_t[tc], in_=st[0:rows, 0:c.FWID])])
            s.collective([("exa", tc)], [("ga", tc)], lambda g, tc=tc: g.collective_compute(
                "AllGather", ALU.bypass, replica_groups=groups, ins=[exa_t[tc].opt()], outs=[ga_t[tc].opt()]))

        PQ = Phase()
        cqf = PQ.t("cqf", [128, QC, NT], F32)
        cqn = PQ.t("cqn", [128, QC, NT], BF16)
        PQ.sqb = PQ.ring("sqb", 2, [128, max(QC, KVC), TS], BF16)
        PQ.rt = PQ.ring("rt", 2, [128, TS], F32)
        PQ.rr = PQ.ring("rr", 2, [128, TS], F32)
        tmp64 = PQ.ring("t64", 4, [64, TS], F32)
        wuq = PQ.t("wuq", [128, QC, H * c.QKH], BF16)
        s.dma("pool", s.slot("wuq%d" % l), [], ["wuq"], [
            lambda q: q.dma_start(out=wuq[:], in_=w_uq[l].rearrange("(k p) c -> p k c", p=128))])
        qnf = PQ.ring("qnf", 2, [128, TS], F32)
        sqA = PQ.ring("sqA", 2, [128, TS], BF16)
        sqB = PQ.ring("sqB", 2, [64, TS], BF16)
        rqr = PQ.ring("rq", 2, [128, TS], F32)
        qnb = PQ.ring("qnb", 2, [128, TS], BF16)
        qrb = PQ.ring("qrb", 2, [64, TS], BF16)
        rtq = PQ.ring("rtq", 2, [128, TS], F32)
        proj_block(c.OFF_CQ, c.QR, [(i * 128, 128, 0, ev_copy(lambda i_, t0, sz, M: cqf[:, i_, t0:t0 + sz], lambda i_, ti: ("cqf", i_, ti)))
                                    for i in range(QC)], act_tiles)
        for ti, (t0, sz, isc) in enumerate(tiles):
            if (t0, sz, isc) not in act_tiles:
                continue
            feat_norm(PQ, cqf, QC, c.QR, "qa", cqn, ti, t0, sz, "cqf", "cqn")
            cq_reads = [("cqn", i, ti) for i in range(QC)]
            for h in range(H):
                hb = h * c.QKH
                bA = nb()
                mm_group(bA, ["wuq"] + cq_reads,
                         [(wuq[:, i, hb:hb + 128], cqn[:, i, t0:t0 + sz]) for i in range(QC)], outap=ps[bA][:, 0:sz])
                bB = nb()
                mm_group(bB, ["wuq"] + cq_reads,
                         [(wuq[:, i, hb + 128:hb + 192], cqn[:, i, t0:t0 + sz]) for i in range(QC)], outap=ps[bB][0:64, 0:sz])
                bC = nb()

                def fnsw(e, hb=hb, bC=bC, t0=t0, sz=sz):
                    ins = None
                    for (pb, co) in ((0, hb + 160), (32, hb + 128)):
                        for i in range(QC):
                            ins = e.matmul(ps[bC][pb:pb + 32, 0:sz], wuq[:, i, co:co + 32], cqn[:, i, t0:t0 + sz],
                                           start=(i == 0), stop=(i == QC - 1))
                    return ins
                s.op("pe", ["wuq"] + cq_reads, [("ps", bC)], fnsw)
                sa, _, sak = sqA.next()
                sb_, _, sbk = sqB.next()
                s.op("act", [("ps", bA)], [sak], lambda e, sa=sa, bA=bA, sz=sz: e.activation(out=sa[:, 0:sz], in_=ps[bA][:, 0:sz], func=AF.Square))
                s.op("act", [("ps", bB)], [sbk], lambda e, sb_=sb_, bB=bB, sz=sz: e.activation(out=sb_[:, 0:sz], in_=ps[bB][0:64, 0:sz], func=AF.Square))
                bS = nb()
                mm_group(bS, [sak, sbk, "ones"], [(ones[:], sa[:, 0:sz]), (ones[0:64, :], sb_[:, 0:sz])], outap=ps[bS][:, 0:sz])
                rtt, _, rtk = rtq.next()
                rq, _, rqk = rqr.next()
                rms_rstd(bS, sz, c.QKH, rq[:, 0:sz], rqk, rtt, rtk)
                qn, _, qnk = qnf.next()
                s.op("dve", [("ps", bA), ("gsc", 0)], [qnk], lambda e, qn=qn, bA=bA, sz=sz: e.tensor_scalar(
                    out=qn[:, 0:sz], in0=ps[bA][:, 0:sz], scalar1=gsc[:, 0:1], scalar2=None, op0=ALU.mult))
                qb, _, qbk = qnb.next()
                s.op("dve", [qnk, rqk], [qbk], lambda e, qb=qb, qn=qn, rq=rq, sz=sz: e.tensor_tensor(
                    out=qb[:, 0:sz], in0=qn[:, 0:sz], in1=rq[:, 0:sz], op=ALU.mult))
                s.dma("sp", stslot(), [qbk], [("q_s", h, ti, 0)], [
                    lambda q, qb=qb, h=h, t0=t0, sz=sz: q.dma_start(out=q_s[h, 0:128, t0:t0 + sz], in_=qb[:, 0:sz])])
                t1, _, k1 = tmp64.next()
                t2, _, k2 = tmp64.next()
                s.op("dve", [("ps", bB), ("gsc", 1), "rope"], [k1], lambda e, t1=t1, bB=bB, t0=t0, sz=sz: e.scalar_tensor_tensor(
                    out=t1[:, 0:sz], in0=ps[bB][0:64, 0:sz], scalar=gsc[0:64, 1:2], in1=rope[:, 0, t0:t0 + sz],
                    op0=ALU.mult, op1=ALU.mult))
                s.op("dve", [("ps", bC), ("gsc", 2), "rope"], [k2], lambda e, t2=t2, bC=bC, t0=t0, sz=sz: e.scalar_tensor_tensor(
                    out=t2[:, 0:sz], in0=ps[bC][0:64, 0:sz], scalar=gsc[0:64, 2:3], in1=rope[:, 1, t0:t0 + sz],
                    op0=ALU.mult, op1=ALU.mult))
                s.op("dve", [k1, k2], [k1], lambda e, t1=t1, t2=t2, sz=sz: e.tensor_tensor(
                    out=t1[:, 0:sz], in0=t1[:, 0:sz], in1=t2[:, 0:sz], op=ALU.add))
                qb2, _, qbk2 = qrb.next()
                s.op("dve", [k1, rqk], [qbk2], lambda e, t1=t1, qb2=qb2, rq=rq, sz=sz: e.tensor_tensor(
                    out=qb2[:, 0:sz], in0=t1[:, 0:sz], in1=rq[0:64, 0:sz], op=ALU.mult))
                s.dma("sp", stslot(), [qbk2], [("q_s", h, ti, 1)], [
                    lambda q, qb2=qb2, h=h, t0=t0, sz=sz: q.dma_start(out=q_s[h, 128:192, t0:t0 + sz], in_=qb2[:, 0:sz])])
        PQ.close()

        PC = Phase()
        uT = PC.t("uT", [128, CC, NT + 4], BF16)
        pbT = PC.t("pbT", [128, CC, NT], BF16)
        cxT = PC.t("cxT", [128, CC, NT], BF16)
        exct = PC.t("exct", [128, CC, 4], F32)
        gct = PC.t("gct", [128, R, CC, 4], F32)
        hal = PC.t("hal", [128, CC, 4], F32)
        ctmp = PC.ring("ctmp", 2, [128, LT], F32)

        def ucol(t0):
            return t0 + 1 if t0 < LT else t0 + 3

        proj_block(c.OFF_CX, c.CW, [(i * 128, 128, 0, ev_copy(lambda i_, t0, sz, M: cxT[:, i_, t0:t0 + sz], lambda i_, ti: ("cxT", i_, ti)))
                                    for i in range(CC)], act_tiles)

        def ev_u(b, ti, t0, sz, M, idx, pb):
            uc = ucol(t0)
            s.op("dve", [("ps", b), ("cxT", idx, ti)], [("uT", idx, ti)], lambda e: e.tensor_tensor(
                out=uT[:, idx, uc:uc + sz], in0=ps[b][:, 0:sz], in1=cxT[:, idx, t0:t0 + sz], op=ALU.mult))
        proj_block(c.OFF_CC, c.CW, [(i * 128, 128, 0, ev_u) for i in range(CC)], act_tiles)
        proj_block(c.OFF_CB, c.CW, [(i * 128, 128, 0, ev_copy(lambda i_, t0, sz, M: pbT[:, i_, t0:t0 + sz], lambda i_, ti: ("pbT", i_, ti)))
                                    for i in range(CC)], act_tiles)
        ti_last_lat = c.NLT - 1
        ti_ctx = len(tiles) - 1
        srcs = [(1, 0), (LT, ti_last_lat)]
        if not last:
            srcs += [(LT + 3, ti_ctx), (LT + 3 + CT - 1, ti_ctx)]
        else:
            s.op("dve", [], [("exct", 2), ("exct", 3)], lambda e: e.memset(exct[:, :, 2:4], 0.0))
        for kind, (col, ti) in enumerate(srcs):
            s.op("dve", [("uT", i, ti) for i in range(CC)], [("exct", kind)], lambda e, kind=kind, col=col: e.tensor_copy(
                out=exct[:, :, kind], in_=uT[:, :, col]))
        s.dma("sp", stslot(), [("exct", k) for k in range(4)], ["exc"], [
            lambda q: q.dma_start(out=exc, in_=exct[:].rearrange("p c k -> p (c k)"))])
        s.collective(["exc"], ["gc"], lambda g: g.collective_compute(
            "AllGather", ALU.bypass, replica_groups=groups, ins=[exc.opt()], outs=[gc.opt()]))
        s.dma("sp", ldslot(), ["gc"], ["gct"], [
            lambda q: q.dma_start(out=gct[:].rearrange("p r c k -> p r (c k)"), in_=gc.rearrange("(r p) x -> p r x", p=128))])
        streams = [(0, LT, 1, 1, 0)] if last else [(0, LT, 1, 1, 0), (LT, CT, 3, 3, 2)]
        for (t0, n, uo, kl, kr_) in streams:
            for side, kind, selo in ((0, kl, 0), (1, kr_, 4)):
                hk = ("hal", t0, side)
                dst = hal[:, :, (0 if t0 == 0 else 2) + side]
                s.op("dve", ["gct", "vecs"], [hk], lambda e, dst=dst, kind=kind, selo=selo: e.tensor_scalar(
                    out=dst, in0=gct[:, 0, :, kind], scalar1=vcol("sel", selo), scalar2=None, op0=ALU.mult))
                for r in range(1, R):
                    s.op("dve", ["gct", "vecs", hk], [hk], lambda e, dst=dst, kind=kind, selo=selo, r=r: e.scalar_tensor_tensor(
                        out=dst, in0=gct[:, r, :, kind], scalar=vcol("sel", selo + r), in1=dst, op0=ALU.mult, op1=ALU.add))
                ucolumn = (t0 + uo - 1) if side == 0 else (t0 + uo + n)
                s.op("dve", [hk], [("uTh", t0, side)], lambda e, dst=dst, ucolumn=ucolumn: e.tensor_copy(
                    out=uT[:, :, ucolumn], in_=dst))
            tis = [ti for ti, tl in enumerate(tiles) if tl[0] >= t0 and tl[0] < t0 + n]
            for i in range(CC):
                ureads = [("uT", i, ti) for ti in tis] + [("uTh", t0, 0), ("uTh", t0, 1), "vecs"]
                ct, _, ck = ctmp.next()
                base = t0 + uo
                s.op("dve", ureads, [ck], lambda e, ct=ct, i=i, base=base, n=n: e.tensor_scalar(
                    out=ct[:, 0:n], in0=uT[:, i, base - 1:base - 1 + n], scalar1=vcol(("convw", l), 0 * CC + i), scalar2=None, op0=ALU.mult))
                s.op("dve", ureads + [ck], [ck], lambda e, ct=ct, i=i, base=base, n=n: e.scalar_tensor_tensor(
                    out=ct[:, 0:n], in0=uT[:, i, base:base + n], scalar=vcol(("convw", l), 1 * CC + i), in1=ct[:, 0:n], op0=ALU.mult, op1=ALU.add))
                s.op("dve", ureads + [ck], [ck], lambda e, ct=ct, i=i, base=base, n=n: e.scalar_tensor_tensor(
                    out=ct[:, 0:n], in0=uT[:, i, base + 1:base + 1 + n], scalar=vcol(("convw", l), 2 * CC + i), in1=ct[:, 0:n], op0=ALU.mult, op1=ALU.add))
                s.op("dve", [ck] + [("pbT", i, ti) for ti in tis], [("convT", i, t0)], lambda e, ct=ct, i=i, n=n, t0=t0: e.tensor_tensor(
                    out=convT[:, i, t0:t0 + n], in0=ct[:, 0:n], in1=pbT[:, i, t0:t0 + n], op=ALU.mult))
        PC.close()
        PA.close()
        if upto == "p1" and l == 0:
            PXc.close()
            break

        PA = Phase()
        fT = PA.t("fT", [128, c.FG, NT], BF16)
        PF_ = Phase()
        pf = PF_.t("pf", [128, NKC, c.FWID], BF16)
        fns = []
        for tc in range(c.LCH):
            fns.append(lambda q, tc=tc: q.dma_start(out=pf[:, tc * R:(tc + 1) * R, :], in_=ga_t[tc].rearrange("(r p) f -> p r f", p=128)))
        if not last:
            for r in range(R):
                kc = c.SEQ // 128 + (r * CT) // 128
                p0 = (r * CT) % 128
                fns.append(lambda q, r=r, kc=kc, p0=p0: q.dma_start(out=pf[p0:p0 + CT, kc, :], in_=ga_t[c.LCH][r * CT:(r + 1) * CT, :]))
        s.dma("sp", ldslot(), [("ga", tc) for tc in range(c.NTC) if not (last and tc == c.LCH)], ["pf"], fns)
        csr = PF_.ring("csr", 4, [128, 2, TS], BF16)
        zt = PF_.ring("zt", 2, [128, 4, TS], BF16)
        fstreams = [(tl, False) for tl in tiles if not tl[2]] + ([] if last else [(tiles[-1], True)])
        for gp in range(c.FG // 2):
            for ((t0, sz, isc), _) in fstreams:
                banks = [nb() for _ in range(4)]
                if not isc:
                    ksteps = []
                    for k in range(c.SEQ // 128):
                        p0 = (k % R) * LT + (k // R) * 128
                        ksteps.append((k, 128, dft_c[p0:p0 + 128, t0:t0 + sz], dft_s[p0:p0 + 128, t0:t0 + sz]))
                else:
                    ksteps = [(c.SEQ // 128 + k, c.CKS, dftx_c[k * c.CKS:(k + 1) * c.CKS, :], dftx_s[k * c.CKS:(k + 1) * c.CKS, :])
                              for k in range(c.NCK)]
                nks = len(ksteps)
                for si, (kc, ksz, srcc, srcs_) in enumerate(ksteps):
                    cs, cslot, ckey = csr.next()
                    s.dma("sp", cslot, [], [ckey], [
                        lambda q, cs=cs, srcc=srcc, ksz=ksz: q.dma_start(out=cs[0:ksz, 0, 0:sz], in_=srcc),
                        lambda q, cs=cs, srcs_=srcs_, ksz=ksz: q.dma_start(out=cs[0:ksz, 1, 0:sz], in_=srcs_)])

                    def fn(e, cs=cs, kc=kc, ksz=ksz, si=si):
                        ins = None
                        for gi in range(2):
                            g = gp * 2 + gi
                            for tr in range(2):
                                ins = e.matmul(ps[banks[gi * 2 + tr]][:, 0:sz], pf[0:ksz, kc, g * 128:(g + 1) * 128],
                                               cs[0:ksz, tr, 0:sz], start=(si == 0), stop=(si == nks - 1))
                        return ins
                    s.op("pe", ["pf", ckey], [("ps", b) for b in banks], fn)
                z, _, zkey = zt.next()
                for j4 in range(4):
                    if j4 % 2:
                        s.op("act", [("ps", banks[j4])], [(zkey, j4)], lambda e, z=z, j4=j4: e.activation(
                            out=z[:, j4, 0:sz], in_=ps[banks[j4]][:, 0:sz], func=AF.Copy))
                    else:
                        s.op("dve", [("ps", banks[j4])], [(zkey, j4)], lambda e, z=z, j4=j4: e.tensor_copy(
                            out=z[:, j4, 0:sz], in_=ps[banks[j4]][:, 0:sz]))
                for gi in range(2):
                    g = gp * 2 + gi
                    b = nb()
                    mm_group(b, ["dch", (zkey, gi * 2), (zkey, gi * 2 + 1)],
                             [(dch[:, 0, :], z[:, gi * 2, 0:sz]), (dch[:, 1, :], z[:, gi * 2 + 1, 0:sz])], outap=ps[b][:, 0:sz])
                    s.op("act" if gi else "dve", [("ps", b)], [("fT", g, t0)],
                         (lambda e, g=g, b=b: e.activation(out=fT[:, g, t0:t0 + sz], in_=ps[b][:, 0:sz], func=AF.Copy)) if gi else
                         (lambda e, g=g, b=b: e.tensor_copy(out=fT[:, g, t0:t0 + sz], in_=ps[b][:, 0:sz])))
        PF_.close()
        if upto == "fourier" and l == 0:
            PA.close()
            PXc.close()
            break

        attnT = PA.t("attnT", [128, H, NT], BF16)
        PT = Phase()
        ckv = PT.t("ckv", [128, KVC, NK], BF16)
        krk = PT.t("krk", [128, NK], BF16)
        wukv = PT.t("wukv", [128, KVC, H * (c.NOPE + c.VH)], BF16)
        s.dma("pool", s.slot("wukv%d" % l), [], ["wukv"], [
            lambda q: q.dma_start(out=wukv[:], in_=w_ukv[l].rearrange("(k p) c -> p k c", p=128))])
        fns = []
        for xi in range(NXB):
            if xi < 2 * KVC:
                i, hf = xi // 2, xi % 2
                dl = ckv[hf * 64:(hf + 1) * 64, i, 0:c.SEQ]
                dc = ckv[hf * 64:(hf + 1) * 64, i, c.SEQ:NK]
            else:
                j = xi - 2 * KVC
                dl = krk[j * 64:(j + 1) * 64, 0:c.SEQ]
                dc = krk[j * 64:(j + 1) * 64, c.SEQ:NK]
            fns.append(lambda q, xi=xi, dl=dl: q.dma_start(out=dl.rearrange("p (r t) -> p r t", r=R),
                                                         in_=gb_t[xi][:, 0:LT].rearrange("(r p) t -> p r t", p=64)))
            fns.append(lambda q, xi=xi, dc=dc: q.dma_start(out=dc.rearrange("p (r t) -> p r t", r=R),
                                                         in_=gb_t[xi][:, LT:NT].rearrange("(r p) t -> p r t", p=64)))
        s.dma("sp", ldslot(), [("gb", xi) for xi in range(NXB)], ["ckv", "krk"], fns)
        knr = PT.ring("kn", 2, [128, NK], BF16)
        vr = PT.ring("vv", 2, [128, NKC, c.VH], BF16)
        rkr = PT.ring("rk", 2, [128, NKC], F32)
        rkt = PT.ring("rkt", 2, [128, NKC], F32)
        sqk = PT.ring("sqk", 2, [128, 512], BF16)
        qnr = PT.ring("qn", 2, [128, NT], BF16)
        qrr = PT.ring("qr", 2, [64, NT], BF16)
        ptr = PT.ring("pt", 4, [128, TS], BF16)
        rsr = PT.ring("rs", 2, [128, TS], F32)
        KT = min(512, c.SEQ)
        ktiles = [(i * KT, KT) for i in range(c.SEQ // KT)] + [(c.SEQ, c.CTX)]
        for h in range(H):
            kn, _, knk = knr.next()
            vv, _, vk = vr.next()
            rk, _, rkk = rkr.next()
            c0k = h * (c.NOPE + c.VH)
            bss = reserve()
            for (k0, kn_sz) in ktiles:
                b = nb()
                mm_group(b, ["wukv", "ckv"], [(wukv[:, i, c0k:c0k + 128], ckv[:, i, k0:k0 + kn_sz]) for i in range(KVC)],
                         outap=ps[b][:, 0:kn_sz])
                s.op("act", [("ps", b), "vecs"], [(knk, k0)], lambda e, kn=kn, b=b, k0=k0, kn_sz=kn_sz: e.activation(
                    out=kn[:, k0:k0 + kn_sz], in_=ps[b][:, 0:kn_sz], func=AF.Identity, scale=vcol(("gk_n", l)), bias=0.0))
                sq, _, sqkey = sqk.next()
                s.op("act", [("ps", b)], [sqkey], lambda e, sq=sq, b=b, kn_sz=kn_sz: e.activation(
                    out=sq[:, 0:kn_sz], in_=ps[b][:, 0:kn_sz], func=AF.Square))
                sub = [(ci, cs_, csz) for ci, (cs_, csz) in enumerate(kchunks) if cs_ >= k0 and cs_ < k0 + kn_sz]

                def fn(e, sq=sq, sub=sub, k0=k0):
                    ins = None
                    for (ci, cs_, csz) in sub:
                        e.matmul(ps[bss][0:csz, ci:ci + 1], sq[:, cs_ - k0:cs_ - k0 + csz], ones[:, 0:1], start=True, stop=False)
                        ins = e.matmul(ps[bss][0:csz, ci:ci + 1], krk[64:128, cs_:cs_ + csz], ones[64:128, 0:1], start=False, stop=True)
                    return ins
                s.op("pe", [sqkey, "krk", "ones"], [("ps", bss)], fn)
            if c.CKS < 128:
                pass
            rt_, _, rtk_ = rkt.next()
            for (ci, (cs_, csz)) in enumerate(kchunks):
                pass
            full = [ci for ci, (cs_, csz) in enumerate(kchunks) if csz == 128]
            part = [ci for ci, (cs_, csz) in enumerate(kchunks) if csz < 128]
            nf = len(full)
            s.op("act", [("ps", bss), "epsb"], [(rtk_, 0)], lambda e, rt_=rt_: e.activation(
                out=rt_[:, 0:nf], in_=ps[bss][:, 0:nf], func=AF.Sqrt, scale=1.0 / c.QKH, bias=epsb[:, 0:1]))
            s.op("dve", [(rtk_, 0)], [(rkk, 0)], lambda e, rt_=rt_, rk=rk: e.reciprocal(out=rk[:, 0:nf], in_=rt_[:, 0:nf]))
            for ci in part:
                csz = kchunks[ci][1]
                s.op("act", [("ps", bss), "epsb"], [(rtk_, 1)], lambda e, rt_=rt_, ci=ci, csz=csz: e.activation(
                    out=rt_[0:csz, ci:ci + 1], in_=ps[bss][0:csz, ci:ci + 1], func=AF.Sqrt, scale=1.0 / c.QKH, bias=epsb[0:csz, 0:1]))
                s.op("dve", [(rtk_, 1)], [(rkk, 1)], lambda e, rt_=rt_, rk=rk, ci=ci, csz=csz: e.reciprocal(
                    out=rk[0:csz, ci:ci + 1], in_=rt_[0:csz, ci:ci + 1]))
            rk_reads = [(rkk, 0)] + ([(rkk, 1)] if part else [])
            release(bss)
            c0v = c0k + c.NOPE
            for g0 in range(0, NKC, 4):
                b = nb()
                grp = list(range(g0, min(NKC, g0 + 4)))

                def fn(e, grp=grp, b=b):
                    ins = None
                    for gi, ci in enumerate(grp):
                        cs_, csz = kchunks[ci]
                        for i in range(KVC):
                            ins = e.matmul(ps[b][0:csz, gi * 128:(gi + 1) * 128], ckv[:, i, cs_:cs_ + csz], wukv[:, i, c0v:c0v + c.VH],
                                           start=(i == 0), stop=(i == KVC - 1))
                    return ins
                s.op("pe", ["wukv", "ckv"], [("ps", b)], fn)
                allfull = all(kchunks[ci][1] == 128 for ci in grp)
                if allfull:
                    ng = len(grp)
                    s.op("dve" if (g0 // 4) % 2 else "act", [("ps", b)], [(vk, g0)],
                         (lambda e, vv=vv, b=b, g0=g0, ng=ng: e.tensor_copy(out=vv[:, g0:g0 + ng, :].rearrange("p a b -> p (a b)"), in_=ps[b][:, 0:ng * 128])) if (g0 // 4) % 2 else
                         (lambda e, vv=vv, b=b, g0=g0, ng=ng: e.activation(out=vv[:, g0:g0 + ng, :].rearrange("p a b -> p (a b)"), in_=ps[b][:, 0:ng * 128], func=AF.Copy)))
                else:
                    for gi, ci in enumerate(grp):
                        csz = kchunks[ci][1]
                        s.op("dve", [("ps", b)], [(vk, g0, gi)], lambda e, vv=vv, b=b, gi=gi, ci=ci, csz=csz: e.tensor_copy(
                            out=vv[0:csz, ci, :], in_=ps[b][0:csz, gi * 128:(gi + 1) * 128]))
            v_reads = []
            for g0 in range(0, NKC, 4):
                grp = list(range(g0, min(NKC, g0 + 4)))
                if all(kchunks[ci][1] == 128 for ci in grp):
                    v_reads.append((vk, g0))
                else:
                    v_reads += [(vk, g0, gi) for gi in range(len(grp))]
            k_reads = [(knk, k0) for (k0, _) in ktiles]
            qn, qslot, qnk = qnr.next()
            qr_, qslot2, qrk = qrr.next()
            qtis = [ti for ti, tl in enumerate(tiles) if tl in act_tiles]
            ncols = LT if last else NT
            s.dma("sp", qslot, [("q_s", h, ti, 0) for ti in qtis], [qnk], [
                lambda q, qn=qn, h=h: q.dma_start(out=qn[:, 0:ncols], in_=q_s[h, 0:128, 0:ncols])])
            s.dma("sp", qslot2, [("q_s", h, ti, 1) for ti in qtis], [qrk], [
                lambda q, qr_=qr_, h=h: q.dma_start(out=qr_[:, 0:ncols], in_=q_s[h, 128:192, 0:ncols])])
            for ti in qtis:
                t0, sz, isc = tiles[ti]
                kcs = ctx_kchunks if isc else list(range(NKC))
                bo = reserve()
                bsum = reserve()
                sbanks = {}
                pts = {}

                def emit_s(ci):
                    cs_, csz = kchunks[ci]
                    b = nb()
                    sbanks[ci] = b
                    mm_group(b, k_reads + ["krk", qnk, qrk],
                             [(kn[:, cs_:cs_ + csz], qn[:, t0:t0 + sz]), (krk[0:64, cs_:cs_ + csz], qr_[:, t0:t0 + sz])],
                             outap=ps[b][0:csz, 0:sz])

                def emit_exp(ci):
                    cs_, csz = kchunks[ci]
                    b = sbanks[ci]
                    pt, _, pk = ptr.next()
                    pts[ci] = (pt, pk)
                    s.op("act", [("ps", b)] + rk_reads, [pk], lambda e, pt=pt, b=b, ci=ci, csz=csz: e.activation(
                        out=pt[0:csz, 0:sz], in_=ps[b][0:csz, 0:sz], func=AF.Exp, scale=rk[0:csz, ci:ci + 1]))

                def emit_pv(idx, ci):
                    cs_, csz = kchunks[ci]
                    pt, pk = pts[ci]

                    def fn(e):
                        e.matmul(ps[bo][:, 0:sz], vv[0:csz, ci, :], pt[0:csz, 0:sz], start=(idx == 0), stop=(idx == len(kcs) - 1))
                        return e.matmul(ps[bsum][:, 0:sz], ones[0:csz, :], pt[0:csz, 0:sz], start=(idx == 0), stop=(idx == len(kcs) - 1))
                    s.op("pe", v_reads + [pk, "ones"], [("ps", bo), ("ps", bsum)], fn)

                LA = 3
                for i0 in range(min(LA, len(kcs))):
                    emit_s(kcs[i0])
                for idx, ci in enumerate(kcs):
                    emit_exp(ci)
                    if idx + LA < len(kcs):
                        emit_s(kcs[idx + LA])
                    emit_pv(idx, ci)
                rs, _, rsk = rsr.next()
                s.op("dve", [("ps", bsum)], [rsk], lambda e, rs=rs, bsum=bsum: e.reciprocal(out=rs[:, 0:sz], in_=ps[bsum][:, 0:sz]))
                s.op("dve", [("ps", bo), rsk], [("attnT", h, ti)], lambda e, rs=rs, bo=bo, h=h: e.tensor_tensor(
                    out=attnT[:, h, t0:t0 + sz], in0=ps[bo][:, 0:sz], in1=rs[:, 0:sz], op=ALU.mult))
                release(bo, bsum)
        PT.close()
        if upto == "attn" and l == 0:
            PA.close()
            PXc.close()
            break

        PM = Phase()
        hT = PM.t("hTm", [128, KD, NT], BF16)
        for ti, (t0, sz, isc) in enumerate(tiles):
            if (t0, sz, isc) not in act_tiles:
                continue
            s.dma("sp", ldslot(), [("h_s", ti)], [("hTm", ti)], [
                lambda q, t0=t0, sz=sz: q.dma_start(out=hT[:, :, t0:t0 + sz], in_=h_s[:, :, t0:t0 + sz].rearrange("k p t -> p k t"))])
        nbr = c.FG + H + CC
        wmr = PM.ring("wm", 2, [128, nbr, 128], BF16)
        gtr = PM.ring("gt", 3, [128, TS], F32)
        ttr = PM.ring("tt", 3, [128, TS], F32)
        mstg = PM.ring("mstg", 2, [128, TS], BF16)
        for j in range(KD):
            wg, wgslot, wgkey = wring.next()
            fns = []
            for br in range(3):
                for a in range(0, KD, 8):
                    bnd = min(KD, a + 8)
                    src = w_in[l, a * 128:bnd * 128, c.OFF_G + br * D + j * 128:c.OFF_G + br * D + (j + 1) * 128].rearrange("(k p) c -> p k c", p=128)
                    fns.append(lambda q, src=src, a=a, bnd=bnd, br=br, wg=wg: q.dma_start(out=wg[:, a:bnd, br * 128:(br + 1) * 128], in_=src))
            s.dma("pool", wgslot, [], [wgkey], fns)
            wm, wmslot, wmkey = wmr.next()
            s.dma("pool", wmslot, [], [wmkey], [
                lambda q, wm=wm, j=j: q.dma_start(out=wm[:, 0:c.FG, :], in_=w_f_out[l, :, j * 128:(j + 1) * 128].rearrange("(k p) c -> p k c", p=128)),
                lambda q, wm=wm, j=j: q.dma_start(out=wm[:, c.FG:c.FG + H, :], in_=w_mla_out[l, :, j * 128:(j + 1) * 128].rearrange("(k p) c -> p k c", p=128)),
                lambda q, wm=wm, j=j: q.dma_start(out=wm[:, c.FG + H:nbr, :], in_=w_conv_out[l, :, j * 128:(j + 1) * 128].rearrange("(k p) c -> p k c", p=128))])
            for ti, (t0, sz, isc) in enumerate(tiles):
                if (t0, sz, isc) not in act_tiles:
                    continue
                hreads = [("hTm", ti)]
                gts = []
                for br in range(3):
                    b = nb()
                    mm_group(b, [wgkey] + hreads, [(wg[:, k, br * 128:(br + 1) * 128], hT[:, k, t0:t0 + sz]) for k in range(KD)],
                             outap=ps[b][:, 0:sz])
                    gt, _, gk = gtr.next()
                    s.op("act", [("ps", b), "vecs"], [gk], lambda e, gt=gt, b=b, br=br, j=j: e.activation(
                        out=gt[:, 0:sz], in_=ps[b][:, 0:sz], func=AF.Sigmoid, bias=vcol(("bgate", l), br * KD + j), scale=1.0))
                    gts.append((gt, gk))
                srcs3 = [
                    ([(wm[:, g, :], fT[:, g, t0:t0 + sz]) for g in range(c.FG)], [("fT", g, t0) for g in range(c.FG)]),
                    ([(wm[:, c.FG + h, :], attnT[:, h, t0:t0 + sz]) for h in range(H)], [("attnT", h, ti) for h in range(H)]),
                    ([(wm[:, c.FG + H + i, :], convT[:, i, t0:t0 + sz]) for i in range(CC)],
                     [("convT", i, 0 if not isc else LT) for i in range(CC)]),
                ]
                tts = []
                for br in range(3):
                    b = nb()
                    mm_group(b, [wmkey] + srcs3[br][1], srcs3[br][0], outap=ps[b][:, 0:sz])
                    tt, _, tk = ttr.next()
                    gt, gk = gts[br]
                    s.op("dve", [("ps", b), gk], [tk], lambda e, tt=tt, b=b, gt=gt: e.tensor_tensor(
                        out=tt[:, 0:sz], in0=ps[b][:, 0:sz], in1=gt[:, 0:sz], op=ALU.mult))
                    tts.append((tt, tk))
                s.op("dve", [tts[0][1], tts[1][1]], [tts[0][1]], lambda e, a=tts[0][0], b_=tts[1][0]: e.tensor_tensor(
                    out=a[:, 0:sz], in0=a[:, 0:sz], in1=b_[:, 0:sz], op=ALU.add))
                ms, _, mk = mstg.next()
                s.op("dve", [tts[0][1], tts[2][1]], [mk], lambda e, ms=ms, a=tts[0][0], b_=tts[2][0]: e.tensor_tensor(
                    out=ms[:, 0:sz], in0=a[:, 0:sz], in1=b_[:, 0:sz], op=ALU.add))
                s.dma("sp", stslot(), [mk], [("m_s", j, ti)], [
                    lambda q, ms=ms, j=j, t0=t0, sz=sz: q.dma_start(out=m_s[j, :, t0:t0 + sz], in_=ms[:, 0:sz])])
        PM.close()
        PA.close()
        PXc.close()
        if upto == "merge" and l == 0:
            break

        def residual_update(P, xcr, b, j, ti, t0, sz, isc, gate_mi, final):
            m = 1 if isc else 0
            xc, xslot, xk = xcr.next()
            s.dma("sp", xslot, [("xs", j, ti)], [xk], [
                lambda q, xc=xc: q.dma_start(out=xc[:, 0:sz], in_=xs[j, :, t0:t0 + sz])])
            s.op("dve", [("ps", b), xk, "modt"], [xk], lambda e, xc=xc: e.scalar_tensor_tensor(
                out=xc[:, 0:sz], in0=ps[b][:, 0:sz], scalar=modt[:, gate_mi, j, m:m + 1], in1=xc[:, 0:sz],
                op0=ALU.mult, op1=ALU.add))
            if final:
                s.dma("sp", stslot(), [xk], [("out", j, ti)], [
                    lambda q, xc=xc: q.dma_start(out=out_T[j, :, t0:t0 + sz], in_=xc[:, 0:sz])])
            else:
                s.dma("sp", stslot(), [xk], [("xs", j, ti)], [
                    lambda q, xc=xc: q.dma_start(out=xs[j, :, t0:t0 + sz], in_=xc[:, 0:sz])])

        PO = Phase()
        wo = PO.t("wo", [128, KD, D], BF16)
        woslot = s.slot("wo%d" % l)
        fns = []
        for a in range(0, KD, 4):
            src = w_out[l, a * 128:(a + 4) * 128, :].rearrange("(k p) c -> p k c", p=128) if KD >= 4 else None
            if KD >= 4:
                fns.append(lambda q, src=src, a=a: q.dma_start(out=wo[:, a:a + 4, :], in_=src))
        if KD < 4:
            fns = [lambda q: q.dma_start(out=wo[:], in_=w_out[l].rearrange("(k p) c -> p k c", p=128))]
        s.dma("pool", woslot, [], ["wo"], fns)
        mtr = PO.ring("mt", 2, [128, KD, TS], BF16)
        xcr = PO.ring("xc", 3, [128, TS], F32)
        for ti, (t0, sz, isc) in enumerate(tiles):
            if (t0, sz, isc) not in act_tiles:
                continue
            mt, mslot, mtk = mtr.next()
            s.dma("sp", mslot, [("m_s", j, ti) for j in range(KD)], [mtk], [
                lambda q, mt=mt, t0=t0, sz=sz: q.dma_start(out=mt[:, :, 0:sz], in_=m_s[:, :, t0:t0 + sz].rearrange("k p t -> p k t"))])
            for j in range(KD):
                b = nb()
                mm_group(b, ["wo", mtk], [(wo[:, k, j * 128:(j + 1) * 128], mt[:, k, 0:sz]) for k in range(KD)], outap=ps[b][:, 0:sz])
                residual_update(PO, xcr, b, j, ti, t0, sz, isc, 2, False)
        PO.close()
        if upto == "mixer" and l == 0:
            break

        for m in range(2):
            s.res[("A", "x", m)] = s.res.get(("A", "nffn", m))
        PFN = Phase()
        h2 = PFN.t("h2", [128, KD, NT], BF16)
        PN2 = Phase()
        norm_phase(PN2, A2, 3, h2, act_tiles, False)
        PN2.close()
        aT = PFN.t("aT", [128, c.FH, NT], BF16)
        sgr = PFN.ring("sg", 3, [128, TS], F32)
        wdr = PFN.ring("wd", 2, [128, c.FH, 128], BF16)
        xcr = PFN.ring("xc", 3, [128, TS], F32)
        for half in range(2):
            for fl in range(c.FH):
                f = half * c.FH + fl
                wt, wslot, wkey = wring.next()
                fns = []
                for wi, wsrc in enumerate((w_ffn_gate, w_ffn_up)):
                    for a in range(0, KD, 8):
                        bnd = min(KD, a + 8)
                        src = wsrc[l, a * 128:bnd * 128, f * 128:(f + 1) * 128].rearrange("(k p) c -> p k c", p=128)
                        fns.append(lambda q, src=src, a=a, bnd=bnd, wi=wi, wt=wt: q.dma_start(out=wt[:, a:bnd, wi * 128:(wi + 1) * 128], in_=src))
                s.dma("pool", wslot, [], [wkey], fns)
                for ti, (t0, sz, isc) in enumerate(tiles):
                    if (t0, sz, isc) not in act_tiles:
                        continue
                    hreads = [("hT", ti, k) for k in range(KD)]
                    bg = nb()
                    mm_group(bg, [wkey] + hreads, [(wt[:, k, 0:128], h2[:, k, t0:t0 + sz]) for k in range(KD)], outap=ps[bg][:, 0:sz])
                    bu = nb()
                    mm_group(bu, [wkey] + hreads, [(wt[:, k, 128:256], h2[:, k, t0:t0 + sz]) for k in range(KD)], outap=ps[bu][:, 0:sz])
                    sg, _, sgk = sgr.next()
                    s.op("act", [("ps", bg)], [sgk], lambda e, sg=sg, bg=bg: e.activation(out=sg[:, 0:sz], in_=ps[bg][:, 0:sz], func=AF.Silu))
                    s.op("dve", [("ps", bu), sgk], [("aT", fl, ti)], lambda e, sg=sg, bu=bu, fl=fl: e.tensor_tensor(
                        out=aT[:, fl, t0:t0 + sz], in0=ps[bu][:, 0:sz], in1=sg[:, 0:sz], op=ALU.mult))
            for j in range(KD):
                wd, wdslot, wdk = wdr.next()
                fns = []
                for a in range(0, c.FH, 8):
                    bnd = min(c.FH, a + 8)
                    src = w_ffn_down[l, (half * c.FH + a) * 128:(half * c.FH + bnd) * 128, j * 128:(j + 1) * 128].rearrange("(k p) c -> p k c", p=128)
                    fns.append(lambda q, src=src, a=a, bnd=bnd, wd=wd: q.dma_start(out=wd[:, a:bnd, :], in_=src))
                s.dma("pool", wdslot, [], [wdk], fns)
                for ti, (t0, sz, isc) in enumerate(tiles):
                    if (t0, sz, isc) not in act_tiles:
                        continue
                    b = nb()
                    mm_group(b, [wdk] + [("aT", fl, ti) for fl in range(c.FH)],
                             [(wd[:, fl, :], aT[:, fl, t0:t0 + sz]) for fl in range(c.FH)], outap=ps[b][:, 0:sz])
                    residual_update(PFN, xcr, b, j, ti, t0, sz, isc, 5, last and half == 1)
        PFN.close()

    if upto is not None:
        s.barrier()
        s.dma("sp", stslot(), [], ["outdbg"], [lambda q: q.dma_start(out=out_T, in_=xs[:, :, 0:LT])])
    s.barrier()
    G.close()
    for p in reversed(ps_cm):
        p.__exit__(None, None, None)
    return nc, s


def host_inputs(cfg, inp):
    c = cfg
    f32 = np.float32
    KD = c.KD

    def fm(v, rows=128):
        v = np.asarray(v, f32)
        return np.ascontiguousarray(v.reshape(-1, rows).T)

    inv_freq = (c.THETA ** (-np.arange(16, dtype=f32) / f32(16))).astype(f32)
    cc_idx = np.arange(128)
    ang_c = 2.0 * np.pi * np.outer(cc_idx, cc_idx) / 128.0
    dft_ch = np.stack([np.cos(ang_c) / math.sqrt(128.0), -np.sin(ang_c) / math.sqrt(128.0)], axis=1).astype(ml_dtypes.bfloat16)
    shared = {}
    for k in ("w_in", "w_uq", "w_ukv", "w_f_out", "w_mla_out", "w_conv_out", "w_out",
              "w_ffn_gate", "w_ffn_up", "w_ffn_down"):
        shared[k] = np.ascontiguousarray(np.asarray(inp[k], f32))
    shared["dft_ch"] = dft_ch
    maps = []
    for core in range(8):
        b, r = core // R, core % R
        m = dict(shared)
        xl = np.asarray(inp["x"][b, r * c.LT:(r + 1) * c.LT, :], f32)
        xc = np.asarray(inp["ctx"][b, r * c.CT:(r + 1) * c.CT, :], f32)
        xt = np.concatenate([xl, xc], axis=0).T
        m["xT"] = np.ascontiguousarray(xt.reshape(KD, 128, c.NT))
        vecs = np.zeros((128, c.NV), f32)
        V = c.voff
        ct = np.stack([fm(inp["c"][b]), fm(inp["c_ctx"])], axis=2)
        vecs[:, V["cT"]:V["cT"] + KD * 2] = ct.reshape(128, KD * 2)
        m["w_ada"] = np.ascontiguousarray(np.asarray(inp["w_ada"], f32)[:, :, r * c.MW:(r + 1) * c.MW])
        if r > 0:
            vecs[:, V["sel"] + (r - 1)] = 1.0
        if r < R - 1:
            vecs[:, V["sel"] + 4 + (r + 1)] = 1.0
        for l in range(c.DEPTH):
            vecs[:, V[("nmix", l)]:V[("nmix", l)] + KD] = fm(inp["norm_mix"][l])
            vecs[:, V[("nffn", l)]:V[("nffn", l)] + KD] = fm(inp["norm_ffn"][l])
            ba = fm(inp["b_ada"][l])
            vecs[:, V[("bada", l)]:V[("bada", l)] + 12 * KD] = np.repeat(ba, 2, axis=1)
            vecs[:, V[("bgate", l)]:V[("bgate", l)] + 3 * KD] = fm(inp["b_gate"][l])
            vecs[:, V[("qa", l)]:V[("qa", l)] + c.QC] = fm(inp["q_a_norm"][l])
            vecs[:, V[("kva", l)]:V[("kva", l)] + c.KVC] = fm(inp["kv_a_norm"][l])
            for pre, nm in (("gq", "q_norm"), ("gk", "k_norm")):
                g = np.asarray(inp[nm][l], f32)
                vecs[:, V[(pre + "_n", l)]] = g[0:128]
                vecs[0:64, V[(pre + "_r", l)]] = g[128:192]
                vecs[0:64, V[(pre + "_s", l)]] = np.concatenate([g[160:192], g[128:160]])
            cw = np.asarray(inp["conv_w"][l], f32)
            for tap in range(3):
                vecs[:, V[("convw", l)] + tap * c.CC:V[("convw", l)] + (tap + 1) * c.CC] = fm(cw[tap])
        m["vecs"] = vecs
        t = (r * c.LT + np.arange(c.LT)).astype(np.int64)
        row = (t // c.GRID_W).astype(f32)
        col = (t % c.GRID_W).astype(f32)
        ang = np.concatenate([row[:, None] * inv_freq[None, :], col[:, None] * inv_freq[None, :]], axis=1).astype(f32)
        cs = np.cos(ang).astype(f32).T
        sn = np.sin(ang).astype(f32).T
        rope = np.zeros((64, 2, c.NT), f32)
        rope[:, 0, :] = 1.0
        rope[0:32, 0, :c.LT] = cs
        rope[32:64, 0, :c.LT] = cs
        rope[0:32, 1, :c.LT] = -sn
        rope[32:64, 1, :c.LT] = sn
        m["rope"] = rope
        tt = np.arange(c.SEQ, dtype=np.int64)[:, None]
        tp = (r * c.LT + np.arange(c.LT, dtype=np.int64))[None, :]
        a = 2.0 * np.pi * ((tt * tp) % c.SEQ).astype(np.float64) / c.SEQ
        m["dft_c"] = (np.cos(a) / math.sqrt(c.SEQ)).astype(ml_dtypes.bfloat16)
        m["dft_s"] = (np.sin(a) / math.sqrt(c.SEQ)).astype(ml_dtypes.bfloat16)
        tt = np.arange(c.CTX, dtype=np.int64)[:, None]
        tp = (r * c.CT + np.arange(c.CT, dtype=np.int64))[None, :]
        a = 2.0 * np.pi * ((tt * tp) % c.CTX).astype(np.float64) / c.CTX
        m["dftx_c"] = (np.cos(a) / math.sqrt(c.CTX)).astype(ml_dtypes.bfloat16)
        m["dftx_s"] = (np.sin(a) / math.sqrt(c.CTX)).astype(ml_dtypes.bfloat16)
        maps.append(m)
    return maps


def assemble(cfg, results):
    c = cfg
    out = np.zeros((c.B, c.SEQ, c.D), np.float32)
    for core in range(8):
        b, r = core // R, core % R
        o = np.asarray(results[core]["outT"], np.float32).reshape(c.D, c.LT)
        out[b, r * c.LT:(r + 1) * c.LT, :] = o.T
    return out


def kernel(**inputs):
    cfg = Cfg(FULL_CFG)
    nc, _ = build_program(cfg)
    maps = host_inputs(cfg, inputs)
    res = run_bass_kernel_spmd(nc, maps, core_ids=list(range(8)))
    return assemble(cfg, res.results)
```

```python
import math
import numpy as np
import ml_dtypes
import concourse.bass as bass
import concourse.mybir as mybir
from concourse.bass_utils import run_bass_kernel_spmd

F32 = mybir.dt.float32
BF16 = mybir.dt.bfloat16
AF = mybir.ActivationFunctionType
ALU = mybir.AluOpType

FULL_CFG = dict(D=2048, B=2, SEQ=4096, DEPTH=2, GRID_W=64, CTX=256, FG=4, H=8, QR=512, KVR=256,
                CW=512, DFF=5632, NOPE=128, ROPE=64, VH=128, EPS=1e-6, THETA=10000.0)
R = 4


class Cfg:
    def __init__(self, d):
        self.__dict__.update(d)
        c = self
        c.KD = c.D // 128
        c.FWID = c.FG * 128
        c.QC = c.QR // 128
        c.KVC = c.KVR // 128
        c.CC = c.CW // 128
        c.QKH = c.NOPE + c.ROPE
        c.OFF_F = 0
        c.OFF_CQ = c.OFF_F + c.FWID
        c.OFF_CKV = c.OFF_CQ + c.QR
        c.OFF_KR = c.OFF_CKV + c.KVR
        c.OFF_CX = c.OFF_KR + c.ROPE
        c.OFF_CB = c.OFF_CX + c.CW
        c.OFF_CC = c.OFF_CB + c.CW
        c.OFF_G = c.OFF_CC + c.CW
        c.N_IN = c.OFF_G + 3 * c.D
        c.LT = c.SEQ // R
        c.CT = c.CTX // R
        c.NT = c.LT + c.CT
        c.TS = d.get('TS', min(512, c.LT))
        c.NLT = c.LT // c.TS
        c.LCH = c.LT // 128
        c.NTC = c.LCH + 1
        c.NK = c.SEQ + c.CTX
        c.CKS = min(128, c.CTX)
        c.NCK = c.CTX // c.CKS
        c.NKC = c.SEQ // 128 + c.NCK
        c.FC = c.DFF // 128
        c.FH = c.FC // 2
        c.MW = 6 * c.D // R
        c.MJ = c.MW // 128
        assert c.MW % 128 == 0
        off = {}
        n = 0

        def add(name, w):
            nonlocal n
            off[name] = n
            n += w
        add("cT", c.KD * 2)
        add("sel", 8)
        for l in range(c.DEPTH):
            add(("nmix", l), c.KD)
            add(("nffn", l), c.KD)
            add(("bada", l), 6 * c.KD * 2)
            add(("bgate", l), 3 * c.KD)
            add(("qa", l), c.QC)
            add(("kva", l), c.KVC)
            for nm in ("gq_n", "gq_r", "gq_s", "gk_n", "gk_r", "gk_s"):
                add((nm, l), 1)
            add(("convw", l), 3 * c.CC)
        c.voff = off
        c.NV = n


class Slot:
    def __init__(self, sched):
        self.sched = sched
        self.sems = {}
        self.counts = {}

    def toks(self):
        return [(self.sems[k], self.counts[k]) for k in self.sems if self.counts[k]]

    def sem(self, kind):
        if kind not in self.sems:
            self.sems[kind] = self.sched.newsem("sl%d" % self.sched.nsem)
            self.counts[kind] = 0
        return self.sems[kind]


class Sched:
    def __init__(self, nc):
        self.nc = nc
        self.eng = {"pe": nc.tensor, "act": nc.scalar, "dve": nc.vector, "pool": nc.gpsimd, "sp": nc.sync}
        self.esem = {}
        self.ecnt = {}
        self.nsem = 0
        for e in ("pe", "act", "dve", "pool"):
            self.esem[e] = self.newsem("e_" + e)
            self.ecnt[e] = 0
        self.seen = {e: {} for e in self.eng}
        self.res = {}
        self.slots = []
        self.free_slots = []
        self.nwait = 0

    def newsem(self, name):
        self.nsem += 1
        return self.nc.semaphore(name).__enter__()

    def slot(self, name=None):
        if self.free_slots:
            return self.free_slots.pop()
        s = Slot(self)
        self.slots.append(s)
        return s

    def put_slot(self, sl):
        self.free_slots.append(sl)

    def _deps(self, e, reads, writes):
        toks = []
        for r in reads:
            st = self.res.get(r)
            if st and st[0]:
                toks.append(st[0])
        for w in writes:
            st = self.res.get(w)
            if st:
                if st[0]:
                    toks.append(st[0])
                toks.extend(st[1])
        return toks

    def _wait(self, e, toks):
        need = {}
        for (sem, v) in toks:
            if e == "pe" and sem is self.esem["pe"]:
                continue
            if self.seen[e].get(sem, 0) >= v:
                continue
            if need.get(sem, 0) < v:
                need[sem] = v
        for sem, v in need.items():
            self.eng[e].wait_ge(sem, v)
            self.seen[e][sem] = v
            self.nwait += 1

    def _record(self, reads, writes, tok):
        for r in reads:
            st = self.res.get(r)
            if st is None:
                st = self.res[r] = [None, []]
            st[1].append(tok)
        for w in writes:
            self.res[w] = [tok, []]

    def op(self, e, reads, writes, fn):
        self._wait(e, self._deps(e, reads, writes))
        ins = fn(self.eng[e])
        self.ecnt[e] += 1
        ins.then_inc(self.esem[e], 1)
        tok = (self.esem[e], self.ecnt[e])
        self.seen[e][self.esem[e]] = max(self.seen[e].get(self.esem[e], 0), 0)
        self._record(reads, writes, tok)
        return tok

    def dma(self, q, slot, reads, writes, fns):
        toks = self._deps(q, reads, writes)
        toks.extend(slot.toks())
        self._wait(q, toks)
        sem = slot.sem(q)
        for fn in fns:
            ins = fn(self.eng[q])
            ins.then_inc(sem, 16)
            slot.counts[q] += 16
        tok = (sem, slot.counts[q])
        self._record(reads, writes, tok)
        return tok

    def collective(self, reads, writes, fn):
        sem = self.newsem("cc%d" % self.nsem)
        self._wait("pool", self._deps("pool", reads, writes))
        fn(self.eng["pool"]).then_inc(sem)
        tok = (sem, 1)
        self._record(reads, writes, tok)
        return tok

    def barrier(self):
        toks = [(self.esem[e], self.ecnt[e]) for e in self.esem if self.ecnt[e]]
        for sl in self.slots:
            toks += sl.toks()
        for st in self.res.values():
            if st is None:
                continue
            if st[0]:
                toks.append(st[0])
            toks.extend(st[1])
        for e in self.eng:
            self._wait(e, toks)
        self.res = {}


class Ring:
    def __init__(self, s, nc, name, n, shape, dtype):
        self.bufs = [nc.sbuf_tensor("%s%d" % (name, i), shape, dtype) for i in range(n)]
        self.t = [b.__enter__() for b in self.bufs]
        self.slots = [None] * n
        self.s = s
        self.name = name
        self.n = n
        self.i = 0

    def next(self):
        i = self.i % self.n
        self.i += 1
        if self.slots[i] is None:
            self.slots[i] = self.s.slot()
        return self.t[i], self.slots[i], (self.name, i)

    def close(self):
        for sl in self.slots:
            if sl is not None:
                self.s.put_slot(sl)
        for b in reversed(self.bufs):
            b.__exit__(None, None, None)


def build_program(cfg, upto=None):
    c = cfg
    nc = bass.Bass("TRN2", target_bir_lowering=False)
    s = Sched(nc)
    L, D, KD, NT, LT, CT, TS = c.DEPTH, c.D, c.KD, c.NT, c.LT, c.CT, c.TS
    H, QC, KVC, CC, NK, NKC = c.H, c.QC, c.KVC, c.CC, c.NK, c.NKC

    def din(name, shape, dt=F32):
        return nc.dram_tensor(name, list(shape), dt, kind="ExternalInput").ap()

    def dscr(name, shape, dt):
        return nc.dram_tensor(name, list(shape), dt).ap()

    xT_in = din("xT", [KD, 128, NT])
    vecs_in = din("vecs", [128, c.NV])
    rope_in = din("rope", [64, 2, NT])
    dft_c = din("dft_c", [c.SEQ, LT], BF16)
    dft_s = din("dft_s", [c.SEQ, LT], BF16)
    dftx_c = din("dftx_c", [c.CTX, CT], BF16)
    dftx_s = din("dftx_s", [c.CTX, CT], BF16)
    dft_ch = din("dft_ch", [128, 2, 128], BF16)
    w_ada = din("w_ada", [L, D, c.MW])
    w_in = din("w_in", [L, D, c.N_IN])
    w_uq = din("w_uq", [L, c.QR, H * c.QKH])
    w_ukv = din("w_ukv", [L, c.KVR, H * (c.NOPE + c.VH)])
    w_f_out = din("w_f_out", [L, c.FWID, D])
    w_mla_out = din("w_mla_out", [L, H * c.VH, D])
    w_conv_out = din("w_conv_out", [L, c.CW, D])
    w_out = din("w_out", [L, D, D])
    w_ffn_gate = din("w_ffn_gate", [L, D, c.DFF])
    w_ffn_up = din("w_ffn_up", [L, D, c.DFF])
    w_ffn_down = din("w_ffn_down", [L, c.DFF, D])
    out_T = nc.dram_tensor("outT", [KD, 128, LT], F32, kind="ExternalOutput").ap()

    xs = dscr("xs", [KD, 128, NT], F32)
    h_s = dscr("h_s", [KD, 128, NT], BF16)
    q_s = dscr("q_s", [H, 192, NT], BF16)
    m_s = dscr("m_s", [KD, 128, NT], BF16)
    exa_t = [dscr("exa%d" % tc, [128 if tc < c.LCH else CT, c.FWID], BF16) for tc in range(c.NTC)]
    ga_t = [dscr("ga%d" % tc, [R * (128 if tc < c.LCH else CT), c.FWID], BF16) for tc in range(c.NTC)]
    NXB = 2 * KVC + 2
    exb_t = [dscr("exb%d" % i, [64, NT], BF16) for i in range(NXB)]
    gb_t = [dscr("gb%d" % i, [R * 64, NT], BF16) for i in range(NXB)]
    exc = dscr("exc", [128, 4 * CC], F32)
    exm = dscr("exm", [128, c.MJ * 2], F32)
    gm = dscr("gm", [R * 128, c.MJ * 2], F32)
    gc = dscr("gc", [R * 128, 4 * CC], F32)
    groups = [[0, 1, 2, 3], [4, 5, 6, 7]]

    tiles = [(i * TS, TS, False) for i in range(c.NLT)] + [(LT, CT, True)]
    kchunks = [(i * 128, 128) for i in range(c.SEQ // 128)] + [(c.SEQ + i * c.CKS, c.CKS) for i in range(c.NCK)]
    ctx_kchunks = list(range(c.SEQ // 128, NKC))

    uid = [0]

    def T(name, shape, dt):
        uid[0] += 1
        cm = nc.sbuf_tensor("%s_u%d" % (name, uid[0]), list(shape), dt)
        return cm, cm.__enter__()

    class Phase:
        def __init__(self):
            self.cms = []
            self.rings = []

        def t(self, name, shape, dt):
            cm, t = T(name, shape, dt)
            self.cms.append(cm)
            return t

        def ring(self, name, n, shape, dt):
            uid[0] += 1
            r = Ring(s, nc, "%s_u%d_" % (name, uid[0]), n, shape, dt)
            self.cms.append(r)
            return r

        def close(self):
            s.barrier()
            for cm in reversed(self.cms):
                if isinstance(cm, Ring):
                    cm.close()
                else:
                    cm.__exit__(None, None, None)

    G = Phase()
    vecs = G.t("vecs", [128, c.NV], F32)
    ones = G.t("ones", [128, 128], BF16)
    rope = G.t("ropet", [64, 2, NT], F32)
    dch = G.t("dch", [128, 2, 128], BF16)
    modt = G.t("modt", [128, 6, KD, 2], F32)
    A1 = G.t("A1", [128, KD, 2], F32)
    A2 = G.t("A2", [128, KD, 2], F32)
    gsc = G.t("gsc", [128, 8], F32)
    s2 = G.t("s2", [128, KD, 2], BF16)
    gmt = G.t("gmt", [128, R * c.MJ * 2], F32)
    mst = G.t("mst", [128, c.MJ * 2], F32)
    epsb = G.t("epsb", [128, 1], F32)
    wring = G.ring("wr", 3, [128, KD, 512], BF16)
    ps_cm = [nc.psum_tensor("ps%d" % i, [128, 512], F32) for i in range(8)]
    ps = [p.__enter__() for p in ps_cm]
    psi = [0]

    reserved = set()

    def nb():
        while True:
            b = psi[0] % 8
            psi[0] += 1
            if b not in reserved:
                return b

    def reserve():
        b = nb()
        reserved.add(b)
        return b

    def release(*bs):
        for b in bs:
            reserved.discard(b)

    ld0 = s.slot("ld0")
    st_slots = [s.slot("st%d" % i) for i in range(4)]
    sti = [0]

    def stslot():
        sl = st_slots[sti[0] % 4]
        sti[0] += 1
        return sl

    ld_slots = [s.slot("ldx%d" % i) for i in range(4)]
    ldi = [0]

    def ldslot():
        sl = ld_slots[ldi[0] % 4]
        ldi[0] += 1
        return sl

    V = c.voff

    def vcol(name, i=0, n=1, rows=128):
        o = V[name] + i
        return vecs[0:rows, o:o + n]

    s.dma("sp", ld0, [], ["vecs", "rope", "dch"], [
        lambda q: q.dma_start(out=vecs[:], in_=vecs_in),
        lambda q: q.dma_start(out=rope[:], in_=rope_in),
        lambda q: q.dma_start(out=dch[:], in_=dft_ch),
    ])
    s.op("dve", [], ["ones"], lambda e: e.memset(ones[:], 1.0))
    s.op("dve", [], ["epsb"], lambda e: e.memset(epsb[:], c.EPS))
    s.dma("sp", stslot(), [], [("xs", j, ti) for j in range(KD) for ti in range(len(tiles))],
          [lambda q: q.dma_start(out=xs, in_=xT_in)])
    s.op("act", ["vecs"], ["s2"], lambda e: e.activation(
        out=s2[:].rearrange("p k t -> p (k t)"), in_=vcol("cT", 0, KD * 2), func=AF.Silu))

    def wload(wap, l, k0, nk, c0, ncol, dst, slot, key, extra_reads=()):
        fns = []
        step = 8
        for a in range(0, nk, step):
            b = min(nk, a + step)
            src = wap[l, (k0 + a) * 128:(k0 + b) * 128, c0:c0 + ncol].rearrange("(k p) c -> p k c", p=128)
            fns.append(lambda q, src=src, a=a, b=b: q.dma_start(out=dst[:, a:b, 0:ncol], in_=src))
        return s.dma("pool", slot, list(extra_reads), [key], fns)

    def mm_group(bank, reads, pairs, outap=None, writes=None):
        o = outap if outap is not None else ps[bank][:]
        n = len(pairs)

        def fn(e):
            ins = None
            for i, (a, b) in enumerate(pairs):
                ins = e.matmul(o, a, b, start=(i == 0), stop=(i == n - 1))
            return ins
        return s.op("pe", reads, writes if writes is not None else [("ps", bank)], fn)

    def rms_rstd(bank, sz, dim, dst, dst_key, rt, rt_key, rows=128):
        s.op("act", [("ps", bank), "epsb"], [rt_key], lambda e: e.activation(
            out=rt[0:rows, 0:sz], in_=ps[bank][0:rows, 0:sz], func=AF.Sqrt, scale=1.0 / dim, bias=epsb[0:rows, 0:1]))
        s.op("dve", [rt_key], [dst_key], lambda e: e.reciprocal(out=dst, in_=rt[0:rows, 0:sz]))

    for l in range(L):
        last = (l == L - 1)
        act_tiles = [t for t in tiles if not (last and t[2])]

        mb = reserve()
        jj = 0
        for c0 in range(0, c.MW, 512):
            ncol = min(512, c.MW - c0)
            wt, wslot, wkey = wring.next()
            wload(w_ada, l, 0, KD, c0, ncol, wt, wslot, wkey)
            for j4 in range(ncol // 128):
                mm_group(mb, [wkey, "s2"], [(wt[:, k, j4 * 128:(j4 + 1) * 128], s2[:, k, :]) for k in range(KD)],
                         outap=ps[mb][:, jj * 2:(jj + 1) * 2])
                jj += 1
        s.op("dve", [("ps", mb)], ["mst"], lambda e: e.tensor_copy(out=mst[:], in_=ps[mb][:, 0:c.MJ * 2]))
        release(mb)
        s.dma("sp", stslot(), ["mst"], ["exm"], [lambda q: q.dma_start(out=exm, in_=mst[:])])
        s.collective(["exm"], ["gm"], lambda g: g.collective_compute(
            "AllGather", ALU.bypass, replica_groups=groups, ins=[exm.opt()], outs=[gm.opt()]))
        s.dma("sp", ldslot(), ["gm"], ["gmt"], [lambda q: q.dma_start(
            out=gmt[:].rearrange("p (r x) -> p r x", r=R), in_=gm.rearrange("(r p) x -> p r x", p=128))])
        s.op("dve", ["gmt", "vecs"], ["modt"], lambda e: e.tensor_tensor(
            out=modt[:].rearrange("p m k t -> p (m k t)"), in0=gmt[:], in1=vcol(("bada", l), 0, 6 * KD * 2), op=ALU.add))
        for (Ax, nm, mi) in ((A1, "nmix", 1), (A2, "nffn", 4)):
            for t2 in range(2):
                s.op("dve", ["modt", "vecs"], [("A", nm, t2)], lambda e, Ax=Ax, nm=nm, mi=mi, t2=t2: e.scalar_tensor_tensor(
                    out=Ax[:, :, t2], in0=modt[:, mi, :, t2], scalar=1.0, in1=vcol((nm, l), 0, KD),
                    op0=ALU.add, op1=ALU.mult))
        sc = float(c.QKH) ** -0.5
        for i, nm in enumerate(("gq_n", "gq_r", "gq_s")):
            s.op("dve", ["vecs"], [("gsc", i)], lambda e, i=i, nm=nm: e.tensor_scalar(
                out=gsc[:, i:i + 1], in0=vcol((nm, l)), scalar1=sc, scalar2=None, op0=ALU.mult))
        if upto == "mod" and l == 0:
            break

        def norm_phase(P, Ax, mi_shift, hT, use_tiles, store_h):
            KH = (KD + 1) // 2
            xr = P.ring("xr", 2, [128, KH, TS], F32)
            sqr = P.ring("sqr", 1, [128, KD, TS], BF16)
            rt = P.ring("rt", 2, [128, TS], F32)
            rr = P.ring("rr", 2, [128, TS], F32)
            tmp = P.ring("ntmp", 3, [128, TS], F32)
            for ti, (t0, sz, isc) in enumerate(tiles):
                if (t0, sz, isc) not in use_tiles:
                    continue
                m = 1 if isc else 0
                xch = {}
                for hf in range(2):
                    k0, k1 = hf * KH, min(KD, (hf + 1) * KH)
                    if k0 >= k1:
                        continue
                    xt, xslot, xkey = xr.next()
                    s.dma("sp", xslot, [("xs", j, ti) for j in range(k0, k1)], [xkey], [
                        lambda q, xt=xt, t0=t0, sz=sz, k0=k0, k1=k1: q.dma_start(
                            out=xt[:, 0:k1 - k0, 0:sz], in_=xs[k0:k1, :, t0:t0 + sz].rearrange("k p t -> p k t"))])
                    for k in range(k0, k1):
                        xch[k] = (xt, k - k0, xkey)
                sq, _, sqkey = sqr.next()
                for k in range(KD):
                    xt, kk, xkey = xch[k]
                    s.op("act", [xkey], [(sqkey, k)], lambda e, k=k, kk=kk, sq=sq, xt=xt, sz=sz: e.activation(
                        out=sq[:, k, 0:sz], in_=xt[:, kk, 0:sz], func=AF.Square))
                b = nb()
                mm_group(b, [(sqkey, k) for k in range(KD)] + ["ones"],
                         [(ones[:], sq[:, k, 0:sz]) for k in range(KD)], outap=ps[b][:, 0:sz])
                rtt, _, rtkey = rt.next()
                rrt, _, rrkey = rr.next()
                rms_rstd(b, sz, D, rrt[:, 0:sz], rrkey, rtt, rtkey)
                for k in range(KD):
                    xt, kk, xkey = xch[k]
                    tt, _, tkey = tmp.next()
                    s.op("dve", [xkey, rrkey, ("A", "x", m)], [tkey], lambda e, k=k, kk=kk, tt=tt, xt=xt, rrt=rrt, sz=sz, m=m: e.scalar_tensor_tensor(
                        out=tt[:, 0:sz], in0=xt[:, kk, 0:sz], scalar=Ax[:, k, m:m + 1], in1=rrt[:, 0:sz],
                        op0=ALU.mult, op1=ALU.mult))
                    s.op("act", [tkey, "modt"], [("hT", ti, k)], lambda e, k=k, tt=tt, sz=sz, t0=t0, m=m: e.activation(
                        out=hT[:, k, t0:t0 + sz], in_=tt[:, 0:sz], func=AF.Identity,
                        bias=modt[:, mi_shift, k, m:m + 1], scale=1.0))
                if store_h:
                    s.dma("sp", stslot(), [("hT", ti, k) for k in range(KD)], [("h_s", ti)], [
                        lambda q, t0=t0, sz=sz: q.dma_start(
                            out=h_s[:, :, t0:t0 + sz].rearrange("k p t -> p k t"), in_=hT[:, :, t0:t0 + sz])])

        for m in range(2):
            s.res[("A", "x", m)] = s.res.get(("A", "nmix", m))

        PXc = Phase()
        convT = PXc.t("convT", [128, CC, NT], BF16)
        PA = Phase()
        hT = PA.t("hT", [128, KD, NT], BF16)
        stg = PA.ring("stg", 2, [128, 512], BF16)
        PN = Phase()
        norm_phase(PN, A1, 0, hT, tiles, True)
        PN.close()
        if upto == "norm" and l == 0:
            PA.close()
            PXc.close()
            break

        def proj_block(col0, ncol, handlers, use_tiles):
            wt, wslot, wkey = wring.next()
            wload(w_in, l, 0, KD, col0, ncol, wt, wslot, wkey)
            for idx, (co, M, pb, evac) in enumerate(handlers):
                for ti, (t0, sz, isc) in enumerate(tiles):
                    if (t0, sz, isc) not in use_tiles:
                        continue
                    b = nb()
                    mm_group(b, [wkey] + [("hT", ti, k) for k in range(KD)],
                             [(wt[:, k, co:co + M], hT[:, k, t0:t0 + sz]) for k in range(KD)],
                             outap=ps[b][pb:pb + M, 0:sz])
                    evac(b, ti, t0, sz, M, idx, pb)
            return wkey

        def ev_copy(dst_fn, key_fn):
            def evac(b, ti, t0, sz, M, idx, pb):
                dst = dst_fn(idx, t0, sz, M)
                if (idx + ti) % 2:
                    s.op("act", [("ps", b)], [key_fn(idx, ti)], lambda e: e.activation(out=dst, in_=ps[b][pb:pb + M, 0:sz], func=AF.Copy))
                else:
                    s.op("dve", [("ps", b)], [key_fn(idx, ti)], lambda e: e.tensor_copy(out=dst, in_=ps[b][pb:pb + M, 0:sz]))
            return evac

        def feat_norm(P, src, nch, dim, gname, dst, ti, t0, sz, srckey, dstkey):
            sq, _, sqkey = P.sqb.next()
            for i in range(nch):
                s.op("act", [(srckey, i, ti)], [(sqkey, i)], lambda e, i=i, sq=sq: e.activation(
                    out=sq[:, i, 0:sz], in_=src[:, i, t0:t0 + sz], func=AF.Square))
            b = nb()
            mm_group(b, [(sqkey, i) for i in range(nch)] + ["ones"],
                     [(ones[:], sq[:, i, 0:sz]) for i in range(nch)], outap=ps[b][:, 0:sz])
            rtt, _, rtkey = P.rt.next()
            rrt, _, rrkey = P.rr.next()
            rms_rstd(b, sz, dim, rrt[:, 0:sz], rrkey, rtt, rtkey)
            for i in range(nch):
                s.op("dve", [(srckey, i, ti), rrkey, "vecs"], [(dstkey, i, ti)], lambda e, i=i, rrt=rrt: e.scalar_tensor_tensor(
                    out=dst[:, i, t0:t0 + sz], in0=src[:, i, t0:t0 + sz], scalar=vcol((gname, l), i), in1=rrt[:, 0:sz],
                    op0=ALU.mult, op1=ALU.mult))

        nti = len(tiles)
        P1 = Phase()
        ckvf = P1.t("ckvf", [128, KVC, NT], F32)
        krf = P1.t("krf", [64, 2, NT], F32)
        ckvn = P1.t("ckvn", [128, KVC, NT], BF16)
        krb = P1.t("krb", [64, 2, NT], BF16)
        P1.sqb = P1.ring("sqb", 2, [128, max(QC, KVC), TS], BF16)
        P1.rt = P1.ring("rt", 2, [128, TS], F32)
        P1.rr = P1.ring("rr", 2, [128, TS], F32)
        tmp64 = P1.ring("t64", 4, [64, TS], F32)
        hs = [(i * 128, 128, 0, ev_copy(lambda i_, t0, sz, M: ckvf[:, i_, t0:t0 + sz], lambda i_, ti: ("ckvf", i_, ti)))
              for i in range(KVC)]
        hs.append((c.KVR, 64, 0, ev_copy(lambda i_, t0, sz, M: krf[0:64, 0, t0:t0 + sz], lambda i_, ti: ("krf", 0, ti))))
        hs.append((c.KVR + 32, 32, 0, ev_copy(lambda i_, t0, sz, M: krf[0:32, 1, t0:t0 + sz], lambda i_, ti: ("krf", 1, ti))))
        hs.append((c.KVR, 32, 32, ev_copy(lambda i_, t0, sz, M: krf[32:64, 1, t0:t0 + sz], lambda i_, ti: ("krf", 2, ti))))
        proj_block(c.OFF_CKV, c.KVR + 64, hs, tiles)
        for ti, (t0, sz, isc) in enumerate(tiles):
            feat_norm(P1, ckvf, KVC, c.KVR, "kva", ckvn, ti, t0, sz, "ckvf", "ckvn")
        for ti, (t0, sz, isc) in enumerate(tiles):
            t1, _, k1 = tmp64.next()
            t2, _, k2 = tmp64.next()
            s.op("dve", [("krf", 0, ti), "vecs", "rope"], [k1], lambda e, t1=t1, t0=t0, sz=sz: e.scalar_tensor_tensor(
                out=t1[:, 0:sz], in0=krf[0:64, 0, t0:t0 + sz], scalar=vcol(("gk_r", l), rows=64), in1=rope[:, 0, t0:t0 + sz],
                op0=ALU.mult, op1=ALU.mult))
            s.op("dve", [("krf", 1, ti), ("krf", 2, ti), "vecs", "rope"], [k2], lambda e, t2=t2, t0=t0, sz=sz: e.scalar_tensor_tensor(
                out=t2[:, 0:sz], in0=krf[0:64, 1, t0:t0 + sz], scalar=vcol(("gk_s", l), rows=64), in1=rope[:, 1, t0:t0 + sz],
                op0=ALU.mult, op1=ALU.mult))
            s.op("dve", [k1, k2], [("krb", 0, ti)], lambda e, t1=t1, t2=t2, t0=t0, sz=sz: e.tensor_tensor(
                out=krb[:, 0, t0:t0 + sz], in0=t1[:, 0:sz], in1=t2[:, 0:sz], op=ALU.add))
            s.op("act", [("krf", 0, ti)], [("krb", 1, ti)], lambda e, t0=t0, sz=sz: e.activation(
                out=krb[:, 1, t0:t0 + sz], in_=krf[0:64, 0, t0:t0 + sz], func=AF.Square))
        for xi in range(NXB):
            if xi < 2 * KVC:
                i, hf = xi // 2, xi % 2
                rd = [("ckvn", i, ti) for ti in range(nti)]
                src = ckvn[hf * 64:(hf + 1) * 64, i, :]
            else:
                j = xi - 2 * KVC
                rd = [("krb", j, ti) for ti in range(nti)]
                src = krb[:, j, :]
            s.dma("sp", stslot(), rd, [("exb", xi)], [lambda q, xi=xi, src=src: q.dma_start(out=exb_t[xi], in_=src)])
            s.collective([("exb", xi)], [("gb", xi)], lambda g, xi=xi: g.collective_compute(
                "AllGather", ALU.bypass, replica_groups=groups, ins=[exb_t[xi].opt()], outs=[gb_t[xi].opt()]))
        P1.close()

        wt, wslot, wkey = wring.next()
        wload(w_in, l, 0, KD, c.OFF_F, c.FWID, wt, wslot, wkey)
        for tc in range(c.NTC):
            isc = tc == c.LCH
            if isc and last:
                continue
            t0 = tc * 128
            rows = CT if isc else 128
            ti = len(tiles) - 1 if isc else t0 // TS
            b = nb()
            mm_group(b, [wkey] + [("hT", ti, k) for k in range(KD)],
                     [(hT[:, k, t0:t0 + rows], wt[:, k, 0:c.FWID]) for k in range(KD)],
                     outap=ps[b][0:rows, 0:c.FWID])
            st, _, skey = stg.next()
            if tc % 2:
                s.op("act", [("ps", b)], [skey], lambda e, st=st, b=b, rows=rows: e.activation(
                    out=st[0:rows, 0:c.FWID], in_=ps[b][0:rows, 0:c.FWID], func=AF.Copy))
            else:
                s.op("dve", [("ps", b)], [skey], lambda e, st=st, b=b, rows=rows: e.tensor_copy(
                    out=st[0:rows, 0:c.FWID], in_=ps[b][0:rows, 0:c.FWID]))
            s.dma("sp", stslot(), [skey], [("exa", tc)], [
                lambda q, st=st, tc=tc, rows=rows: q.dma_start(out=exa_t[tc], in_=st[0:rows, 0:c.FWID])])
            s.collective([("exa", tc)], [("ga", tc)], lambda g, tc=tc: g.collective_compute(
                "AllGather", ALU.bypass, replica_groups=groups, ins=[exa_t[tc].opt()], outs=[ga_t[tc].opt()]))

        PQ = Phase()
        cqf = PQ.t("cqf", [128, QC, NT], F32)
        cqn = PQ.t("cqn", [128, QC, NT], BF16)
        PQ.sqb = PQ.ring("sqb", 2, [128, max(QC, KVC), TS], BF16)
        PQ.rt = PQ.ring("rt", 2, [128, TS], F32)
        PQ.rr = PQ.ring("rr", 2, [128, TS], F32)
        tmp64 = PQ.ring("t64", 4, [64, TS], F32)
        wuq = PQ.t("wuq", [128, QC, H * c.QKH], BF16)
        s.dma("pool", s.slot("wuq%d" % l), [], ["wuq"], [
            lambda q: q.dma_start(out=wuq[:], in_=w_uq[l].rearrange("(k p) c -> p k c", p=128))])
        qnf = PQ.ring("qnf", 2, [128, TS], F32)
        sqA = PQ.ring("sqA", 2, [128, TS], BF16)
        sqB = PQ.ring("sqB", 2, [64, TS], BF16)
        rqr = PQ.ring("rq", 2, [128, TS], F32)
        qnb = PQ.ring("qnb", 2, [128, TS], BF16)
        qrb = PQ.ring("qrb", 2, [64, TS], BF16)
        rtq = PQ.ring("rtq", 2, [128, TS], F32)
        proj_block(c.OFF_CQ, c.QR, [(i * 128, 128, 0, ev_copy(lambda i_, t0, sz, M: cqf[:, i_, t0:t0 + sz], lambda i_, ti: ("cqf", i_, ti)))
                                    for i in range(QC)], act_tiles)
        for ti, (t0, sz, isc) in enumerate(tiles):
            if (t0, sz, isc) not in act_tiles:
                continue
            feat_norm(PQ, cqf, QC, c.QR, "qa", cqn, ti, t0, sz, "cqf", "cqn")
            cq_reads = [("cqn", i, ti) for i in range(QC)]
            for h in range(H):
                hb = h * c.QKH
                bA = nb()
                mm_group(bA, ["wuq"] + cq_reads,
                         [(wuq[:, i, hb:hb + 128], cqn[:, i, t0:t0 + sz]) for i in range(QC)], outap=ps[bA][:, 0:sz])
                bB = nb()
                mm_group(bB, ["wuq"] + cq_reads,
                         [(wuq[:, i, hb + 128:hb + 192], cqn[:, i, t0:t0 + sz]) for i in range(QC)], outap=ps[bB][0:64, 0:sz])
                bC = nb()

                def fnsw(e, hb=hb, bC=bC, t0=t0, sz=sz):
                    ins = None
                    for (pb, co) in ((0, hb + 160), (32, hb + 128)):
                        for i in range(QC):
                            ins = e.matmul(ps[bC][pb:pb + 32, 0:sz], wuq[:, i, co:co + 32], cqn[:, i, t0:t0 + sz],
                                           start=(i == 0), stop=(i == QC - 1))
                    return ins
                s.op("pe", ["wuq"] + cq_reads, [("ps", bC)], fnsw)
                sa, _, sak = sqA.next()
                sb_, _, sbk = sqB.next()
                s.op("act", [("ps", bA)], [sak], lambda e, sa=sa, bA=bA, sz=sz: e.activation(out=sa[:, 0:sz], in_=ps[bA][:, 0:sz], func=AF.Square))
                s.op("act", [("ps", bB)], [sbk], lambda e, sb_=sb_, bB=bB, sz=sz: e.activation(out=sb_[:, 0:sz], in_=ps[bB][0:64, 0:sz], func=AF.Square))
                bS = nb()
                mm_group(bS, [sak, sbk, "ones"], [(ones[:], sa[:, 0:sz]), (ones[0:64, :], sb_[:, 0:sz])], outap=ps[bS][:, 0:sz])
                rtt, _, rtk = rtq.next()
                rq, _, rqk = rqr.next()
                rms_rstd(bS, sz, c.QKH, rq[:, 0:sz], rqk, rtt, rtk)
                qn, _, qnk = qnf.next()
                s.op("dve", [("ps", bA), ("gsc", 0)], [qnk], lambda e, qn=qn, bA=bA, sz=sz: e.tensor_scalar(
                    out=qn[:, 0:sz], in0=ps[bA][:, 0:sz], scalar1=gsc[:, 0:1], scalar2=None, op0=ALU.mult))
                qb, _, qbk = qnb.next()
                s.op("dve", [qnk, rqk], [qbk], lambda e, qb=qb, qn=qn, rq=rq, sz=sz: e.tensor_tensor(
                    out=qb[:, 0:sz], in0=qn[:, 0:sz], in1=rq[:, 0:sz], op=ALU.mult))
                s.dma("sp", stslot(), [qbk], [("q_s", h, ti, 0)], [
                    lambda q, qb=qb, h=h, t0=t0, sz=sz: q.dma_start(out=q_s[h, 0:128, t0:t0 + sz], in_=qb[:, 0:sz])])
                t1, _, k1 = tmp64.next()
                t2, _, k2 = tmp64.next()
                s.op("dve", [("ps", bB), ("gsc", 1), "rope"], [k1], lambda e, t1=t1, bB=bB, t0=t0, sz=sz: e.scalar_tensor_tensor(
                    out=t1[:, 0:sz], in0=ps[bB][0:64, 0:sz], scalar=gsc[0:64, 1:2], in1=rope[:, 0, t0:t0 + sz],
                    op0=ALU.mult, op1=ALU.mult))
                s.op("dve", [("ps", bC), ("gsc", 2), "rope"], [k2], lambda e, t2=t2, bC=bC, t0=t0, sz=sz: e.scalar_tensor_tensor(
                    out=t2[:, 0:sz], in0=ps[bC][0:64, 0:sz], scalar=gsc[0:64, 2:3], in1=rope[:, 1, t0:t0 + sz],
                    op0=ALU.mult, op1=ALU.mult))
                s.op("dve", [k1, k2], [k1], lambda e, t1=t1, t2=t2, sz=sz: e.tensor_tensor(
                    out=t1[:, 0:sz], in0=t1[:, 0:sz], in1=t2[:, 0:sz], op=ALU.add))
                qb2, _, qbk2 = qrb.next()
                s.op("dve", [k1, rqk], [qbk2], lambda e, t1=t1, qb2=qb2, rq=rq, sz=sz: e.tensor_tensor(
                    out=qb2[:, 0:sz], in0=t1[:, 0:sz], in1=rq[0:64, 0:sz], op=ALU.mult))
                s.dma("sp", stslot(), [qbk2], [("q_s", h, ti, 1)], [
                    lambda q, qb2=qb2, h=h, t0=t0, sz=sz: q.dma_start(out=q_s[h, 128:192, t0:t0 + sz], in_=qb2[:, 0:sz])])
        PQ.close()

        PC = Phase()
        uT = PC.t("uT", [128, CC, NT + 4], BF16)
        pbT = PC.t("pbT", [128, CC, NT], BF16)
        cxT = PC.t("cxT", [128, CC, NT], BF16)
        exct = PC.t("exct", [128, CC, 4], F32)
        gct = PC.t("gct", [128, R, CC, 4], F32)
        hal = PC.t("hal", [128, CC, 4], F32)
        ctmp = PC.ring("ctmp", 2, [128, LT], F32)

        def ucol(t0):
            return t0 + 1 if t0 < LT else t0 + 3

        proj_block(c.OFF_CX, c.CW, [(i * 128, 128, 0, ev_copy(lambda i_, t0, sz, M: cxT[:, i_, t0:t0 + sz], lambda i_, ti: ("cxT", i_, ti)))
                                    for i in range(CC)], act_tiles)

        def ev_u(b, ti, t0, sz, M, idx, pb):
            uc = ucol(t0)
            s.op("dve", [("ps", b), ("cxT", idx, ti)], [("uT", idx, ti)], lambda e: e.tensor_tensor(
                out=uT[:, idx, uc:uc + sz], in0=ps[b][:, 0:sz], in1=cxT[:, idx, t0:t0 + sz], op=ALU.mult))
        proj_block(c.OFF_CC, c.CW, [(i * 128, 128, 0, ev_u) for i in range(CC)], act_tiles)
        proj_block(c.OFF_CB, c.CW, [(i * 128, 128, 0, ev_copy(lambda i_, t0, sz, M: pbT[:, i_, t0:t0 + sz], lambda i_, ti: ("pbT", i_, ti)))
                                    for i in range(CC)], act_tiles)
        ti_last_lat = c.NLT - 1
        ti_ctx = len(tiles) - 1
        srcs = [(1, 0), (LT, ti_last_lat)]
        if not last:
            srcs += [(LT + 3, ti_ctx), (LT + 3 + CT - 1, ti_ctx)]
        else:
            s.op("dve", [], [("exct", 2), ("exct", 3)], lambda e: e.memset(exct[:, :, 2:4], 0.0))
        for kind, (col, ti) in enumerate(srcs):
            s.op("dve", [("uT", i, ti) for i in range(CC)], [("exct", kind)], lambda e, kind=kind, col=col: e.tensor_copy(
                out=exct[:, :, kind], in_=uT[:, :, col]))
        s.dma("sp", stslot(), [("exct", k) for k in range(4)], ["exc"], [
            lambda q: q.dma_start(out=exc, in_=exct[:].rearrange("p c k -> p (c k)"))])
        s.collective(["exc"], ["gc"], lambda g: g.collective_compute(
            "AllGather", ALU.bypass, replica_groups=groups, ins=[exc.opt()], outs=[gc.opt()]))
        s.dma("sp", ldslot(), ["gc"], ["gct"], [
            lambda q: q.dma_start(out=gct[:].rearrange("p r c k -> p r (c k)"), in_=gc.rearrange("(r p) x -> p r x", p=128))])
        streams = [(0, LT, 1, 1, 0)] if last else [(0, LT, 1, 1, 0), (LT, CT, 3, 3, 2)]
        for (t0, n, uo, kl, kr_) in streams:
            for side, kind, selo in ((0, kl, 0), (1, kr_, 4)):
                hk = ("hal", t0, side)
                dst = hal[:, :, (0 if t0 == 0 else 2) + side]
                s.op("dve", ["gct", "vecs"], [hk], lambda e, dst=dst, kind=kind, selo=selo: e.tensor_scalar(
                    out=dst, in0=gct[:, 0, :, kind], scalar1=vcol("sel", selo), scalar2=None, op0=ALU.mult))
                for r in range(1, R):
                    s.op("dve", ["gct", "vecs", hk], [hk], lambda e, dst=dst, kind=kind, selo=selo, r=r: e.scalar_tensor_tensor(
                        out=dst, in0=gct[:, r, :, kind], scalar=vcol("sel", selo + r), in1=dst, op0=ALU.mult, op1=ALU.add))
                ucolumn = (t0 + uo - 1) if side == 0 else (t0 + uo + n)
                s.op("dve", [hk], [("uTh", t0, side)], lambda e, dst=dst, ucolumn=ucolumn: e.tensor_copy(
                    out=uT[:, :, ucolumn], in_=dst))
            tis = [ti for ti, tl in enumerate(tiles) if tl[0] >= t0 and tl[0] < t0 + n]
            for i in range(CC):
                ureads = [("uT", i, ti) for ti in tis] + [("uTh", t0, 0), ("uTh", t0, 1), "vecs"]
                ct, _, ck = ctmp.next()
                base = t0 + uo
                s.op("dve", ureads, [ck], lambda e, ct=ct, i=i, base=base, n=n: e.tensor_scalar(
                    out=ct[:, 0:n], in0=uT[:, i, base - 1:base - 1 + n], scalar1=vcol(("convw", l), 0 * CC + i), scalar2=None, op0=ALU.mult))
                s.op("dve", ureads + [ck], [ck], lambda e, ct=ct, i=i, base=base, n=n: e.scalar_tensor_tensor(
                    out=ct[:, 0:n], in0=uT[:, i, base:base + n], scalar=vcol(("convw", l), 1 * CC + i), in1=ct[:, 0:n], op0=ALU.mult, op1=ALU.add))
                s.op("dve", ureads + [ck], [ck], lambda e, ct=ct, i=i, base=base, n=n: e.scalar_tensor_tensor(
                    out=ct[:, 0:n], in0=uT[:, i, base + 1:base + 1 + n], scalar=vcol(("convw", l), 2 * CC + i), in1=ct[:, 0:n], op0=ALU.mult, op1=ALU.add))
                s.op("dve", [ck] + [("pbT", i, ti) for ti in tis], [("convT", i, t0)], lambda e, ct=ct, i=i, n=n, t0=t0: e.tensor_tensor(
                    out=convT[:, i, t0:t0 + n], in0=ct[:, 0:n], in1=pbT[:, i, t0:t0 + n], op=ALU.mult))
        PC.close()
        PA.close()
        if upto == "p1" and l == 0:
            PXc.close()
            break

        PA = Phase()
        fT = PA.t("fT", [128, c.FG, NT], BF16)
        PF_ = Phase()
        pf = PF_.t("pf", [128, NKC, c.FWID], BF16)
        fns = []
        for tc in range(c.LCH):
            fns.append(lambda q, tc=tc: q.dma_start(out=pf[:, tc * R:(tc + 1) * R, :], in_=ga_t[tc].rearrange("(r p) f -> p r f", p=128)))
        if not last:
            for r in range(R):
                kc = c.SEQ // 128 + (r * CT) // 128
                p0 = (r * CT) % 128
                fns.append(lambda q, r=r, kc=kc, p0=p0: q.dma_start(out=pf[p0:p0 + CT, kc, :], in_=ga_t[c.LCH][r * CT:(r + 1) * CT, :]))
        s.dma("sp", ldslot(), [("ga", tc) for tc in range(c.NTC) if not (last and tc == c.LCH)], ["pf"], fns)
        csr = PF_.ring("csr", 4, [128, 2, TS], BF16)
        zt = PF_.ring("zt", 2, [128, 4, TS], BF16)
        fstreams = [(tl, False) for tl in tiles if not tl[2]] + ([] if last else [(tiles[-1], True)])
        for gp in range(c.FG // 2):
            for ((t0, sz, isc), _) in fstreams:
                banks = [nb() for _ in range(4)]
                if not isc:
                    ksteps = []
                    for k in range(c.SEQ // 128):
                        p0 = (k % R) * LT + (k // R) * 128
                        ksteps.append((k, 128, dft_c[p0:p0 + 128, t0:t0 + sz], dft_s[p0:p0 + 128, t0:t0 + sz]))
                else:
                    ksteps = [(c.SEQ // 128 + k, c.CKS, dftx_c[k * c.CKS:(k + 1) * c.CKS, :], dftx_s[k * c.CKS:(k + 1) * c.CKS, :])
                              for k in range(c.NCK)]
                nks = len(ksteps)
                for si, (kc, ksz, srcc, srcs_) in enumerate(ksteps):
                    cs, cslot, ckey = csr.next()
                    s.dma("sp", cslot, [], [ckey], [
                        lambda q, cs=cs, srcc=srcc, ksz=ksz: q.dma_start(out=cs[0:ksz, 0, 0:sz], in_=srcc),
                        lambda q, cs=cs, srcs_=srcs_, ksz=ksz: q.dma_start(out=cs[0:ksz, 1, 0:sz], in_=srcs_)])

                    def fn(e, cs=cs, kc=kc, ksz=ksz, si=si):
                        ins = None
                        for gi in range(2):
                            g = gp * 2 + gi
                            for tr in range(2):
                                ins = e.matmul(ps[banks[gi * 2 + tr]][:, 0:sz], pf[0:ksz, kc, g * 128:(g + 1) * 128],
                                               cs[0:ksz, tr, 0:sz], start=(si == 0), stop=(si == nks - 1))
                        return ins
                    s.op("pe", ["pf", ckey], [("ps", b) for b in banks], fn)
                z, _, zkey = zt.next()
                for j4 in range(4):
                    if j4 % 2:
                        s.op("act", [("ps", banks[j4])], [(zkey, j4)], lambda e, z=z, j4=j4: e.activation(
                            out=z[:, j4, 0:sz], in_=ps[banks[j4]][:, 0:sz], func=AF.Copy))
                    else:
                        s.op("dve", [("ps", banks[j4])], [(zkey, j4)], lambda e, z=z, j4=j4: e.tensor_copy(
                            out=z[:, j4, 0:sz], in_=ps[banks[j4]][:, 0:sz]))
                for gi in range(2):
                    g = gp * 2 + gi
                    b = nb()
                    mm_group(b, ["dch", (zkey, gi * 2), (zkey, gi * 2 + 1)],
                             [(dch[:, 0, :], z[:, gi * 2, 0:sz]), (dch[:, 1, :], z[:, gi * 2 + 1, 0:sz])], outap=ps[b][:, 0:sz])
                    s.op("act" if gi else "dve", [("ps", b)], [("fT", g, t0)],
                         (lambda e, g=g, b=b: e.activation(out=fT[:, g, t0:t0 + sz], in_=ps[b][:, 0:sz], func=AF.Copy)) if gi else
                         (lambda e, g=g, b=b: e.tensor_copy(out=fT[:, g, t0:t0 + sz], in_=ps[b][:, 0:sz])))
        PF_.close()
        if upto == "fourier" and l == 0:
            PA.close()
            PXc.close()
            break

        attnT = PA.t("attnT", [128, H, NT], BF16)
        PT = Phase()
        ckv = PT.t("ckv", [128, KVC, NK], BF16)
        krk = PT.t("krk", [128, NK], BF16)
        wukv = PT.t("wukv", [128, KVC, H * (c.NOPE + c.VH)], BF16)
        s.dma("pool", s.slot("wukv%d" % l), [], ["wukv"], [
            lambda q: q.dma_start(out=wukv[:], in_=w_ukv[l].rearrange("(k p) c -> p k c", p=128))])
        fns = []
        for xi in range(NXB):
            if xi < 2 * KVC:
                i, hf = xi // 2, xi % 2
                dl = ckv[hf * 64:(hf + 1) * 64, i, 0:c.SEQ]
                dc = ckv[hf * 64:(hf + 1) * 64, i, c.SEQ:NK]
            else:
                j = xi - 2 * KVC
                dl = krk[j * 64:(j + 1) * 64, 0:c.SEQ]
                dc = krk[j * 64:(j + 1) * 64, c.SEQ:NK]
            fns.append(lambda q, xi=xi, dl=dl: q.dma_start(out=dl.rearrange("p (r t) -> p r t", r=R),
                                                         in_=gb_t[xi][:, 0:LT].rearrange("(r p) t -> p r t", p=64)))
            fns.append(lambda q, xi=xi, dc=dc: q.dma_start(out=dc.rearrange("p (r t) -> p r t", r=R),
                                                         in_=gb_t[xi][:, LT:NT].rearrange("(r p) t -> p r t", p=64)))
        s.dma("sp", ldslot(), [("gb", xi) for xi in range(NXB)], ["ckv", "krk"], fns)
        knr = PT.ring("kn", 2, [128, NK], BF16)
        vr = PT.ring("vv", 2, [128, NKC, c.VH], BF16)
        rkr = PT.ring("rk", 2, [128, NKC], F32)
        rkt = PT.ring("rkt", 2, [128, NKC], F32)
        sqk = PT.ring("sqk", 2, [128, 512], BF16)
        qnr = PT.ring("qn", 2, [128, NT], BF16)
        qrr = PT.ring("qr", 2, [64, NT], BF16)
        ptr = PT.ring("pt", 4, [128, TS], BF16)
        rsr = PT.ring("rs", 2, [128, TS], F32)
        accr = PT.ring("acc", 2, [128, TS], F32)
        accbr = PT.ring("accb", 2, [128, TS], BF16)
        KT = min(512, c.SEQ)
        ktiles = [(i * KT, KT) for i in range(c.SEQ // KT)] + [(c.SEQ, c.CTX)]
        for h in range(H):
            kn, _, knk = knr.next()
            vv, _, vk = vr.next()
            rk, _, rkk = rkr.next()
            c0k = h * (c.NOPE + c.VH)
            bss = reserve()
            for (k0, kn_sz) in ktiles:
                b = nb()
                mm_group(b, ["wukv", "ckv"], [(wukv[:, i, c0k:c0k + 128], ckv[:, i, k0:k0 + kn_sz]) for i in range(KVC)],
                         outap=ps[b][:, 0:kn_sz])
                s.op("act", [("ps", b), "vecs"], [(knk, k0)], lambda e, kn=kn, b=b, k0=k0, kn_sz=kn_sz: e.activation(
                    out=kn[:, k0:k0 + kn_sz], in_=ps[b][:, 0:kn_sz], func=AF.Identity, scale=vcol(("gk_n", l)), bias=0.0))
                sq, _, sqkey = sqk.next()
                s.op("act", [("ps", b)], [sqkey], lambda e, sq=sq, b=b, kn_sz=kn_sz: e.activation(
                    out=sq[:, 0:kn_sz], in_=ps[b][:, 0:kn_sz], func=AF.Square))
                sub = [(ci, cs_, csz) for ci, (cs_, csz) in enumerate(kchunks) if cs_ >= k0 and cs_ < k0 + kn_sz]

                def fn(e, sq=sq, sub=sub, k0=k0):
                    ins = None
                    for (ci, cs_, csz) in sub:
                        e.matmul(ps[bss][0:csz, ci:ci + 1], sq[:, cs_ - k0:cs_ - k0 + csz], ones[:, 0:1], start=True, stop=False)
                        ins = e.matmul(ps[bss][0:csz, ci:ci + 1], krk[64:128, cs_:cs_ + csz], ones[64:128, 0:1], start=False, stop=True)
                    return ins
                s.op("pe", [sqkey, "krk", "ones"], [("ps", bss)], fn)
            if c.CKS < 128:
                pass
            rt_, _, rtk_ = rkt.next()
            for (ci, (cs_, csz)) in enumerate(kchunks):
                pass
            full = [ci for ci, (cs_, csz) in enumerate(kchunks) if csz == 128]
            part = [ci for ci, (cs_, csz) in enumerate(kchunks) if csz < 128]
            nf = len(full)
            s.op("act", [("ps", bss), "epsb"], [(rtk_, 0)], lambda e, rt_=rt_: e.activation(
                out=rt_[:, 0:nf], in_=ps[bss][:, 0:nf], func=AF.Sqrt, scale=1.0 / c.QKH, bias=epsb[:, 0:1]))
            s.op("dve", [(rtk_, 0)], [(rkk, 0)], lambda e, rt_=rt_, rk=rk: e.reciprocal(out=rk[:, 0:nf], in_=rt_[:, 0:nf]))
            for ci in part:
                csz = kchunks[ci][1]
                s.op("act", [("ps", bss), "epsb"], [(rtk_, 1)], lambda e, rt_=rt_, ci=ci, csz=csz: e.activation(
                    out=rt_[0:csz, ci:ci + 1], in_=ps[bss][0:csz, ci:ci + 1], func=AF.Sqrt, scale=1.0 / c.QKH, bias=epsb[0:csz, 0:1]))
                s.op("dve", [(rtk_, 1)], [(rkk, 1)], lambda e, rt_=rt_, rk=rk, ci=ci, csz=csz: e.reciprocal(
                    out=rk[0:csz, ci:ci + 1], in_=rt_[0:csz, ci:ci + 1]))
            rk_reads = [(rkk, 0)] + ([(rkk, 1)] if part else [])
            release(bss)
            c0v = c0k + c.NOPE
            for g0 in range(0, NKC, 4):
                b = nb()
                grp = list(range(g0, min(NKC, g0 + 4)))

                def fn(e, grp=grp, b=b):
                    ins = None
                    for gi, ci in enumerate(grp):
                        cs_, csz = kchunks[ci]
                        for i in range(KVC):
                            ins = e.matmul(ps[b][0:csz, gi * 128:(gi + 1) * 128], ckv[:, i, cs_:cs_ + csz], wukv[:, i, c0v:c0v + c.VH],
                                           start=(i == 0), stop=(i == KVC - 1))
                    return ins
                s.op("pe", ["wukv", "ckv"], [("ps", b)], fn)
                allfull = all(kchunks[ci][1] == 128 for ci in grp)
                if allfull:
                    ng = len(grp)
                    s.op("dve" if (g0 // 4) % 2 else "act", [("ps", b)], [(vk, g0)],
                         (lambda e, vv=vv, b=b, g0=g0, ng=ng: e.tensor_copy(out=vv[:, g0:g0 + ng, :].rearrange("p a b -> p (a b)"), in_=ps[b][:, 0:ng * 128])) if (g0 // 4) % 2 else
                         (lambda e, vv=vv, b=b, g0=g0, ng=ng: e.activation(out=vv[:, g0:g0 + ng, :].rearrange("p a b -> p (a b)"), in_=ps[b][:, 0:ng * 128], func=AF.Copy)))
                else:
                    for gi, ci in enumerate(grp):
                        csz = kchunks[ci][1]
                        s.op("dve", [("ps", b)], [(vk, g0, gi)], lambda e, vv=vv, b=b, gi=gi, ci=ci, csz=csz: e.tensor_copy(
                            out=vv[0:csz, ci, :], in_=ps[b][0:csz, gi * 128:(gi + 1) * 128]))
            v_reads = []
            for g0 in range(0, NKC, 4):
                grp = list(range(g0, min(NKC, g0 + 4)))
                if all(kchunks[ci][1] == 128 for ci in grp):
                    v_reads.append((vk, g0))
                else:
                    v_reads += [(vk, g0, gi) for gi in range(len(grp))]
            k_reads = [(knk, k0) for (k0, _) in ktiles]
            qn, qslot, qnk = qnr.next()
            qr_, qslot2, qrk = qrr.next()
            qtis = [ti for ti, tl in enumerate(tiles) if tl in act_tiles]
            ncols = LT if last else NT
            s.dma("sp", qslot, [("q_s", h, ti, 0) for ti in qtis], [qnk], [
                lambda q, qn=qn, h=h: q.dma_start(out=qn[:, 0:ncols], in_=q_s[h, 0:128, 0:ncols])])
            s.dma("sp", qslot2, [("q_s", h, ti, 1) for ti in qtis], [qrk], [
                lambda q, qr_=qr_, h=h: q.dma_start(out=qr_[:, 0:ncols], in_=q_s[h, 128:192, 0:ncols])])
            for ti in qtis:
                t0, sz, isc = tiles[ti]
                kcs = ctx_kchunks if isc else list(range(NKC))
                bo = reserve()
                acc, _, acck = accr.next()
                s.op("dve", [], [acck], lambda e, acc=acc: e.memset(acc[:, 0:sz], 0.0))
                sbanks = {}
                pts = {}

                def emit_s(ci):
                    cs_, csz = kchunks[ci]
                    b = nb()
                    sbanks[ci] = b
                    mm_group(b, k_reads + ["krk", qnk, qrk],
                             [(kn[:, cs_:cs_ + csz], qn[:, t0:t0 + sz]), (krk[0:64, cs_:cs_ + csz], qr_[:, t0:t0 + sz])],
                             outap=ps[b][0:csz, 0:sz])

                def emit_exp(ci):
                    cs_, csz = kchunks[ci]
                    b = sbanks[ci]
                    pt, _, pk = ptr.next()
                    pts[ci] = (pt, pk)
                    s.op("act", [("ps", b)] + rk_reads, [pk], lambda e, pt=pt, b=b, ci=ci, csz=csz: e.activation(
                        out=pt[0:csz, 0:sz], in_=ps[b][0:csz, 0:sz], func=AF.Exp, scale=rk[0:csz, ci:ci + 1]))

                def emit_pv(idx, ci):
                    cs_, csz = kchunks[ci]
                    pt, pk = pts[ci]

                    def fn(e):
                        return e.matmul(ps[bo][:, 0:sz], vv[0:csz, ci, :], pt[0:csz, 0:sz], start=(idx == 0), stop=(idx == len(kcs) - 1))
                    s.op("pe", v_reads + [pk], [("ps", bo)], fn)
                    s.op("dve", [pk, acck], [acck], lambda e: e.tensor_tensor(
                        out=acc[0:csz, 0:sz], in0=acc[0:csz, 0:sz], in1=pt[0:csz, 0:sz], op=ALU.add))

                LA = 3
                for i0 in range(min(LA, len(kcs))):
                    emit_s(kcs[i0])
                for idx, ci in enumerate(kcs):
                    emit_exp(ci)
                    if idx + LA < len(kcs):
                        emit_s(kcs[idx + LA])
                    emit_pv(idx, ci)
                accb, _, accbk = accbr.next()
                s.op("dve", [acck], [accbk], lambda e, acc=acc, accb=accb: e.tensor_copy(out=accb[:, 0:sz], in_=acc[:, 0:sz]))
                bsum = nb()
                mm_group(bsum, [accbk, "ones"], [(ones[:], accb[:, 0:sz])], outap=ps[bsum][:, 0:sz])
                rs, _, rsk = rsr.next()
                s.op("dve", [("ps", bsum)], [rsk], lambda e, rs=rs, bsum=bsum: e.reciprocal(out=rs[:, 0:sz], in_=ps[bsum][:, 0:sz]))
                s.op("dve", [("ps", bo), rsk], [("attnT", h, ti)], lambda e, rs=rs, bo=bo, h=h: e.tensor_tensor(
                    out=attnT[:, h, t0:t0 + sz], in0=ps[bo][:, 0:sz], in1=rs[:, 0:sz], op=ALU.mult))
                release(bo)
        PT.close()
        if upto == "attn" and l == 0:
            PA.close()
            PXc.close()
            break

        PM = Phase()
        hT = PM.t("hTm", [128, KD, NT], BF16)
        for ti, (t0, sz, isc) in enumerate(tiles):
            if (t0, sz, isc) not in act_tiles:
                continue
            s.dma("sp", ldslot(), [("h_s", ti)], [("hTm", ti)], [
                lambda q, t0=t0, sz=sz: q.dma_start(out=hT[:, :, t0:t0 + sz], in_=h_s[:, :, t0:t0 + sz].rearrange("k p t -> p k t"))])
        nbr = c.FG + H + CC
        wmr = PM.ring("wm", 2, [128, nbr, 128], BF16)
        gtr = PM.ring("gt", 3, [128, TS], F32)
        ttr = PM.ring("tt", 3, [128, TS], F32)
        mstg = PM.ring("mstg", 2, [128, TS], BF16)
        for j in range(KD):
            wg, wgslot, wgkey = wring.next()
            fns = []
            for br in range(3):
                for a in range(0, KD, 8):
                    bnd = min(KD, a + 8)
                    src = w_in[l, a * 128:bnd * 128, c.OFF_G + br * D + j * 128:c.OFF_G + br * D + (j + 1) * 128].rearrange("(k p) c -> p k c", p=128)
                    fns.append(lambda q, src=src, a=a, bnd=bnd, br=br, wg=wg: q.dma_start(out=wg[:, a:bnd, br * 128:(br + 1) * 128], in_=src))
            s.dma("pool", wgslot, [], [wgkey], fns)
            wm, wmslot, wmkey = wmr.next()
            s.dma("pool", wmslot, [], [wmkey], [
                lambda q, wm=wm, j=j: q.dma_start(out=wm[:, 0:c.FG, :], in_=w_f_out[l, :, j * 128:(j + 1) * 128].rearrange("(k p) c -> p k c", p=128)),
                lambda q, wm=wm, j=j: q.dma_start(out=wm[:, c.FG:c.FG + H, :], in_=w_mla_out[l, :, j * 128:(j + 1) * 128].rearrange("(k p) c -> p k c", p=128)),
                lambda q, wm=wm, j=j: q.dma_start(out=wm[:, c.FG + H:nbr, :], in_=w_conv_out[l, :, j * 128:(j + 1) * 128].rearrange("(k p) c -> p k c", p=128))])
            for ti, (t0, sz, isc) in enumerate(tiles):
                if (t0, sz, isc) not in act_tiles:
                    continue
                hreads = [("hTm", ti)]
                gts = []
                for br in range(3):
                    b = nb()
                    mm_group(b, [wgkey] + hreads, [(wg[:, k, br * 128:(br + 1) * 128], hT[:, k, t0:t0 + sz]) for k in range(KD)],
                             outap=ps[b][:, 0:sz])
                    gt, _, gk = gtr.next()
                    s.op("act", [("ps", b), "vecs"], [gk], lambda e, gt=gt, b=b, br=br, j=j: e.activation(
                        out=gt[:, 0:sz], in_=ps[b][:, 0:sz], func=AF.Sigmoid, bias=vcol(("bgate", l), br * KD + j), scale=1.0))
                    gts.append((gt, gk))
                srcs3 = [
                    ([(wm[:, g, :], fT[:, g, t0:t0 + sz]) for g in range(c.FG)], [("fT", g, t0) for g in range(c.FG)]),
                    ([(wm[:, c.FG + h, :], attnT[:, h, t0:t0 + sz]) for h in range(H)], [("attnT", h, ti) for h in range(H)]),
                    ([(wm[:, c.FG + H + i, :], convT[:, i, t0:t0 + sz]) for i in range(CC)],
                     [("convT", i, 0 if not isc else LT) for i in range(CC)]),
                ]
                tts = []
                for br in range(3):
                    b = nb()
                    mm_group(b, [wmkey] + srcs3[br][1], srcs3[br][0], outap=ps[b][:, 0:sz])
                    tt, _, tk = ttr.next()
                    gt, gk = gts[br]
                    s.op("dve", [("ps", b), gk], [tk], lambda e, tt=tt, b=b, gt=gt: e.tensor_tensor(
                        out=tt[:, 0:sz], in0=ps[b][:, 0:sz], in1=gt[:, 0:sz], op=ALU.mult))
                    tts.append((tt, tk))
                s.op("dve", [tts[0][1], tts[1][1]], [tts[0][1]], lambda e, a=tts[0][0], b_=tts[1][0]: e.tensor_tensor(
                    out=a[:, 0:sz], in0=a[:, 0:sz], in1=b_[:, 0:sz], op=ALU.add))
                ms, _, mk = mstg.next()
                s.op("dve", [tts[0][1], tts[2][1]], [mk], lambda e, ms=ms, a=tts[0][0], b_=tts[2][0]: e.tensor_tensor(
                    out=ms[:, 0:sz], in0=a[:, 0:sz], in1=b_[:, 0:sz], op=ALU.add))
                s.dma("sp", stslot(), [mk], [("m_s", j, ti)], [
                    lambda q, ms=ms, j=j, t0=t0, sz=sz: q.dma_start(out=m_s[j, :, t0:t0 + sz], in_=ms[:, 0:sz])])
        PM.close()
        PA.close()
        PXc.close()
        if upto == "merge" and l == 0:
            break

        def residual_update(P, xcr, b, j, ti, t0, sz, isc, gate_mi, final):
            m = 1 if isc else 0
            xc, xslot, xk = xcr.next()
            s.dma("sp", xslot, [("xs", j, ti)], [xk], [
                lambda q, xc=xc: q.dma_start(out=xc[:, 0:sz], in_=xs[j, :, t0:t0 + sz])])
            s.op("dve", [("ps", b), xk, "modt"], [xk], lambda e, xc=xc: e.scalar_tensor_tensor(
                out=xc[:, 0:sz], in0=ps[b][:, 0:sz], scalar=modt[:, gate_mi, j, m:m + 1], in1=xc[:, 0:sz],
                op0=ALU.mult, op1=ALU.add))
            if final:
                s.dma("sp", stslot(), [xk], [("out", j, ti)], [
                    lambda q, xc=xc: q.dma_start(out=out_T[j, :, t0:t0 + sz], in_=xc[:, 0:sz])])
            else:
                s.dma("sp", stslot(), [xk], [("xs", j, ti)], [
                    lambda q, xc=xc: q.dma_start(out=xs[j, :, t0:t0 + sz], in_=xc[:, 0:sz])])

        PO = Phase()
        wo = PO.t("wo", [128, KD, D], BF16)
        woslot = s.slot("wo%d" % l)
        fns = []
        for a in range(0, KD, 4):
            src = w_out[l, a * 128:(a + 4) * 128, :].rearrange("(k p) c -> p k c", p=128) if KD >= 4 else None
            if KD >= 4:
                fns.append(lambda q, src=src, a=a: q.dma_start(out=wo[:, a:a + 4, :], in_=src))
        if KD < 4:
            fns = [lambda q: q.dma_start(out=wo[:], in_=w_out[l].rearrange("(k p) c -> p k c", p=128))]
        s.dma("pool", woslot, [], ["wo"], fns)
        mtr = PO.ring("mt", 2, [128, KD, TS], BF16)
        xcr = PO.ring("xc", 3, [128, TS], F32)
        for ti, (t0, sz, isc) in enumerate(tiles):
            if (t0, sz, isc) not in act_tiles:
                continue
            mt, mslot, mtk = mtr.next()
            s.dma("sp", mslot, [("m_s", j, ti) for j in range(KD)], [mtk], [
                lambda q, mt=mt, t0=t0, sz=sz: q.dma_start(out=mt[:, :, 0:sz], in_=m_s[:, :, t0:t0 + sz].rearrange("k p t -> p k t"))])
            for j in range(KD):
                b = nb()
                mm_group(b, ["wo", mtk], [(wo[:, k, j * 128:(j + 1) * 128], mt[:, k, 0:sz]) for k in range(KD)], outap=ps[b][:, 0:sz])
                residual_update(PO, xcr, b, j, ti, t0, sz, isc, 2, False)
        PO.close()
        if upto == "mixer" and l == 0:
            break

        for m in range(2):
            s.res[("A", "x", m)] = s.res.get(("A", "nffn", m))
        PFN = Phase()
        h2 = PFN.t("h2", [128, KD, NT], BF16)
        PN2 = Phase()
        norm_phase(PN2, A2, 3, h2, act_tiles, False)
        PN2.close()
        aT = PFN.t("aT", [128, c.FH, NT], BF16)
        sgr = PFN.ring("sg", 3, [128, TS], F32)
        wdr = PFN.ring("wd", 2, [128, c.FH, 128], BF16)
        xcr = PFN.ring("xc", 3, [128, TS], F32)
        for half in range(2):
            for fl in range(c.FH):
                f = half * c.FH + fl
                wt, wslot, wkey = wring.next()
                fns = []
                for wi, wsrc in enumerate((w_ffn_gate, w_ffn_up)):
                    for a in range(0, KD, 8):
                        bnd = min(KD, a + 8)
                        src = wsrc[l, a * 128:bnd * 128, f * 128:(f + 1) * 128].rearrange("(k p) c -> p k c", p=128)
                        fns.append(lambda q, src=src, a=a, bnd=bnd, wi=wi, wt=wt: q.dma_start(out=wt[:, a:bnd, wi * 128:(wi + 1) * 128], in_=src))
                s.dma("pool", wslot, [], [wkey], fns)
                for ti, (t0, sz, isc) in enumerate(tiles):
                    if (t0, sz, isc) not in act_tiles:
                        continue
                    hreads = [("hT", ti, k) for k in range(KD)]
                    bg = nb()
                    mm_group(bg, [wkey] + hreads, [(wt[:, k, 0:128], h2[:, k, t0:t0 + sz]) for k in range(KD)], outap=ps[bg][:, 0:sz])
                    bu = nb()
                    mm_group(bu, [wkey] + hreads, [(wt[:, k, 128:256], h2[:, k, t0:t0 + sz]) for k in range(KD)], outap=ps[bu][:, 0:sz])
                    sg, _, sgk = sgr.next()
                    s.op("act", [("ps", bg)], [sgk], lambda e, sg=sg, bg=bg: e.activation(out=sg[:, 0:sz], in_=ps[bg][:, 0:sz], func=AF.Silu))
                    s.op("dve", [("ps", bu), sgk], [("aT", fl, ti)], lambda e, sg=sg, bu=bu, fl=fl: e.tensor_tensor(
                        out=aT[:, fl, t0:t0 + sz], in0=ps[bu][:, 0:sz], in1=sg[:, 0:sz], op=ALU.mult))
            for j in range(KD):
                wd, wdslot, wdk = wdr.next()
                fns = []
                for a in range(0, c.FH, 8):
                    bnd = min(c.FH, a + 8)
                    src = w_ffn_down[l, (half * c.FH + a) * 128:(half * c.FH + bnd) * 128, j * 128:(j + 1) * 128].rearrange("(k p) c -> p k c", p=128)
                    fns.append(lambda q, src=src, a=a, bnd=bnd, wd=wd: q.dma_start(out=wd[:, a:bnd, :], in_=src))
                s.dma("pool", wdslot, [], [wdk], fns)
                for ti, (t0, sz, isc) in enumerate(tiles):
                    if (t0, sz, isc) not in act_tiles:
                        continue
                    b = nb()
                    mm_group(b, [wdk] + [("aT", fl, ti) for fl in range(c.FH)],
                             [(wd[:, fl, :], aT[:, fl, t0:t0 + sz]) for fl in range(c.FH)], outap=ps[b][:, 0:sz])
                    residual_update(PFN, xcr, b, j, ti, t0, sz, isc, 5, last and half == 1)
        PFN.close()

    if upto is not None:
        s.barrier()
        s.dma("sp", stslot(), [], ["outdbg"], [lambda q: q.dma_start(out=out_T, in_=xs[:, :, 0:LT])])
    s.barrier()
    G.close()
    for p in reversed(ps_cm):
        p.__exit__(None, None, None)
    return nc, s


def host_inputs(cfg, inp):
    c = cfg
    f32 = np.float32
    KD = c.KD

    def fm(v, rows=128):
        v = np.asarray(v, f32)
        return np.ascontiguousarray(v.reshape(-1, rows).T)

    inv_freq = (c.THETA ** (-np.arange(16, dtype=f32) / f32(16))).astype(f32)
    cc_idx = np.arange(128)
    ang_c = 2.0 * np.pi * np.outer(cc_idx, cc_idx) / 128.0
    dft_ch = np.stack([np.cos(ang_c) / math.sqrt(128.0), -np.sin(ang_c) / math.sqrt(128.0)], axis=1).astype(ml_dtypes.bfloat16)
    shared = {}
    for k in ("w_in", "w_uq", "w_ukv", "w_f_out", "w_mla_out", "w_conv_out", "w_out",
              "w_ffn_gate", "w_ffn_up", "w_ffn_down"):
        shared[k] = np.ascontiguousarray(np.asarray(inp[k], f32))
    shared["dft_ch"] = dft_ch
    maps = []
    for core in range(8):
        b, r = core // R, core % R
        m = dict(shared)
        xl = np.asarray(inp["x"][b, r * c.LT:(r + 1) * c.LT, :], f32)
        xc = np.asarray(inp["ctx"][b, r * c.CT:(r + 1) * c.CT, :], f32)
        xt = np.concatenate([xl, xc], axis=0).T
        m["xT"] = np.ascontiguousarray(xt.reshape(KD, 128, c.NT))
        vecs = np.zeros((128, c.NV), f32)
        V = c.voff
        ct = np.stack([fm(inp["c"][b]), fm(inp["c_ctx"])], axis=2)
        vecs[:, V["cT"]:V["cT"] + KD * 2] = ct.reshape(128, KD * 2)
        m["w_ada"] = np.ascontiguousarray(np.asarray(inp["w_ada"], f32)[:, :, r * c.MW:(r + 1) * c.MW])
        if r > 0:
            vecs[:, V["sel"] + (r - 1)] = 1.0
        if r < R - 1:
            vecs[:, V["sel"] + 4 + (r + 1)] = 1.0
        for l in range(c.DEPTH):
            vecs[:, V[("nmix", l)]:V[("nmix", l)] + KD] = fm(inp["norm_mix"][l])
            vecs[:, V[("nffn", l)]:V[("nffn", l)] + KD] = fm(inp["norm_ffn"][l])
            ba = fm(inp["b_ada"][l])
            vecs[:, V[("bada", l)]:V[("bada", l)] + 12 * KD] = np.repeat(ba, 2, axis=1)
            vecs[:, V[("bgate", l)]:V[("bgate", l)] + 3 * KD] = fm(inp["b_gate"][l])
            vecs[:, V[("qa", l)]:V[("qa", l)] + c.QC] = fm(inp["q_a_norm"][l])
            vecs[:, V[("kva", l)]:V[("kva", l)] + c.KVC] = fm(inp["kv_a_norm"][l])
            for pre, nm in (("gq", "q_norm"), ("gk", "k_norm")):
                g = np.asarray(inp[nm][l], f32)
                vecs[:, V[(pre + "_n", l)]] = g[0:128]
                vecs[0:64, V[(pre + "_r", l)]] = g[128:192]
                vecs[0:64, V[(pre + "_s", l)]] = np.concatenate([g[160:192], g[128:160]])
            cw = np.asarray(inp["conv_w"][l], f32)
            for tap in range(3):
                vecs[:, V[("convw", l)] + tap * c.CC:V[("convw", l)] + (tap + 1) * c.CC] = fm(cw[tap])
        m["vecs"] = vecs
        t = (r * c.LT + np.arange(c.LT)).astype(np.int64)
        row = (t // c.GRID_W).astype(f32)
        col = (t % c.GRID_W).astype(f32)
        ang = np.concatenate([row[:, None] * inv_freq[None, :], col[:, None] * inv_freq[None, :]], axis=1).astype(f32)
        cs = np.cos(ang).astype(f32).T
        sn = np.sin(ang).astype(f32).T
        rope = np.zeros((64, 2, c.NT), f32)
        rope[:, 0, :] = 1.0
        rope[0:32, 0, :c.LT] = cs
        rope[32:64, 0, :c.LT] = cs
        rope[0:32, 1, :c.LT] = -sn
        rope[32:64, 1, :c.LT] = sn
        m["rope"] = rope
        tt = np.arange(c.SEQ, dtype=np.int64)[:, None]
        tp = (r * c.LT + np.arange(c.LT, dtype=np.int64))[None, :]
        a = 2.0 * np.pi * ((tt * tp) % c.SEQ).astype(np.float64) / c.SEQ
        m["dft_c"] = (np.cos(a) / math.sqrt(c.SEQ)).astype(ml_dtypes.bfloat16)
        m["dft_s"] = (np.sin(a) / math.sqrt(c.SEQ)).astype(ml_dtypes.bfloat16)
        tt = np.arange(c.CTX, dtype=np.int64)[:, None]
        tp = (r * c.CT + np.arange(c.CT, dtype=np.int64))[None, :]
        a = 2.0 * np.pi * ((tt * tp) % c.CTX).astype(np.float64) / c.CTX
        m["dftx_c"] = (np.cos(a) / math.sqrt(c.CTX)).astype(ml_dtypes.bfloat16)
        m["dftx_s"] = (np.sin(a) / math.sqrt(c.CTX)).astype(ml_dtypes.bfloat16)
        maps.append(m)
    return maps


def assemble(cfg, results):
    c = cfg
    out = np.zeros((c.B, c.SEQ, c.D), np.float32)
    for core in range(8):
        b, r = core // R, core % R
        o = np.asarray(results[core]["outT"], np.float32).reshape(c.D, c.LT)
        out[b, r * c.LT:(r + 1) * c.LT, :] = o.T
    return out


def kernel(**inputs):
    cfg = Cfg(FULL_CFG)
    nc, _ = build_program(cfg)
    maps = host_inputs(cfg, inputs)
    res = run_bass_kernel_spmd(nc, maps, core_ids=list(range(8)))
    return assemble(cfg, res.results)
```
